# Optimizing a Trainium2 kernel written in Bass

```python
import math
import jax
import jax.numpy as jnp
from jax import lax
import numpy as np

D_MODEL = 1024
BATCH = 8
SEQ = 8192
DEPTH = 2

PLE_DIM = 256
MIX_WIDTH = D_MODEL
N_EVEN = (DEPTH + 1) // 2
N_ODD = DEPTH // 2
DEEPNORM_ALPHA = (2.0 * DEPTH) ** 0.25
DEEPNORM_BETA = (8.0 * DEPTH) ** -0.25
LN_EPS = 1e-5
RMS_EPS = 1e-6

DN_HEADS = 4
DN_DK = 128
DN_DV = 128
DN_WIDTH = DN_HEADS * DN_DV
DN_QKV = 2 * DN_HEADS * DN_DK + DN_WIDTH
DN_AB = 4 * DN_HEADS
DN_CONV = 5
DN_CHUNK = 64

RK_HEADS = 8
RK_HEAD = 64
RK_WIDTH = RK_HEADS * RK_HEAD
RK_DECAY_LORA = 64
RK_ICLR_LORA = 64
RK_SHIFT = 3 * RK_WIDTH + 2 * RK_DECAY_LORA + RK_ICLR_LORA
RK_GN_EPS = 64e-5

EVEN_SPLITS = [DN_QKV, DN_QKV + DN_AB, DN_QKV + DN_AB + DN_WIDTH, DN_QKV + DN_AB + DN_WIDTH + RK_SHIFT]
EVEN_IN = DN_QKV + DN_AB + DN_WIDTH + RK_SHIFT + RK_WIDTH
RK_SPLITS = [RK_WIDTH, 2 * RK_WIDTH, 3 * RK_WIDTH, 3 * RK_WIDTH + RK_DECAY_LORA, 3 * RK_WIDTH + 2 * RK_DECAY_LORA]

HY_WIDTH = MIX_WIDTH
HY_ORDER = 2
HY_SHORT = 3
HY_EMB = 33
HY_FILTER_WIDTH = 64
HY_N_FILTERS = HY_ORDER * 2 * HY_WIDTH
HY_TARGET = 1e-2
HY_FAST_PCT = 0.3
HY_SLOW_PCT = 1.5
ODD_IN = (HY_ORDER + 1) * HY_WIDTH + HY_WIDTH

kernel_name = 'bidir_deltanet_rwkv7_hyena_deepnorm_trunk'


def layer_norm(u, g, b):
    uf = u.astype(jnp.float32)
    mu = jnp.mean(uf, -1, keepdims=True)
    var = jnp.mean(jnp.square(uf - mu), -1, keepdims=True)
    return ((uf - mu) * lax.rsqrt(var + LN_EPS) * g + b).astype(u.dtype)


def rms_norm(u, g):
    uf = u.astype(jnp.float32)
    return (uf * lax.rsqrt(jnp.mean(jnp.square(uf), -1, keepdims=True) + RMS_EPS) * g).astype(u.dtype)


def l2_normalize(u):
    return u * lax.rsqrt(jnp.sum(jnp.square(u), -1, keepdims=True) + RMS_EPS)


def dwconv_centred(u, w, b=None):
    width = w.shape[0]
    pad = width // 2
    y = lax.conv_general_dilated(u, w.astype(u.dtype)[:, None, :], window_strides=(1,), padding=[(pad, pad)],
                                 dimension_numbers=('NWC', 'WIO', 'NWC'), feature_group_count=u.shape[-1])
    return y if b is None else y + b.astype(u.dtype)


def bidir_token_shift(s, mu):
    prev = jnp.pad(s, ((0, 0), (1, 0), (0, 0)))[:, :-1]
    nxt = jnp.pad(s, ((0, 0), (0, 1), (0, 0)))[:, 1:]
    return s + mu[0] * (prev - s) + mu[1] * (nxt - s)


def gated_delta_chunked(q, k, v, g, beta):
    bsz, heads, seq, dk = q.shape
    dv = v.shape[-1]
    c = DN_CHUNK
    n = seq // c
    q = q * (dk ** -0.5)
    chunks = lambda t: t.reshape(bsz, heads, n, c, *t.shape[3:])
    q, k, v, g, beta = chunks(q), chunks(k), chunks(v), chunks(g), chunks(beta)
    g = jnp.cumsum(g, axis=-1)
    incl = jnp.tril(jnp.ones((c, c), dtype=bool))
    strict = jnp.tril(jnp.ones((c, c), dtype=bool), -1)
    decay = jnp.exp(jnp.where(incl, g[..., :, None] - g[..., None, :], -jnp.inf))
    k_beta = k * beta[..., None]
    kk = jnp.einsum('bhnid,bhnjd->bhnij', k_beta, k) * decay
    t_mat = jnp.where(strict, kk, 0.0) + jnp.eye(c, dtype=q.dtype)
    u = lax.linalg.triangular_solve(t_mat, v * beta[..., None], left_side=True, lower=True, unit_diagonal=True)
    w = lax.linalg.triangular_solve(t_mat, k_beta * jnp.exp(g)[..., None], left_side=True, lower=True, unit_diagonal=True)
    qk = jnp.einsum('bhnid,bhnjd->bhnij', q, k) * decay
    q_dec = q * jnp.exp(g)[..., None]
    g_last = g[..., -1]
    k_tail = k * jnp.exp(g_last[..., None] - g)[..., None]

    def step(state, xs):
        u_n, w_n, qd_n, qk_n, kt_n, gl_n = xs
        v_new = u_n - jnp.einsum('bhck,bhkv->bhcv', w_n, state)
        o_n = jnp.einsum('bhck,bhkv->bhcv', qd_n, state) + jnp.einsum('bhij,bhjv->bhiv', qk_n, v_new)
        state = state * jnp.exp(gl_n)[..., None, None] + jnp.einsum('bhck,bhcv->bhkv', kt_n, v_new)
        return state, o_n

    xs = tuple(jnp.moveaxis(t, 2, 0) for t in (u, w, q_dec, qk, k_tail, g_last))
    s0 = jnp.zeros((bsz, heads, dk, dv), jnp.float32)
    _, o = lax.scan(step, s0, xs)
    return jnp.moveaxis(o, 0, 2).reshape(bsz, heads, seq, dv)


def rwkv7_scan(r, w, k, v, a, b):
    bsz, seq, heads, hd = r.shape

    def step(state, xs):
        r_t, w_t, k_t, v_t, a_t, b_t = xs
        sa = jnp.einsum('bhvk,bhk->bhv', state, a_t)
        state = state * w_t[:, :, None, :] + sa[..., None] * b_t[:, :, None, :] + v_t[..., None] * k_t[:, :, None, :]
        return state, jnp.einsum('bhvk,bhk->bhv', state, r_t)

    xs = tuple(jnp.moveaxis(t.astype(jnp.float32), 1, 0) for t in (r, w, k, v, a, b))
    _, y = lax.scan(step, jnp.zeros((bsz, heads, hd, hd), jnp.float32), xs)
    return jnp.moveaxis(y, 0, 1)


def rwkv_decay(wd, w0, w2):
    w_log = -jax.nn.softplus(-(w0 + jnp.tanh(wd) @ w2)) - 0.5
    return jnp.exp(-jnp.exp(w_log))


def hyena_position_features(seq):
    bands = (HY_EMB - 1) // 2
    t = jnp.linspace(0.0, 1.0, seq, dtype=jnp.float32)[:, None]
    f = jnp.linspace(1e-4, bands - 1, bands, dtype=jnp.float32)[None, :]
    ang = (2.0 * math.pi / seq) * jnp.arange(seq, dtype=jnp.float32)[:, None] * f
    return jnp.concatenate([t, jnp.cos(ang), -jnp.sin(ang)], axis=-1), t


def hyena_filters(seq, w1, b1, w2, b2, w3, b3, freq, w_out, deltas):
    f32 = jnp.float32
    feats, t = hyena_position_features(seq)
    freq = freq.astype(f32)
    hdn = jnp.sin(freq * (feats @ w1.astype(f32) + b1.astype(f32)))
    hdn = jnp.sin(freq * (hdn @ w2.astype(f32) + b2.astype(f32)))
    hdn = jnp.sin(freq * (hdn @ w3.astype(f32) + b3.astype(f32)))
    h = (hdn @ w_out.astype(f32)) * jnp.exp(-t * jnp.abs(deltas.astype(f32)))
    h = h.reshape(seq, HY_ORDER, 2, HY_WIDTH)
    fwd, bwd = h[:, :, 0], h[:, :, 1]
    circ = jnp.concatenate([fwd, jnp.zeros_like(fwd[:1]), jnp.flip(bwd[1:], axis=0)], axis=0)
    return circ / (jnp.sum(jnp.abs(circ), axis=0, keepdims=True) + RMS_EPS)


def fft_long_conv(u, filt):
    seq = u.shape[1]
    u_f = jnp.fft.rfft(u, n=2 * seq, axis=1)
    h_f = jnp.fft.rfft(filt, axis=0)
    return jnp.fft.irfft(u_f * h_f[None], n=2 * seq, axis=1)[:, :seq]


def even_mixer(x, w_in, dn_conv, dn_a_log, dn_dt_bias, dn_norm, rk_mu, rk_w0, rk_w2, rk_a0, rk_a2,
               rk_k_k, rk_k_a, rk_r_k, rk_ln_w, rk_ln_b):
    bsz, seq, _ = x.shape
    f32 = jnp.float32
    proj = x @ w_in
    dn_qkv, dn_ab, dn_gate, rk_in, rk_gate = jnp.split(proj, EVEN_SPLITS, axis=-1)

    qkv = jax.nn.silu(dwconv_centred(dn_qkv, dn_conv)).astype(f32)
    q, k, v = jnp.split(qkv, [DN_HEADS * DN_DK, 2 * DN_HEADS * DN_DK], axis=-1)
    q = l2_normalize(q.reshape(bsz, seq, DN_HEADS, DN_DK)).transpose(0, 2, 1, 3)
    k = l2_normalize(k.reshape(bsz, seq, DN_HEADS, DN_DK)).transpose(0, 2, 1, 3)
    v = v.reshape(bsz, seq, DN_HEADS, DN_DV).transpose(0, 2, 1, 3)
    ab = dn_ab.astype(f32).reshape(bsz, seq, 4, DN_HEADS).transpose(2, 0, 3, 1)
    g = -jnp.exp(dn_a_log.astype(f32))[:, None, :, None] * jax.nn.softplus(ab[:2] + dn_dt_bias.astype(f32)[:, None, :, None])
    beta = jax.nn.sigmoid(ab[2:])
    flip_l = lambda t: jnp.flip(t, axis=2)
    o = gated_delta_chunked(q, k, v, g[0], beta[0]) + flip_l(
        gated_delta_chunked(flip_l(q), flip_l(k), flip_l(v), flip_l(g[1]), flip_l(beta[1])))
    o = rms_norm(o.transpose(0, 2, 1, 3), dn_norm).reshape(bsz, seq, DN_WIDTH)
    y_dn = o.astype(x.dtype) * jax.nn.silu(dn_gate)

    s = bidir_token_shift(rk_in, rk_mu).astype(f32)
    r, k, v, wd_f, wd_b, a_d = jnp.split(s, RK_SPLITS, axis=-1)
    w_f = rwkv_decay(wd_f, rk_w0[0], rk_w2[0])
    w_b = rwkv_decay(wd_b, rk_w0[1], rk_w2[1])
    a = jax.nn.sigmoid(rk_a0 + a_d @ rk_a2)
    heads = lambda t: t.reshape(bsz, seq, RK_HEADS, RK_HEAD)
    kk = l2_normalize(heads(k * rk_k_k))
    k = k * (1.0 + (a - 1.0) * rk_k_a)
    r, k, v, a, w_f, w_b = heads(r), heads(k), heads(v), heads(a), heads(w_f), heads(w_b)
    a_vec = -kk
    b_vec = kk * a
    flip_t = lambda t: jnp.flip(t, axis=1)
    wkv = rwkv7_scan(r, w_f, k, v, a_vec, b_vec) + flip_t(
        rwkv7_scan(flip_t(r), flip_t(w_b), flip_t(k), flip_t(v), flip_t(a_vec), flip_t(b_vec)))
    mean = jnp.mean(wkv, -1, keepdims=True)
    var = jnp.mean(jnp.square(wkv - mean), -1, keepdims=True)
    wkv = ((wkv - mean) * lax.rsqrt(var + RK_GN_EPS)).reshape(bsz, seq, RK_WIDTH) * rk_ln_w + rk_ln_b
    bonus = (jnp.sum(r * k * rk_r_k, -1, keepdims=True) * v).reshape(bsz, seq, RK_WIDTH)
    y_rk = (wkv + bonus).astype(x.dtype) * jax.nn.silu(rk_gate)
    return jnp.concatenate([y_dn, y_rk], axis=-1)


def odd_mixer(x, w_in, conv_w, conv_b, f_w1, f_b1, f_w2, f_b2, f_w3, f_b3, f_freq, f_out, deltas, skip):
    bsz, seq, _ = x.shape
    f32 = jnp.float32
    proj = x @ w_in
    xv, gate = jnp.split(proj, [(HY_ORDER + 1) * HY_WIDTH], axis=-1)
    xv = dwconv_centred(xv, conv_w, conv_b).astype(f32)
    x1, x2, v = jnp.split(xv, 3, axis=-1)
    filt = hyena_filters(seq, f_w1, f_b1, f_w2, f_b2, f_w3, f_b3, f_freq, f_out, deltas)
    skip = skip.astype(f32)
    z = x1 * (fft_long_conv(v, filt[:, 0]) + skip[0] * v)
    y = x2 * (fft_long_conv(z, filt[:, 1]) + skip[1] * z)
    return y.astype(x.dtype) * jax.nn.silu(gate)


def setup_inputs(seed: int = 0) -> dict:
    key = jax.random.key(seed)
    keys = iter(list(jax.random.split(key, 48)))
    f32 = jnp.float32
    nrm = lambda shape, scale: jax.random.normal(next(keys), shape, f32) * scale
    unif = lambda shape, lo, hi: jax.random.uniform(next(keys), shape, f32, lo, hi)
    x = nrm((BATCH, SEQ, D_MODEL), 1.0)
    p = nrm((DEPTH, BATCH, SEQ, PLE_DIM), 1.0)
    even_w_in = nrm((N_EVEN, D_MODEL, EVEN_IN), D_MODEL ** -0.5)
    dn_conv = nrm((N_EVEN, DN_CONV, DN_QKV), DN_CONV ** -0.5)
    dn_a_log = jnp.log(unif((N_EVEN, 2, DN_HEADS), 1.0, 16.0))
    dt = jnp.exp(unif((N_EVEN, 2, DN_HEADS), math.log(1e-3), math.log(1e-1)))
    dn_dt_bias = dt + jnp.log(-jnp.expm1(-dt))
    dn_norm = 1.0 + nrm((N_EVEN, DN_DV), 0.02)
    rk_mu = unif((N_EVEN, 2, RK_SHIFT), 0.0, 0.5)
    rk_w0 = unif((N_EVEN, 2, RK_WIDTH), -6.0, -1.0)
    rk_w2 = nrm((N_EVEN, 2, RK_DECAY_LORA, RK_WIDTH), 0.1 * RK_DECAY_LORA ** -0.5)
    rk_a0 = nrm((N_EVEN, RK_WIDTH), 0.1)
    rk_a2 = nrm((N_EVEN, RK_ICLR_LORA, RK_WIDTH), 0.5 * RK_ICLR_LORA ** -0.5)
    rk_k_k = 0.85 + nrm((N_EVEN, RK_WIDTH), 0.02)
    rk_k_a = 1.0 + nrm((N_EVEN, RK_WIDTH), 0.02)
    rk_r_k = -0.04 + nrm((N_EVEN, RK_HEADS, RK_HEAD), 0.1)
    rk_ln_w = 1.0 + nrm((N_EVEN, RK_WIDTH), 0.02)
    rk_ln_b = nrm((N_EVEN, RK_WIDTH), 0.02)
    odd_w_in = nrm((N_ODD, D_MODEL, ODD_IN), D_MODEL ** -0.5)
    hy_conv_w = nrm((N_ODD, HY_SHORT, (HY_ORDER + 1) * HY_WIDTH), HY_SHORT ** -0.5)
    hy_conv_b = nrm((N_ODD, (HY_ORDER + 1) * HY_WIDTH), 0.02)
    hy_ffn_w1 = nrm((N_ODD, HY_EMB, HY_FILTER_WIDTH), HY_EMB ** -0.5)
    hy_ffn_b1 = nrm((N_ODD, HY_FILTER_WIDTH), 0.1)
    hy_ffn_w2 = nrm((N_ODD, HY_FILTER_WIDTH, HY_FILTER_WIDTH), HY_FILTER_WIDTH ** -0.5)
    hy_ffn_b2 = nrm((N_ODD, HY_FILTER_WIDTH), 0.1)
    hy_ffn_w3 = nrm((N_ODD, HY_FILTER_WIDTH, HY_FILTER_WIDTH), HY_FILTER_WIDTH ** -0.5)
    hy_ffn_b3 = nrm((N_ODD, HY_FILTER_WIDTH), 0.1)
    hy_ffn_freq = 1.0 + nrm((N_ODD, HY_FILTER_WIDTH), 0.02)
    hy_ffn_out = nrm((N_ODD, HY_FILTER_WIDTH, HY_N_FILTERS), HY_FILTER_WIDTH ** -0.5)
    max_decay = math.log(HY_TARGET) / HY_FAST_PCT
    min_decay = math.log(HY_TARGET) / HY_SLOW_PCT
    base = jnp.abs(jnp.linspace(min_decay, max_decay, HY_WIDTH, dtype=f32))
    hy_deltas = jnp.tile(base, HY_ORDER * 2)[None] + nrm((N_ODD, HY_N_FILTERS), 0.01)
    hy_skip = nrm((N_ODD, HY_ORDER, HY_WIDTH), 0.5)
    w_out = nrm((DEPTH, MIX_WIDTH, D_MODEL), MIX_WIDTH ** -0.5 * DEEPNORM_BETA)
    ln_g = 1.0 + nrm((DEPTH, D_MODEL), 0.02)
    ln_b = nrm((DEPTH, D_MODEL), 0.02)
    ple_w = nrm((DEPTH, PLE_DIM, D_MODEL), PLE_DIM ** -0.5)
    ple_norm = 1.0 + nrm((DEPTH, D_MODEL), 0.02)
    ple_gate = nrm((DEPTH, D_MODEL, D_MODEL), D_MODEL ** -0.5)
    return {'x': x, 'p': p, 'even_w_in': even_w_in, 'dn_conv': dn_conv, 'dn_a_log': dn_a_log,
            'dn_dt_bias': dn_dt_bias, 'dn_norm': dn_norm, 'rk_mu': rk_mu, 'rk_w0': rk_w0, 'rk_w2': rk_w2,
            'rk_a0': rk_a0, 'rk_a2': rk_a2, 'rk_k_k': rk_k_k, 'rk_k_a': rk_k_a, 'rk_r_k': rk_r_k,
            'rk_ln_w': rk_ln_w, 'rk_ln_b': rk_ln_b, 'odd_w_in': odd_w_in, 'hy_conv_w': hy_conv_w,
            'hy_conv_b': hy_conv_b, 'hy_ffn_w1': hy_ffn_w1, 'hy_ffn_b1': hy_ffn_b1, 'hy_ffn_w2': hy_ffn_w2,
            'hy_ffn_b2': hy_ffn_b2, 'hy_ffn_w3': hy_ffn_w3, 'hy_ffn_b3': hy_ffn_b3, 'hy_ffn_freq': hy_ffn_freq,
            'hy_ffn_out': hy_ffn_out, 'hy_deltas': hy_deltas, 'hy_skip': hy_skip, 'w_out': w_out,
            'ln_g': ln_g, 'ln_b': ln_b, 'ple_w': ple_w, 'ple_norm': ple_norm, 'ple_gate': ple_gate}


def reference(x, p, even_w_in, dn_conv, dn_a_log, dn_dt_bias, dn_norm, rk_mu, rk_w0, rk_w2, rk_a0, rk_a2,
              rk_k_k, rk_k_a, rk_r_k, rk_ln_w, rk_ln_b, odd_w_in, hy_conv_w, hy_conv_b, hy_ffn_w1, hy_ffn_b1,
              hy_ffn_w2, hy_ffn_b2, hy_ffn_w3, hy_ffn_b3, hy_ffn_freq, hy_ffn_out, hy_deltas, hy_skip,
              w_out, ln_g, ln_b, ple_w, ple_norm, ple_gate):
    h = x
    for i in range(DEPTH):
        j = i // 2
        if i % 2 == 0:
            mix = even_mixer(h, even_w_in[j], dn_conv[j], dn_a_log[j], dn_dt_bias[j], dn_norm[j], rk_mu[j],
                             rk_w0[j], rk_w2[j], rk_a0[j], rk_a2[j], rk_k_k[j], rk_k_a[j], rk_r_k[j],
                             rk_ln_w[j], rk_ln_b[j])
        else:
            mix = odd_mixer(h, odd_w_in[j], hy_conv_w[j], hy_conv_b[j], hy_ffn_w1[j], hy_ffn_b1[j],
                            hy_ffn_w2[j], hy_ffn_b2[j], hy_ffn_w3[j], hy_ffn_b3[j], hy_ffn_freq[j],
                            hy_ffn_out[j], hy_deltas[j], hy_skip[j])
        h = layer_norm(DEEPNORM_ALPHA * h + mix @ w_out[i], ln_g[i], ln_b[i])
        e = rms_norm(p[i] @ ple_w[i], ple_norm[i])
        h = h + jax.nn.sigmoid(h @ ple_gate[i]) * e
    return h
```

```python
import contextlib, math
import numpy as np
import concourse.bass as bass
import concourse.mybir as mybir
from concourse.bass_utils import run_bass_kernel_spmd

F32 = mybir.dt.float32
BF16 = mybir.dt.bfloat16
AF = mybir.ActivationFunctionType
ALU = mybir.AluOpType
AX = mybir.AxisListType

D = 1024
NCORES = 8
ALPHA = 4.0 ** 0.25
LN_EPS = 1e-5
RMS_EPS = 1e-6
GN_EPS = 64e-5
EVEN_IN_PAD = 4352


class Buf:
    __slots__ = ("name", "w", "r")

    def __init__(s, name):
        s.name = name
        s.w = {}
        s.r = {}


class Dom:
    def __init__(s, name, sem, inc):
        s.name = name
        s.sem = sem
        s.inc = inc
        s.count = 0


class Stream:
    def __init__(s, name, eng):
        s.name = name
        s.eng = eng
        s.seen = {}


class T:
    def __init__(s, t, buf):
        s.t = t
        s.buf = buf

    def __getitem__(s, k):
        return s.t[k]


class Ctx:
    def __init__(s, nc):
        s.nc = nc
        s.stacks = [contextlib.ExitStack()]
        s.nwait = 0
        s.ninst = 0
        s.uid = 0

        def sem(n):
            return s.stacks[0].enter_context(nc.semaphore(n))

        s.st = {"pe": Stream("pe", nc.tensor), "act": Stream("act", nc.scalar), "dve": Stream("dve", nc.vector),
                "pool": Stream("pool", nc.gpsimd), "sp": Stream("sp", nc.sync)}
        s.dom = {"pe": Dom("pe", sem("s_pe"), 1), "act": Dom("act", sem("s_act"), 1),
                 "dve": Dom("dve", sem("s_dve"), 1), "pool": Dom("pool", sem("s_pool"), 1)}
        s.M = 12
        s.dma_doms = {}
        s.dma_rr = {}
        for q in ("sp", "pool"):
            s.dma_doms[q] = [Dom(f"{q}_dma{i}", sem(f"s_{q}d{i}"), 16) for i in range(s.M)]
            s.dma_rr[q] = 0
            for dm in s.dma_doms[q]:
                s.dom[dm.name] = dm
        s.rr = 0

    def push(s):
        s.stacks.append(contextlib.ExitStack())

    def pop(s):
        s.barrier()
        s.stacks.pop().close()

    def sb(s, shape, dt=F32, name=None):
        s.uid += 1
        name = f"{name or 'sb'}_{s.uid}"
        t = s.stacks[-1].enter_context(s.nc.sbuf_tensor(name, list(shape), dt))
        return T(t, Buf(name))

    def ps(s, shape, dt=F32, name=None):
        s.uid += 1
        name = name or f"ps{s.uid}"
        t = s.stacks[-1].enter_context(s.nc.psum_tensor(name, list(shape), dt))
        return T(t, Buf(name))

    def dram(s, name, shape, dt=F32, kind="Internal"):
        t = s.nc.dram_tensor(name, list(shape), dt, kind=kind)
        return T(t.ap(), Buf(name))

    def _bufs(s, xs):
        out = []
        for x in xs:
            if x is None:
                continue
            out.append(x.buf if isinstance(x, T) else x)
        return out

    def emit(s, stream, dom, fn, reads=(), writes=(), acc=(), pre=None):
        st = s.st[stream]
        dm = s.dom[dom]
        deps = {}
        if pre is not None and pre[1] > 0:
            deps[pre[0]] = pre[1]

        def add(d):
            for k, v in d.items():
                if deps.get(k, 0) < v:
                    deps[k] = v

        for b in s._bufs(reads):
            add(b.w)
        for b in s._bufs(writes):
            add(b.w)
            add(b.r)
        for b in s._bufs(acc):
            add(b.r)
        for k, v in deps.items():
            if k == dom and dm.inc == 1:
                if stream == "pe":
                    continue
                if v <= dm.count - 8:
                    continue
            if st.seen.get(k, 0) >= v:
                continue
            st.eng.wait_ge(s.dom[k].sem, v * s.dom[k].inc)
            st.seen[k] = v
            s.nwait += 1
        ins = fn(st.eng)
        dm.count += 1
        n = dm.count
        ins.then_inc(dm.sem, dm.inc)
        s.ninst += 1
        for b in s._bufs(reads):
            if b.r.get(dom, 0) < n:
                b.r[dom] = n
        for b in s._bufs(writes):
            b.w = {dom: n}
            b.r = {}
        for b in s._bufs(acc):
            b.w[dom] = n
        return ins

    def pe(s, fn, reads=(), writes=(), acc=()):
        return s.emit("pe", "pe", fn, reads, writes, acc)

    def act(s, fn, reads=(), writes=(), acc=()):
        return s.emit("act", "act", fn, reads, writes, acc)

    def dve(s, fn, reads=(), writes=(), acc=()):
        return s.emit("dve", "dve", fn, reads, writes, acc)

    def pool(s, fn, reads=(), writes=(), acc=()):
        return s.emit("pool", "pool", fn, reads, writes, acc)

    def any2(s, fn, reads=(), writes=(), acc=()):
        s.rr += 1
        if s.rr % 3 == 0:
            return s.pool(fn, reads, writes, acc)
        return s.dve(fn, reads, writes, acc)

    def dma(s, q, out, in_, reads=(), writes=(), acc=(), **kw):
        stream = {"sp": "sp", "pool": "pool"}[q]
        dm = s.dma_doms[q][s.dma_rr[q] % s.M]
        s.dma_rr[q] += 1
        return s.emit(stream, dm.name, lambda e: e.dma_start(out=out, in_=in_, **kw), reads, writes, acc, pre=(dm.name, dm.count))

    def barrier(s):
        for st in s.st.values():
            for k, dm in s.dom.items():
                if dm.count > 0 and st.seen.get(k, 0) < dm.count:
                    if k == st.name and dm.inc == 1 and st.name == "pe":
                        continue
                    st.eng.wait_ge(dm.sem, dm.count * dm.inc)
                    st.seen[k] = dm.count
                    s.nwait += 1

    def close(s):
        while s.stacks:
            s.stacks.pop().close()


def bc(ap, shape):
    return ap.to_broadcast(list(shape))


def host_consts(L):
    c = {}
    i = np.arange(128)
    c["ident"] = np.eye(128, dtype=np.float32)
    c["ones"] = np.ones((128, 128), np.float32)
    bd = np.zeros((128, 128), np.float32)
    bd[:64, :64] = 1
    bd[64:, 64:] = 1
    c["bd"] = bd
    J, C = np.meshgrid(i, i, indexing="ij")
    su = (C > J).astype(np.float32)
    sl = (C < J).astype(np.float32)
    iu = (C >= J).astype(np.float32)
    il = (C <= J).astype(np.float32)
    def rep4(m):
        return np.ascontiguousarray(np.broadcast_to(m[:, None, :], (128, 4, 128))).astype(np.float32)
    for d, (tri, sT, s_, iT) in enumerate([(iu, su, sl, iu), (il, sl, su, il)]):
        c[f"tri{d}"] = tri.astype(np.float32)
        c[f"nsT{d}"] = rep4(-sT)
        c[f"ns{d}"] = rep4(-s_)
        c[f"iT{d}"] = rep4(iT)
        mid = 63 if d == 0 else 64
        sel = (tri[:, mid:mid + 1] * np.ones((1, 128))).astype(np.float32)
        c[f"trim{d}"] = (tri - sel).astype(np.float32)
        c[f"suf{d}"] = (sl if d == 0 else su).astype(np.float32)
        m2 = np.stack([sT, iT], 0)
        c[f"mpa{d}"] = np.ascontiguousarray(np.broadcast_to(m2.transpose(1, 0, 2)[:, None], (128, 4, 2, 128))).astype(np.float32)
        c[f"ps{d}"] = rep4(s_)
    return c


CONST_SHAPES = {"ident": [128, 128], "ones": [128, 128], "bd": [128, 128]}
for _d in range(2):
    CONST_SHAPES.update({f"tri{_d}": [128, 128], f"nsT{_d}": [128, 4, 128], f"ns{_d}": [128, 4, 128],
                         f"iT{_d}": [128, 4, 128], f"trim{_d}": [128, 128], f"suf{_d}": [128, 128],
                         f"mpa{_d}": [128, 4, 2, 128], f"ps{_d}": [128, 4, 128]})


class Prog:
    def __init__(s, L, debug=False, layers=(0, 1)):
        s.L = L
        s.NT = L // 128
        s.TB = min(512, L)
        s.NB = L // s.TB
        s.TBp = min(256, L)
        s.debug = debug
        s.layers = layers
        s.nc = bass.Bass("TRN2", target_bir_lowering=False)
        s.c = Ctx(s.nc)
        s.inputs = {}
        s.dbg_out = []

    def inp(s, name, shape):
        t = s.nc.dram_tensor(name, list(shape), F32, kind="ExternalInput").ap()
        s.inputs[name] = T(t, Buf(name))
        return s.inputs[name]

    def scratch(s, name, shape, dt=F32):
        kind = "ExternalOutput" if (s.debug and dt == F32) else "Internal"
        t = s.c.dram(name, shape, dt, kind=kind)
        if kind == "ExternalOutput":
            s.dbg_out.append(name)
        return t

    def load_const(s, name, shape, dt=F32, q="sp"):
        src = s.inp(name, shape)
        t = s.c.sb(shape, F32, name="c_" + name)
        s.c.dma(q, t[:], src[:], reads=[src], writes=[t])
        return t

    def load_bc(s, src_ap, n, srcbuf, name):
        t = s.c.sb([128, n], F32, name=name)
        s.c.dma("sp", t[:], src_ap.partition_broadcast(128), reads=[srcbuf], writes=[t])
        return t

    def psum_banks(s):
        s.P = [s.c.ps([128, 512], F32, name=f"bank{i}") for i in range(8)]

    def proj_fm(s, srcT, W, ncol, dst, tm_cols=None, tm_dst=None):
        c = s.c
        L, TB, NB = s.L, s.TB, s.NB
        nct = ncol // 128
        c.push()
        Wb = c.sb([128, 8, ncol], BF16, name="Wb")
        stage = c.sb([128, ncol], F32, name="wstage")
        Wv = W.t.rearrange("(k p) n -> p k n", p=128)
        for k in range(8):
            c.dma("sp", stage[:], Wv[:, k, :], reads=[W], writes=[stage])
            if k % 2 == 0:
                c.act(lambda e: e.copy(Wb[:, k, :], stage[:]), reads=[stage], acc=[Wb])
            else:
                c.dve(lambda e: e.tensor_copy(Wb[:, k, :], stage[:]), reads=[stage], acc=[Wb])
        xs = [c.sb([128, 8, TB], F32, name="xs0")] * 2
        xb = [c.sb([128, 8, TB], BF16, name=f"xb{i}") for i in range(2)]
        ob = [c.sb([128, 4, TB], F32, name=f"ob{i}") for i in range(2)]
        xv = srcT.t.rearrange("(k p) l -> p k l", p=128)
        dv = dst.t.rearrange("(g p) l -> p g l", p=128)
        nob = 0
        for b in range(NB):
            t0 = b * TB
            X, XB = xs[b % 2], xb[b % 2]
            c.dma("sp", X[:], xv[:, :, t0:t0 + TB], reads=[srcT], writes=[X])
            c.act(lambda e: e.copy(XB[:, 0:4, :], X[:, 0:4, :]), reads=[X], writes=[XB])
            c.dve(lambda e: e.tensor_copy(XB[:, 4:8, :], X[:, 4:8, :]), reads=[X], acc=[XB])
            for g0 in range(0, nct, 4):
                gn = min(4, nct - g0)
                O = ob[nob % 2]
                nob += 1
                for gi in range(gn):
                    ct = g0 + gi
                    pb = s.P[(ct) % 8]
                    for k in range(8):
                        c.pe(lambda e: e.matmul(pb[:, 0:TB], Wb[:, k, ct * 128:(ct + 1) * 128], XB[:, k, :],
                                                start=(k == 0), stop=(k == 7)),
                             reads=[Wb, XB], writes=[pb] if k == 0 else (), acc=[pb] if k else ())
                    if ct % 2 == 0:
                        c.act(lambda e: e.copy(O[:, gi, :], pb[:, 0:TB]), reads=[pb], writes=[O] if gi == 0 else (), acc=[O] if gi else ())
                    else:
                        c.dve(lambda e: e.tensor_copy(O[:, gi, :], pb[:, 0:TB]), reads=[pb], writes=[O] if gi == 0 else (), acc=[O] if gi else ())
                c.dma("pool", dv[:, g0:g0 + gn, t0:t0 + TB], O[:, 0:gn, :], reads=[O], acc=[dst])
            if tm_cols is not None:
                c0, c1 = tm_cols
                for tt in range(TB // 128):
                    pb = s.P[tt % 2]
                    for k in range(8):
                        c.pe(lambda e: e.matmul(pb[:, 0:c1 - c0], XB[:, k, tt * 128:(tt + 1) * 128], Wb[:, k, c0:c1],
                                                start=(k == 0), stop=(k == 7)),
                             reads=[Wb, XB], writes=[pb] if k == 0 else (), acc=[pb] if k else ())
                    ti = b * (TB // 128) + tt
                    c.dve(lambda e: e.tensor_copy(tm_dst[:, ti, :], pb[:, 0:c1 - c0]), reads=[pb], acc=[tm_dst])
        c.pop()

    def dn_gates(s, ABraw, GB, dtb, alog):
        c = s.c
        NT = s.NT
        c.push()
        x = c.sb([128, NT, 8], name="gx")
        t1 = c.sb([128, NT, 8], name="gt1")
        t2 = c.sb([128, NT, 8], name="gt2")
        nA = c.sb([128, 8], name="gnA")
        c.act(lambda e: e.activation(out=nA[:], in_=alog[:], func=AF.Exp), reads=[alog], writes=[nA])
        c.dve(lambda e: e.tensor_tensor(x[:], ABraw[:, :, 0:8], bc(dtb[:, :].unsqueeze(1), [128, NT, 8]), ALU.add),
              reads=[ABraw, dtb], writes=[x])
        c.act(lambda e: e.activation(out=t1[:], in_=x[:], func=AF.Abs), reads=[x], writes=[t1])
        c.act(lambda e: e.activation(out=t1[:], in_=t1[:], func=AF.Exp, scale=-1.0), reads=[t1], writes=[t1])
        c.dve(lambda e: e.tensor_scalar_add(t1[:], t1[:], 1.0), reads=[t1], writes=[t1])
        c.act(lambda e: e.activation(out=t1[:], in_=t1[:], func=AF.Ln), reads=[t1], writes=[t1])
        c.dve(lambda e: e.tensor_scalar_max(t2[:], x[:], 0.0), reads=[x], writes=[t2])
        c.dve(lambda e: e.tensor_tensor(t2[:], t2[:], t1[:], ALU.add), reads=[t1, t2], writes=[t2])
        c.dve(lambda e: e.scalar_tensor_tensor(GB[:, :, 0:8], t2[:], -1.0, bc(nA[:, :].unsqueeze(1), [128, NT, 8]),
                                               ALU.mult, ALU.mult), reads=[t2, nA], acc=[GB])
        c.act(lambda e: e.activation(out=GB[:, :, 8:16], in_=ABraw[:, :, 8:16], func=AF.Sigmoid), reads=[ABraw], acc=[GB])
        c.pop()

    def dn_prep(s, projT, CW, qT, kT, k_tm, v_tm):
        c = s.c
        L, TB, NB = s.L, s.TB, s.NB
        NTB = TB // 128
        c.push()
        pv = projT.t[0:1536, :].rearrange("(t h p) l -> p t h l", t=3, h=4)
        ktv = k_tm.t.rearrange("(n p) h d -> p n h d", p=128)
        vtv = v_tm.t.rearrange("(n p) h d -> p n h d", p=128)
        ident, ones = s.K["ident"], s.K["ones"]
        Xs = [c.sb([128, 3, TB + 4], name=f"dnX{i}") for i in range(2)]
        As = [c.sb([128, 3, TB], name=f"dnA{i}") for i in range(2)]
        Ss = [c.sb([128, 2, TB], name=f"dnS{i}") for i in range(2)]
        KVs = [c.sb([128, 2, NTB, 128], name=f"dnK{i}") for i in range(2)]
        it = 0
        for h in range(4):
            for b in range(NB):
                t0 = b * TB
                X, A, S, KV = Xs[it % 2], As[it % 2], Ss[it % 2], KVs[it % 2]
                it += 1
                lo = max(t0 - 2, 0)
                hi = min(t0 + TB + 2, L)
                if lo != t0 - 2 or hi != t0 + TB + 2:
                    c.dve(lambda e: e.memset(X[:], 0.0), writes=[X])
                c.dma("sp", X[:, :, lo - (t0 - 2):hi - (t0 - 2)], pv[:, :, h, lo:hi], reads=[projT], writes=[X])
                for t in range(3):
                    c.dve(lambda e: e.tensor_scalar(A[:, t, :], X[:, t, 0:TB], CW[:, t, h, 0:1], None, ALU.mult),
                          reads=[X, CW], writes=[A])
                    for j in range(1, 5):
                        c.dve(lambda e: e.scalar_tensor_tensor(A[:, t, :], X[:, t, j:j + TB], CW[:, t, h, j:j + 1], A[:, t, :],
                                                               ALU.mult, ALU.add), reads=[X, CW, A], writes=[A])
                c.act(lambda e: e.activation(out=A[:], in_=A[:], func=AF.Silu), reads=[A], writes=[A])
                c.act(lambda e: e.activation(out=S[:], in_=A[:, 0:2, :], func=AF.Square), reads=[A], writes=[S])
                for t in range(2):
                    pb = s.P[t]
                    c.pe(lambda e: e.matmul(pb[:, 0:TB], ones[:], S[:, t, :], start=True, stop=True), reads=[S, ones], writes=[pb])
                for t in range(2):
                    pb = s.P[t]
                    sc = 128.0 if t == 0 else 1.0
                    c.dve(lambda e: e.tensor_scalar(S[:, t, :], pb[:, 0:TB], sc, RMS_EPS * sc, ALU.mult, ALU.add),
                          reads=[pb, S], writes=[S])
                c.act(lambda e: e.activation(out=S[:], in_=S[:], func=AF.Sqrt), reads=[S], writes=[S])
                c.dve(lambda e: e.reciprocal(S[:], S[:]), reads=[S], writes=[S])
                c.dve(lambda e: e.tensor_tensor(A[:, 0:2, :], A[:, 0:2, :], S[:], ALU.mult), reads=[A, S], writes=[A])
                c.dma("pool", qT[:, h, t0:t0 + TB], A[:, 0, :], reads=[A], acc=[qT])
                c.dma("pool", kT[:, h, t0:t0 + TB], A[:, 1, :], reads=[A], acc=[kT])
                for ti, t in enumerate((1, 2)):
                    pb = s.P[2 + ti]
                    for tt in range(NTB):
                        c.pe(lambda e: e.transpose(pb[:, tt * 128:(tt + 1) * 128], A[:, t, tt * 128:(tt + 1) * 128], ident[:]),
                             reads=[A, ident], writes=[pb])
                for ti in range(2):
                    pb = s.P[2 + ti]
                    if ti == 0:
                        c.act(lambda e: e.copy(KV[:, ti].rearrange("p n d -> p (n d)"), pb[:, 0:TB]), reads=[pb, KV], writes=[KV])
                    else:
                        c.dve(lambda e: e.tensor_copy(KV[:, ti].rearrange("p n d -> p (n d)"), pb[:, 0:TB]), reads=[pb, KV], writes=[KV])
                n0 = t0 // 128
                c.dma("pool", ktv[:, n0:n0 + NTB, h, :], KV[:, 0], reads=[KV], acc=[k_tm])
                c.dma("pool", vtv[:, n0:n0 + NTB, h, :], KV[:, 1], reads=[KV], acc=[v_tm])
        c.pop()

    def rk_prep(s, projT, rk_tm, bonusT):
        c = s.c
        L = s.L
        TB = min(256, L)
        NB = L // TB
        NTB = TB // 128
        I = s.inp
        c.push()
        MU = s.load_const("rk_mu_rkv", [128, 12, 2])
        MUL = s.load_const("rk_mu_lora", [64, 3, 2])
        W2 = s.load_const("rk_w2", [64, 2, 512])
        A2 = s.load_const("rk_a2", [64, 512])
        COLS = s.load_const("rk_cols", [128, 6, 4])
        w0_d = I("rk_w0", [1, 1024]); a0_d = I("rk_a0", [1, 512])
        W0 = s.load_bc(w0_d.t, 1024, w0_d, "rk_w0bc")
        A0 = s.load_bc(a0_d.t, 512, a0_d, "rk_a0bc")
        bd, ident = s.K["bd"], s.K["ident"]
        C0 = c.sb([128, 12], name="rkC0"); C0L = c.sb([64, 3], name="rkC0L"); OMK = c.sb([128, 4], name="rkOMK")
        c.dve(lambda e: e.tensor_tensor(C0[:], MU[:, :, 0], MU[:, :, 1], ALU.add), reads=[MU], writes=[C0])
        c.dve(lambda e: e.tensor_scalar(C0[:], C0[:], -1.0, 1.0, ALU.mult, ALU.add), reads=[C0], writes=[C0])
        c.dve(lambda e: e.tensor_tensor(C0L[:], MUL[:, :, 0], MUL[:, :, 1], ALU.add), reads=[MUL], writes=[C0L])
        c.dve(lambda e: e.tensor_scalar(C0L[:], C0L[:], -1.0, 1.0, ALU.mult, ALU.add), reads=[C0L], writes=[C0L])
        c.dve(lambda e: e.tensor_scalar(OMK[:], COLS[:, 2, :], -1.0, 1.0, ALU.mult, ALU.add), reads=[COLS], writes=[OMK])
        X = c.sb([128, 12, TB + 2], name="rkX"); XL = c.sb([64, 3, TB + 2], name="rkXL")
        Sx = c.sb([128, 12, TB], name="rkSx"); SL = c.sb([64, 3, TB], name="rkSL")
        TMP = c.sb([128, 12, TB], name="rkTMP")
        ET = c.sb([128, 4, TB], name="rkET"); KP = c.sb([128, 4, TB], name="rkKP"); KK = c.sb([128, 4, TB], name="rkKK")
        RS = c.sb([128, 4, TB], name="rkRS")
        TM = [c.sb([128, NTB, 512], name=f"rkTM{i}") for i in range(7)]
        pv = projT.t[2064:2064 + 1536, :].rearrange("(t p) l -> p t l", p=128)
        pl = projT.t[3600:3600 + 192, :].rearrange("(t p) l -> p t l", p=64)
        tmv = rk_tm.t.rearrange("i (n p) c -> i p n c", p=128)
        for b in range(NB):
            t0 = b * TB
            lo = max(t0 - 1, 0); hi = min(t0 + TB + 1, L)
            if lo != t0 - 1 or hi != t0 + TB + 1:
                c.dve(lambda e: e.memset(X[:], 0.0), writes=[X])
                c.dve(lambda e: e.memset(XL[:], 0.0), writes=[XL])
            c.dma("sp", X[:, :, lo - (t0 - 1):hi - (t0 - 1)], pv[:, :, lo:hi], reads=[projT], writes=[X])
            c.dma("sp", XL[:, :, lo - (t0 - 1):hi - (t0 - 1)], pl[:, :, lo:hi], reads=[projT], writes=[XL])
            c.dve(lambda e: e.tensor_tensor(Sx[:], X[:, :, 1:TB + 1], bc(C0[:, :].unsqueeze(2), [128, 12, TB]), ALU.mult), reads=[X, C0], writes=[Sx])
            c.dve(lambda e: e.tensor_tensor(TMP[:], X[:, :, 0:TB], bc(MU[:, :, 0:1], [128, 12, TB]), ALU.mult), reads=[X, MU], writes=[TMP])
            c.dve(lambda e: e.tensor_tensor(Sx[:], Sx[:], TMP[:], ALU.add), reads=[Sx, TMP], writes=[Sx])
            c.dve(lambda e: e.tensor_tensor(TMP[:], X[:, :, 2:TB + 2], bc(MU[:, :, 1:2], [128, 12, TB]), ALU.mult), reads=[X, MU, TMP], writes=[TMP])
            c.dve(lambda e: e.tensor_tensor(Sx[:], Sx[:], TMP[:], ALU.add), reads=[Sx, TMP], writes=[Sx])
            TL = TMP[0:64, 0:3, :]
            c.dve(lambda e: e.tensor_tensor(SL[:], XL[:, :, 1:TB + 1], bc(C0L[:, :].unsqueeze(2), [64, 3, TB]), ALU.mult), reads=[XL, C0L, TMP], writes=[SL])
            c.dve(lambda e: e.tensor_tensor(TL, XL[:, :, 0:TB], bc(MUL[:, :, 0:1], [64, 3, TB]), ALU.mult), reads=[XL, MUL, TMP], writes=[TMP])
            c.dve(lambda e: e.tensor_tensor(SL[:], SL[:], TL, ALU.add), reads=[SL, TMP], writes=[SL])
            c.dve(lambda e: e.tensor_tensor(TL, XL[:, :, 2:TB + 2], bc(MUL[:, :, 1:2], [64, 3, TB]), ALU.mult), reads=[XL, MUL, TMP], writes=[TMP])
            c.dve(lambda e: e.tensor_tensor(SL[:], SL[:], TL, ALU.add), reads=[SL, TMP], writes=[SL])
            c.act(lambda e: e.activation(out=SL[:, 0:2, :], in_=SL[:, 0:2, :], func=AF.Tanh), reads=[SL], writes=[SL])
            for tt in range(NTB):
                tsl = slice(tt * 128, (tt + 1) * 128)
                for i, (lsrc, rhs, addv, dsti) in enumerate([(SL[:, 0, tsl], W2[:, 0, :], W0[:, 0:512], 5),
                                                             (SL[:, 1, tsl], W2[:, 1, :], W0[:, 512:1024], 6),
                                                             (SL[:, 2, tsl], A2[:, :], A0[:, :], 4)]):
                    pb = s.P[4 + i]
                    c.pe(lambda e: e.matmul(pb[:, :], lsrc, rhs, start=True, stop=True), reads=[SL, W2, A2], writes=[pb])
                    c.dve(lambda e: e.tensor_tensor(TM[dsti][:, tt, :], pb[:, :], addv, ALU.add), reads=[pb, W0, A0, TM[dsti]], writes=[TM[dsti]])
            for dsti in (4, 5, 6):
                c.act(lambda e: e.activation(out=TM[dsti][:], in_=TM[dsti][:], func=AF.Sigmoid), reads=[TM[dsti]], writes=[TM[dsti]])
            for dsti in (5, 6):
                c.dve(lambda e: e.tensor_scalar(TM[dsti][:], TM[dsti][:], -math.exp(-0.5), None, ALU.mult), reads=[TM[dsti]], writes=[TM[dsti]])
            for ct in range(4):
                pb = s.P[ct]
                c.pe(lambda e: e.matmul(pb[:, 0:TB], A2[:, ct * 128:(ct + 1) * 128], SL[:, 2, :], start=True, stop=True), reads=[SL, A2], writes=[pb])
                c.act(lambda e: e.activation(out=ET[:, ct, :], in_=pb[:, 0:TB], func=AF.Sigmoid, bias=COLS[:, 0, ct:ct + 1]), reads=[pb, COLS, ET], writes=[ET])
                c.dve(lambda e: e.tensor_scalar(ET[:, ct, :], ET[:, ct, :], COLS[:, 2, ct:ct + 1], OMK[:, ct:ct + 1], ALU.mult, ALU.add), reads=[ET, COLS, OMK], writes=[ET])
            c.dve(lambda e: e.tensor_tensor(KP[:], Sx[:, 4:8, :], ET[:], ALU.mult), reads=[Sx, ET], writes=[KP])
            c.dve(lambda e: e.tensor_tensor(KK[:], Sx[:, 4:8, :], bc(COLS[:, 1, :].unsqueeze(2), [128, 4, TB]), ALU.mult), reads=[Sx, COLS], writes=[KK])
            c.act(lambda e: e.activation(out=RS[:], in_=KK[:], func=AF.Square), reads=[KK], writes=[RS])
            for ct in range(4):
                pb = s.P[ct]
                c.pe(lambda e: e.matmul(pb[:, 0:TB], bd[:], RS[:, ct, :], start=True, stop=True), reads=[RS, bd], writes=[pb])
            for ct in range(4):
                pb = s.P[ct]
                c.dve(lambda e: e.tensor_scalar_add(RS[:, ct, :], pb[:, 0:TB], RMS_EPS), reads=[pb, RS], writes=[RS])
            c.act(lambda e: e.activation(out=RS[:], in_=RS[:], func=AF.Sqrt), reads=[RS], writes=[RS])
            c.dve(lambda e: e.reciprocal(RS[:], RS[:]), reads=[RS], writes=[RS])
            c.dve(lambda e: e.tensor_tensor(KK[:], KK[:], RS[:], ALU.mult), reads=[KK, RS], writes=[KK])
            c.dve(lambda e: e.tensor_tensor(RS[:], Sx[:, 0:4, :], KP[:], ALU.mult), reads=[Sx, KP, RS], writes=[RS])
            c.dve(lambda e: e.tensor_tensor(RS[:], RS[:], bc(COLS[:, 3, :].unsqueeze(2), [128, 4, TB]), ALU.mult), reads=[RS, COLS], writes=[RS])
            for ct in range(4):
                pb = s.P[ct]
                c.pe(lambda e: e.matmul(pb[:, 0:TB], bd[:], RS[:, ct, :], start=True, stop=True), reads=[RS, bd], writes=[pb])
            for ct in range(4):
                pb = s.P[ct]
                c.dve(lambda e: e.tensor_tensor(ET[:, ct, :], pb[:, 0:TB], Sx[:, 8 + ct, :], ALU.mult), reads=[pb, Sx, ET], writes=[ET])
            c.dma("pool", bonusT[:, :, t0:t0 + TB], ET[:], reads=[ET], acc=[bonusT])
            for i, src in enumerate([Sx[:, 0:4, :], KP[:], Sx[:, 8:12, :], KK[:]]):
                for tt in range(NTB):
                    pb = s.P[4 + (i * NTB + tt) % 4]
                    for ct in range(4):
                        c.pe(lambda e: e.transpose(pb[:, ct * 128:(ct + 1) * 128], src[:, ct, tt * 128:(tt + 1) * 128], ident[:]),
                             reads=[Sx, KP, KK, ident], writes=[pb])
                    if (i + tt) % 2 == 0:
                        c.act(lambda e: e.copy(TM[i][:, tt, :], pb[:, :]), reads=[pb, TM[i]], writes=[TM[i]])
                    else:
                        c.dve(lambda e: e.tensor_copy(TM[i][:, tt, :], pb[:, :]), reads=[pb, TM[i]], writes=[TM[i]])
            n0 = t0 // 128
            for i in range(7):
                c.dma("pool", tmv[i, :, n0:n0 + NTB, :], TM[i][:], reads=[TM[i]], acc=[rk_tm])
        c.pop()

    def tri_inverse(s, N0, P0, srcbufs, W):
        c = s.c
        ident = s.K["ident"]
        R, NA, NB_, PA_, PB_ = W["R"], W["NA"], W["NB"], W["PA"], W["PB"]
        c.dve(lambda e: e.tensor_tensor(R[:], P0, bc(ident[:, :].unsqueeze(1), [128, 4, 128]), ALU.add),
              reads=srcbufs + [ident], writes=[R])
        Ncur, Pcur, nb, pb_ = N0, P0, srcbufs, srcbufs
        Nbufs, Pbufs = [NA, NB_], [PA_, PB_]
        v4 = lambda bank: bank[:, :].rearrange("p (h c) -> p h c", h=4)
        for k in range(1, 7):
            Nn, Pn = Nbufs[k % 2], Pbufs[k % 2]
            bN, bP, bR = s.P[5], s.P[6], s.P[7]
            for h in range(4):
                c.pe(lambda e: e.matmul(bN[:, h * 128:(h + 1) * 128], Pcur[:, h, :], Ncur[:, h, :], start=True, stop=True),
                     reads=nb + pb_, writes=[bN])
            if k < 6:
                for h in range(4):
                    c.pe(lambda e: e.matmul(bP[:, h * 128:(h + 1) * 128], Ncur[:, h, :], Pcur[:, h, :], start=True, stop=True),
                         reads=nb + pb_, writes=[bP])
            c.act(lambda e: e.copy(Nn[:], v4(bN)), reads=[bN, Nn], writes=[Nn])
            if k < 6:
                c.dve(lambda e: e.tensor_copy(Pn[:], v4(bP)), reads=[bP, Pn], writes=[Pn])
            for h in range(4):
                c.pe(lambda e: e.matmul(bR[:, h * 128:(h + 1) * 128], Nn[:, h, :], R[:, h, :], start=True, stop=True),
                     reads=[Nn, R], writes=[bR])
            c.dve(lambda e: e.tensor_tensor(R[:], R[:], v4(bR), ALU.add), reads=[R, bR], writes=[R])
            Ncur, Pcur, nb, pb_ = Nn[:], Pn[:], [Nn], [Pn]
        return R

    def dn_pass(s, d, qT, kT, k_tm, v_tm, oT, W):
        c = s.c
        L, NT = s.L, s.NT
        GB = s.GB
        K = s.K
        tri, nsT, ns, iT, ident, ones = K[f"tri{d}"], K[f"nsT{d}"], K[f"ns{d}"], K[f"iT{d}"], K["ident"], K["ones"]
        lastc = 127 if d == 0 else 0
        H = W["H"]
        c.dve(lambda e: e.memset(H[:], 0.0), writes=[H])
        v4 = lambda bank: bank[:, :].rearrange("p (h c) -> p h c", h=4)
        ktv = k_tm.t.rearrange("(n p) h d -> n p h d", p=128)
        vtv = v_tm.t.rearrange("(n p) h d -> n p h d", p=128)
        order = range(NT) if d == 0 else range(NT - 1, -1, -1)
        for it, n in enumerate(order):
            ld = W["ld"][it % 2]
            KT, QT, Ktm, Vtm = ld["KT"], ld["QT"], ld["Ktm"], ld["Vtm"]
            tsl = slice(n * 128, (n + 1) * 128)
            c.dma("sp", KT[:], kT[:, :, tsl], reads=[kT], writes=[KT])
            c.dma("sp", QT[:], qT[:, :, tsl], reads=[qT], writes=[QT])
            c.dma("sp", Ktm[:], ktv[n], reads=[k_tm], writes=[Ktm])
            c.dma("sp", Vtm[:], vtv[n], reads=[v_tm], writes=[Vtm])
            g = GB[:, n, d * 4:(d + 1) * 4]
            beta = GB[:, n, 8 + d * 4:8 + (d + 1) * 4]
            Gc, RG, RB, Grow, E1, E2, X1, X2, AT = (W[k] for k in ("Gc", "RG", "RB", "Grow", "E1", "E2", "X1", "X2", "AT"))
            b0, b1, b2, b3, b4, b5, b6, b7 = s.P
            c.pe(lambda e: e.matmul(b0[:, 0:4], tri[:], g, start=True, stop=True), reads=[tri, GB], writes=[b0])
            c.act(lambda e: e.copy(Gc[:], b0[:, 0:4]), reads=[b0, Gc], writes=[Gc])
            c.dve(lambda e: e.tensor_tensor(RG[:], bc(tri[:, :].unsqueeze(1), [128, 4, 128]), bc(g.unsqueeze(2), [128, 4, 128]), ALU.mult),
                  reads=[tri, GB, RG], writes=[RG])
            c.dve(lambda e: e.tensor_tensor(RB[:], bc(ident[:, :].unsqueeze(1), [128, 4, 128]), bc(beta.unsqueeze(2), [128, 4, 128]), ALU.mult),
                   reads=[ident, GB, RB], writes=[RB])
            c.pe(lambda e: e.matmul(b1[:, :], ones[:], RG[:].rearrange("p h c -> p (h c)"), start=True, stop=True), reads=[ones, RG], writes=[b1])
            c.pe(lambda e: e.matmul(b2[:, :], ones[:], RB[:].rearrange("p h c -> p (h c)"), start=True, stop=True), reads=[ones, RB], writes=[b2])
            for h in range(4):
                c.pe(lambda e: e.matmul(b3[:, h * 128:(h + 1) * 128], KT[:, h, :], KT[:, h, :], start=True, stop=True), reads=[KT], writes=[b3])
            for h in range(4):
                c.pe(lambda e: e.matmul(b4[:, h * 128:(h + 1) * 128], KT[:, h, :], QT[:, h, :], start=True, stop=True), reads=[KT, QT], writes=[b4])
            c.act(lambda e: e.copy(Grow[:], v4(b1)), reads=[b1, Grow], writes=[Grow])
            c.dve(lambda e: e.tensor_tensor(E1[:], Grow[:], bc(Gc[:, :].unsqueeze(2), [128, 4, 128]), ALU.subtract), reads=[Grow, Gc, E1], writes=[E1])
            c.dve(lambda e: e.tensor_scalar_max(E2[:], E1[:], 0.0), reads=[E1, E2], writes=[E2])
            c.dve(lambda e: e.tensor_scalar_min(E1[:], E1[:], 0.0), reads=[E1], writes=[E1])
            c.act(lambda e: e.activation(out=E1[:], in_=E1[:], func=AF.Exp), reads=[E1], writes=[E1])
            c.act(lambda e: e.activation(out=E2[:], in_=E2[:], func=AF.Exp, scale=-1.0), reads=[E2], writes=[E2])
            c.dve(lambda e: e.tensor_tensor(X1[:], E1[:], nsT[:], ALU.mult), reads=[E1, nsT, X1], writes=[X1])
            c.dve(lambda e: e.tensor_tensor(X1[:], X1[:], v4(b2), ALU.mult), reads=[X1, b2], writes=[X1])
            c.dve(lambda e: e.tensor_tensor(X1[:], X1[:], v4(b3), ALU.mult), reads=[X1, b3], writes=[X1])
            c.dve(lambda e: e.tensor_tensor(X2[:], E2[:], ns[:], ALU.mult), reads=[E2, ns, X2], writes=[X2])
            c.dve(lambda e: e.tensor_tensor(X2[:], X2[:], bc(beta.unsqueeze(2), [128, 4, 128]), ALU.mult), reads=[X2, GB], writes=[X2])
            c.dve(lambda e: e.tensor_tensor(X2[:], X2[:], v4(b3), ALU.mult), reads=[X2, b3], writes=[X2])
            c.dve(lambda e: e.tensor_tensor(AT[:], E1[:], iT[:], ALU.mult), reads=[E1, iT, AT], writes=[AT])
            c.dve(lambda e: e.tensor_tensor(AT[:], AT[:], v4(b4), ALU.mult), reads=[AT, b4], writes=[AT])
            R = s.tri_inverse(X2[:], X1[:], [X1, X2], W)
            S4, CO, EGT, GAM, KBN, VB, KTL, QD = (W[k] for k in ("S4", "CO", "EGT", "GAM", "KBN", "VB", "KTL", "QD"))
            c.act(lambda e: e.activation(out=S4[:], in_=Gc[:], func=AF.Exp), reads=[Gc, S4], writes=[S4])
            c.dve(lambda e: e.scalar_tensor_tensor(CO[:], S4[:], -1.0, beta, ALU.mult, ALU.mult), reads=[S4, GB, CO], writes=[CO])
            c.dve(lambda e: e.tensor_tensor(KBN[:], Ktm[:], bc(CO[:, :].unsqueeze(2), [128, 4, 128]), ALU.mult), reads=[Ktm, CO, KBN], writes=[KBN])
            c.dve(lambda e: e.tensor_tensor(VB[:], Vtm[:], bc(beta.unsqueeze(2), [128, 4, 128]), ALU.mult), reads=[Vtm, GB, VB], writes=[VB])
            c.dve(lambda e: e.tensor_tensor(EGT[:], Grow[:, :, lastc], Gc[:], ALU.subtract), reads=[Grow, Gc, EGT], writes=[EGT])
            c.act(lambda e: e.activation(out=EGT[:], in_=EGT[:], func=AF.Exp), reads=[EGT], writes=[EGT])
            c.act(lambda e: e.activation(out=GAM[:], in_=Grow[:, :, lastc], func=AF.Exp), reads=[Grow, GAM], writes=[GAM])
            c.dve(lambda e: e.tensor_tensor(KTL[:], Ktm[:], bc(EGT[:, :].unsqueeze(2), [128, 4, 128]), ALU.mult), reads=[Ktm, EGT, KTL], writes=[KTL])
            c.act(lambda e: e.activation(out=QD[:], in_=Grow[:], func=AF.Exp), reads=[Grow, QD], writes=[QD])
            c.dve(lambda e: e.tensor_tensor(QD[:], QD[:], QT[:], ALU.mult), reads=[QD, QT], writes=[QD])
            WT, U0, U, OS = W["WT"], W["U0"], W["U"], W["OS"]
            for h in range(4):
                c.pe(lambda e: e.matmul(b5[:, h * 128:(h + 1) * 128], KBN[:, h, :], R[:, h, :], start=True, stop=True), reads=[KBN, R], writes=[b5])
            for h in range(4):
                c.pe(lambda e: e.matmul(b6[:, h * 128:(h + 1) * 128], R[:, h, :], VB[:, h, :], start=True, stop=True), reads=[VB, R], writes=[b6])
            c.act(lambda e: e.copy(WT[:], v4(b5)), reads=[b5, WT], writes=[WT])
            c.dve(lambda e: e.tensor_copy(U0[:], v4(b6)), reads=[b6, U0], writes=[U0])
            for h in range(4):
                c.pe(lambda e: e.matmul(b7[:, h * 128:(h + 1) * 128], WT[:, h, :], H[:, h, :], start=True, stop=True), reads=[WT, H], writes=[b7])
            c.dve(lambda e: e.tensor_tensor(U[:], U0[:], v4(b7), ALU.add), reads=[U0, b7, U], writes=[U])
            for h in range(4):
                c.pe(lambda e: e.matmul(b5[:, h * 128:(h + 1) * 128], H[:, h, :], QD[:, h, :], start=True, stop=False), reads=[H, QD], writes=[b5])
                c.pe(lambda e: e.matmul(b5[:, h * 128:(h + 1) * 128], U[:, h, :], AT[:, h, :], start=False, stop=True), reads=[U, AT], writes=[b5])
            for h in range(4):
                c.pe(lambda e: e.matmul(b6[:, h * 128:(h + 1) * 128], KTL[:, h, :], U[:, h, :], start=True, stop=True), reads=[KTL, U], writes=[b6])
            c.act(lambda e: e.copy(OS[:], v4(b5)), reads=[b5, OS], writes=[OS])
            c.dma("pool", oT[:, :, tsl], OS[:], reads=[OS], acc=[oT])
            c.dve(lambda e: e.tensor_tensor(H[:], H[:], bc(GAM[:, :].unsqueeze(2), [128, 4, 128]), ALU.mult), reads=[H, GAM], writes=[H])
            c.dve(lambda e: e.tensor_tensor(H[:], H[:], v4(b6), ALU.add), reads=[H, b6], writes=[H])

    def alloc_chunk_ws(s):
        c = s.c
        W = {}
        for k in ("R", "NA", "NB", "PA", "PB", "RG", "RB", "Grow", "E1", "E2", "X1", "X2", "AT", "KBN", "VB", "KTL", "QD", "WT", "U0", "U", "OS", "H"):
            W[k] = c.sb([128, 4, 128], name="w_" + k)
        for k in ("Gc", "S4", "CO", "EGT", "GAM"):
            W[k] = c.sb([128, 4], name="w_" + k)
        W["ld"] = [{k: c.sb([128, 4, 128], name=f"ld{i}_{k}") for k in ("KT", "QT", "Ktm", "Vtm")} for i in range(2)]
        return W

    def rk_pass(s, d, rk_tm, yT, W, R_):
        c = s.c
        L, NT = s.L, s.NT
        K = s.K
        trim, suf, mpa, psm, ident, ones = K[f"trim{d}"], K[f"suf{d}"], K[f"mpa{d}"], K[f"ps{d}"], K["ident"], K["ones"]
        H, GAM = R_["H"], R_["GAM"]
        c.dve(lambda e: e.memset(H[:], 0.0), writes=[H])
        order = range(NT) if d == 0 else range(NT - 1, -1, -1)
        b0, b1, b2, b3, b4, b5, b6, b7 = s.P
        q4 = lambda ap, n: ap.rearrange("p (q c) -> p q c", q=n)
        for it, n in enumerate(order):
            ld = R_["ld"][it % 2]
            tsl = slice(n * 128, (n + 1) * 128)
            for i, key in [(0, "r"), (1, "kp"), (2, "v"), (3, "kk"), (4, "eta"), (5 + d, "lw")]:
                c.dma("sp", ld[key][:], rk_tm[i, tsl, :], reads=[rk_tm], writes=[ld[key]])
            r, kp, V, kk, eta, LW = (ld[k] for k in ("r", "kp", "v", "kk", "eta", "lw"))
            Ep, En, Ex, Et, Bv, rt, kt, bt, at, kh, bh = (R_[k] for k in ("Ep", "En", "Ex", "Et", "B", "rt", "kt", "bt", "at", "kh", "bh"))
            c.pe(lambda e: e.matmul(b0[:, :], trim[:], LW[:], start=True, stop=True), reads=[trim, LW], writes=[b0])
            c.pe(lambda e: e.matmul(b1[:, :], suf[:], LW[:], start=True, stop=True), reads=[suf, LW], writes=[b1])
            for h in range(8):
                c.pe(lambda e: e.matmul(b2[0:64, h:h + 1], LW[:, h * 64:(h + 1) * 64], ones[:, 0:1], start=True, stop=True), reads=[LW, ones], writes=[b2])
            mid = 63 if d == 0 else 64
            for h in range(8):
                c.pe(lambda e: e.matmul(b2[0:64, 8 + h:9 + h], LW[:, h * 64:(h + 1) * 64], K[f"tri{d}"][:, mid:mid + 1], start=True, stop=True), reads=[LW, K[f"tri{d}"]], writes=[b2])
            c.act(lambda e: e.activation(out=GAM[:], in_=b2[0:64, 0:16], func=AF.Exp), reads=[b2, GAM], writes=[GAM])
            c.act(lambda e: e.activation(out=Ep[:], in_=b0[:, :], func=AF.Exp), reads=[b0, Ep], writes=[Ep])
            c.act(lambda e: e.activation(out=En[:], in_=b0[:, :], func=AF.Exp, scale=-1.0), reads=[b0, En], writes=[En])
            c.dve(lambda e: e.tensor_tensor(Ex[:], b0[:, :], LW[:], ALU.subtract), reads=[b0, LW, Ex], writes=[Ex])
            c.act(lambda e: e.activation(out=Ex[:], in_=Ex[:], func=AF.Exp), reads=[Ex], writes=[Ex])
            c.act(lambda e: e.activation(out=Et[:], in_=b1[:, :], func=AF.Exp), reads=[b1, Et], writes=[Et])
            c.dve(lambda e: e.tensor_tensor(Bv[:], kk[:], eta[:], ALU.mult), reads=[kk, eta, Bv], writes=[Bv])
            c.dve(lambda e: e.tensor_tensor(rt[:], r[:], Ep[:], ALU.mult), reads=[r, Ep, rt], writes=[rt])
            c.dve(lambda e: e.tensor_tensor(kt[:], kp[:], En[:], ALU.mult), reads=[kp, En, kt], writes=[kt])
            c.dve(lambda e: e.tensor_tensor(bt[:], Bv[:], En[:], ALU.mult), reads=[Bv, En, bt], writes=[bt])
            c.dve(lambda e: e.scalar_tensor_tensor(at[:], kk[:], -1.0, Ex[:], ALU.mult, ALU.mult), reads=[kk, Ex, at], writes=[at])
            c.dve(lambda e: e.tensor_tensor(kh[:], kp[:], Et[:], ALU.mult), reads=[kp, Et, kh], writes=[kh])
            c.dve(lambda e: e.tensor_tensor(bh[:], Bv[:], Et[:], ALU.mult), reads=[Bv, Et, bh], writes=[bh])
            RAT, BT, KT2 = R_["RAT"], R_["BT"], R_["KT2"]
            tb = [b3, b4, b5, b6]
            ti = 0
            for src, dst, dbuf in [(at, lambda hs: RAT[:, hs, 0, :], RAT), (rt, lambda hs: RAT[:, hs, 1, :], RAT), (bt, lambda hs: BT[:, hs, :], BT), (kt, lambda hs: KT2[:, hs, :], KT2)]:
                for half in range(2):
                    bank = tb[ti % 4]
                    ti += 1
                    for q in range(4):
                        h = half * 4 + q
                        c.pe(lambda e: e.transpose(bank[0:64, q * 128:(q + 1) * 128], src[:, h * 64:(h + 1) * 64], ident[:]), reads=[src, ident], writes=[bank])
                    hs = slice(half * 4, half * 4 + 4)
                    if ti % 2 == 0:
                        c.act(lambda e: e.copy(dst(hs), q4(bank[0:64, :], 4)), reads=[bank, dbuf], writes=[dbuf])
                    else:
                        c.dve(lambda e: e.tensor_copy(dst(hs), q4(bank[0:64, :], 4)), reads=[bank, dbuf], writes=[dbuf])
            PAs, AKs, NNs, WT, AV, U0, U, YS = (R_[k] for k in ("PAs", "AKs", "NNs", "WT", "AV", "U0", "U", "YS"))
            for hg in range(2):
                hs = slice(hg * 4, hg * 4 + 4)
                for q in range(4):
                    h = hg * 4 + q
                    bpa = (b0, b1)[q // 2]
                    bak = (b2, b3)[q // 2]
                    osl = slice((q % 2) * 256, (q % 2 + 1) * 256)
                    rhs2 = RAT[:, h, :, :].rearrange("p a c -> p (a c)")
                    c.pe(lambda e: e.matmul(bpa[:, osl], BT[:, h, :], rhs2, start=True, stop=True), reads=[BT, RAT], writes=[bpa])
                    c.pe(lambda e: e.matmul(bak[:, osl], KT2[:, h, :], rhs2, start=True, stop=True), reads=[KT2, RAT], writes=[bak])
                    c.pe(lambda e: e.matmul(b4[:, q * 128:(q + 1) * 128], RAT[:, h, 0, :], BT[:, h, :], start=True, stop=True), reads=[BT, RAT], writes=[b4])
                for j in range(2):
                    h2 = slice(hg * 4 + 2 * j, hg * 4 + 2 * j + 2)
                    v22 = lambda bank: bank[:, :].rearrange("p (q a c) -> p q a c", q=2, a=2)
                    c.dve(lambda e: e.tensor_tensor(PAs[:, h2], v22((b0, b1)[j]), mpa[:, 0:2], ALU.mult), reads=[(b0, b1)[j], mpa, PAs], writes=[PAs])
                    c.dve(lambda e: e.tensor_tensor(AKs[:, h2], v22((b2, b3)[j]), mpa[:, 0:2], ALU.mult), reads=[(b2, b3)[j], mpa, AKs], writes=[AKs])
                c.dve(lambda e: e.tensor_tensor(NNs[:, hs], q4(b4[:, :], 4), psm[:], ALU.mult), reads=[b4, psm, NNs], writes=[NNs])
                R = s.tri_inverse(NNs[:, hs], PAs[:, hs, 0, :], [NNs, PAs], W)
                for q in range(4):
                    h = hg * 4 + q
                    c.pe(lambda e: e.matmul(b0[0:64, q * 128:(q + 1) * 128], at[:, h * 64:(h + 1) * 64], R[:, q, :], start=True, stop=True), reads=[at, R], writes=[b0])
                    c.pe(lambda e: e.matmul(b1[:, q * 64:(q + 1) * 64], AKs[:, h, 0, :], V[:, h * 64:(h + 1) * 64], start=True, stop=True), reads=[AKs, V], writes=[b1])
                c.act(lambda e: e.copy(WT[:, hs, :], q4(b0[0:64, :], 4)), reads=[b0, WT], writes=[WT])
                c.dve(lambda e: e.tensor_copy(AV[:, hs, :], q4(b1[:, 0:256], 4)), reads=[b1, AV], writes=[AV])
                for q in range(4):
                    h = hg * 4 + q
                    c.pe(lambda e: e.matmul(b2[:, q * 64:(q + 1) * 64], R[:, q, :], AV[:, h, :], start=True, stop=True), reads=[AV, R], writes=[b2])
                c.act(lambda e: e.copy(U0[:, hs, :], q4(b2[:, 0:256], 4)), reads=[b2, U0], writes=[U0])
                Hs = R_["Hs"]
                c.dve(lambda e: e.tensor_tensor(Hs[:, hs, :], H[:, hs, :], bc(GAM[:, 8 + hg * 4:12 + hg * 4].unsqueeze(2), [64, 4, 64]), ALU.mult), reads=[H, GAM, Hs], writes=[Hs])
                for q in range(4):
                    h = hg * 4 + q
                    c.pe(lambda e: e.matmul(b3[:, q * 64:(q + 1) * 64], WT[:, h, :], Hs[:, h, :], start=True, stop=True), reads=[WT, Hs], writes=[b3])
                c.dve(lambda e: e.tensor_tensor(U[:, hs, :], U0[:, hs, :], q4(b3[:, 0:256], 4), ALU.add), reads=[U0, b3, U], writes=[U])
                for q in range(4):
                    h = hg * 4 + q
                    o = b4[0:64, q * 128:(q + 1) * 128]
                    c.pe(lambda e: e.matmul(o, Hs[:, h, :], RAT[:, h, 1, :], start=True, stop=False), reads=[Hs, RAT], writes=[b4])
                    c.pe(lambda e: e.matmul(o, U[:, h, :], PAs[:, h, 1, :], start=False, stop=False), reads=[U, PAs], writes=[b4])
                    c.pe(lambda e: e.matmul(o, V[:, h * 64:(h + 1) * 64], AKs[:, h, 1, :], start=False, stop=True), reads=[V, AKs], writes=[b4])
                for q in range(4):
                    h = hg * 4 + q
                    o = b5[0:64, q * 64:(q + 1) * 64]
                    c.pe(lambda e: e.matmul(o, bh[:, h * 64:(h + 1) * 64], U[:, h, :], start=True, stop=False), reads=[bh, U], writes=[b5])
                    c.pe(lambda e: e.matmul(o, kh[:, h * 64:(h + 1) * 64], V[:, h * 64:(h + 1) * 64], start=False, stop=True), reads=[kh, V], writes=[b5])
                c.act(lambda e: e.copy(YS[:, hs, :], q4(b4[0:64, :], 4)), reads=[b4, YS], writes=[YS])
                c.dve(lambda e: e.tensor_tensor(H[:, hs, :], H[:, hs, :], bc(GAM[:, hs].unsqueeze(2), [64, 4, 64]), ALU.mult), reads=[H, GAM], writes=[H])
                c.dve(lambda e: e.tensor_tensor(H[:, hs, :], H[:, hs, :], q4(b5[0:64, 0:256], 4), ALU.add), reads=[H, b5], writes=[H])
            c.dma("pool", yT[:, :, tsl], YS[:], reads=[YS], acc=[yT])

    def alloc_rk_ws(s):
        c = s.c
        R_ = {}
        for k in ("Ep", "En", "Ex", "Et", "B", "rt", "kt", "bt", "at", "kh", "bh"):
            R_[k] = c.sb([128, 512], name="r_" + k)
        R_["ld"] = [{k: c.sb([128, 512], name=f"rld{i}_{k}") for k in ("r", "kp", "v", "kk", "eta", "lw")} for i in range(2)]
        R_["RAT"] = c.sb([64, 8, 2, 128], name="r_RAT")
        R_["BT"] = c.sb([64, 8, 128], name="r_BT")
        R_["KT2"] = c.sb([64, 8, 128], name="r_KT2")
        R_["PAs"] = c.sb([128, 8, 2, 128], name="r_PAs")
        R_["AKs"] = c.sb([128, 8, 2, 128], name="r_AKs")
        R_["NNs"] = c.sb([128, 8, 128], name="r_NNs")
        R_["WT"] = c.sb([64, 8, 128], name="r_WT")
        R_["YS"] = c.sb([64, 8, 128], name="r_YS")
        for k in ("AV", "U0", "U"):
            R_[k] = c.sb([128, 8, 64], name="r_" + k)
        R_["H"] = c.sb([64, 8, 64], name="r_H")
        R_["GAM"] = c.sb([64, 16], name="r_GAM")
        R_["Hs"] = c.sb([64, 8, 64], name="r_Hs")
        return R_

    def load_w_bf16(s, src_ap, srcbuf, P, nk, name):
        c = s.c
        Wt = c.sb([P, nk, 1024], BF16, name=name)
        if not hasattr(s, "_wst") or s._wst is None:
            s._wst = c.sb([128, 1024], F32, name=name + "_st")
        st = s._wst
        v = src_ap.rearrange("(k p) n -> p k n", p=P)
        for k in range(nk):
            c.dma("sp", st[:P], v[:, k, :], reads=[srcbuf], writes=[st])
            c.dve(lambda e: e.tensor_copy(Wt[:, k, :], st[:P]), reads=[st, Wt], writes=[Wt])
        return Wt

    def post(s, layer, prep, wparts, resid, pT, out_dst, hT_dst):
        c = s.c
        L = s.L
        TB = s.TBp
        NB = L // TB
        NTB = TB // 128
        I = s.inp
        ident = s.K["ident"]
        s._wst = None
        w_out = I(f"w_out{layer}", [1024, 1024]); ple_w = I(f"ple_w{layer}", [256, 1024]); ple_g = I(f"ple_gate{layer}", [1024, 1024])
        vec = I(f"vecs{layer}", [3, 1024])
        LG = s.load_bc(vec.t[0:1, :], 1024, vec, f"LG{layer}")
        LB = s.load_bc(vec.t[1:2, :], 1024, vec, f"LB{layer}")
        PN = s.load_bc(vec.t[2:3, :], 1024, vec, f"PN{layer}")
        Wp = []
        for (r0, r1, P) in wparts:
            Wp.append(s.load_w_bf16(w_out.t[r0:r1, :], w_out, P, (r1 - r0) // P, f"wout{layer}_{r0}"))
        PW = s.load_w_bf16(ple_w.t, ple_w, 128, 2, f"plew{layer}")
        PG = s.load_w_bf16(ple_g.t, ple_g, 128, 8, f"pleg{layer}")
        Pst = c.sb([128, 2, TB], F32, name="post_pst"); Pb = c.sb([128, 2, TB], BF16, name="post_pb")
        XT = c.sb([128, 1024], name="post_xt"); PRE = c.sb([128, 1024], name="post_pre"); JK = c.sb([128, 1024], name="post_jk")
        HN = c.sb([128, 1024], name="post_hn"); E = c.sb([128, 1024], name="post_e"); SG = c.sb([128, 1024], name="post_sg")
        HNT = c.sb([128, 8, 128], BF16, name="post_hnt"); OT = c.sb([128, 8, 128], F32, name="post_ot")
        ST = c.sb([128, 8], name="post_st")
        pv = pT.t[layer]
        pvv = pv.rearrange("(k p) l -> p k l", p=128)
        b0, b1, b2, b3, b4, b5, b6, b7 = s.P
        for b in range(NB):
            t0 = b * TB
            entries = prep(b)
            c.dma("sp", Pst[:], pvv[:, :, t0:t0 + TB], reads=[pT], writes=[Pst])
            c.act(lambda e: e.copy(Pb[:], Pst[:]), reads=[Pst, Pb], writes=[Pb])
            for tt in range(NTB):
                tok = slice(t0 + tt * 128, t0 + (tt + 1) * 128)
                tl = slice(tt * 128, (tt + 1) * 128)
                c.dma("sp", XT[:], resid[tok, :], reads=[resid], writes=[XT])
                for half in range(2):
                    bank = (b0, b1)[half]
                    ne = len(entries)
                    for ei, (mt, idx, P, wi, wk) in enumerate(entries):
                        c.pe(lambda e: e.matmul(bank[:, :], mt[0:P, idx, tl], Wp[wi][0:P, wk, half * 512:(half + 1) * 512], start=(ei == 0), stop=(ei == ne - 1)),
                             reads=[mt, Wp[wi]], writes=[bank])
                    c.dve(lambda e: e.scalar_tensor_tensor(PRE[:, half * 512:(half + 1) * 512], XT[:, half * 512:(half + 1) * 512], ALPHA, bank[:, :], ALU.mult, ALU.add),
                          reads=[XT, bank, PRE], writes=[PRE])
                c.dve(lambda e: e.reduce_sum(ST[:, 0:1], PRE[:], axis=AX.X), reads=[PRE, ST], writes=[ST])
                c.dve(lambda e: e.tensor_scalar(ST[:, 1:2], ST[:, 0:1], -1.0 / 1024, None, ALU.mult), reads=[ST], writes=[ST])
                c.dve(lambda e: e.tensor_scalar(PRE[:], PRE[:], ST[:, 1:2], None, ALU.add), reads=[PRE, ST], writes=[PRE])
                c.act(lambda e: e.activation(out=JK[:], in_=PRE[:], func=AF.Square, accum_out=ST[:, 2:3]), reads=[PRE, JK, ST], writes=[JK, ST])
                c.dve(lambda e: e.tensor_scalar(ST[:, 3:4], ST[:, 2:3], 1.0 / 1024, LN_EPS, ALU.mult, ALU.add), reads=[ST], writes=[ST])
                c.act(lambda e: e.activation(out=ST[:, 3:4], in_=ST[:, 3:4], func=AF.Sqrt), reads=[ST], writes=[ST])
                c.dve(lambda e: e.reciprocal(ST[:, 3:4], ST[:, 3:4]), reads=[ST], writes=[ST])
                c.dve(lambda e: e.scalar_tensor_tensor(HN[:], PRE[:], ST[:, 3:4], LG[:], ALU.mult, ALU.mult), reads=[PRE, ST, LG, HN], writes=[HN])
                c.dve(lambda e: e.tensor_tensor(HN[:], HN[:], LB[:], ALU.add), reads=[HN, LB], writes=[HN])
                for half in range(2):
                    bank = (b2, b3)[half]
                    for kk in range(2):
                        c.pe(lambda e: e.matmul(bank[:, :], Pb[:, kk, tl], PW[:, kk, half * 512:(half + 1) * 512], start=(kk == 0), stop=(kk == 1)),
                             reads=[Pb, PW], writes=[bank])
                    c.act(lambda e: e.activation(out=JK[:, half * 512:(half + 1) * 512], in_=bank[:, :], func=AF.Square, accum_out=ST[:, 4 + half:5 + half]),
                          reads=[bank, JK, ST], writes=[JK, ST])
                c.dve(lambda e: e.tensor_tensor(ST[:, 6:7], ST[:, 4:5], ST[:, 5:6], ALU.add), reads=[ST], writes=[ST])
                c.dve(lambda e: e.tensor_scalar(ST[:, 6:7], ST[:, 6:7], 1.0 / 1024, RMS_EPS, ALU.mult, ALU.add), reads=[ST], writes=[ST])
                c.act(lambda e: e.activation(out=ST[:, 6:7], in_=ST[:, 6:7], func=AF.Sqrt), reads=[ST], writes=[ST])
                c.dve(lambda e: e.reciprocal(ST[:, 6:7], ST[:, 6:7]), reads=[ST], writes=[ST])
                for half in range(2):
                    bank = (b2, b3)[half]
                    hsl = slice(half * 512, (half + 1) * 512)
                    c.dve(lambda e: e.scalar_tensor_tensor(E[:, hsl], bank[:, :], ST[:, 6:7], PN[:, hsl], ALU.mult, ALU.mult), reads=[bank, ST, PN, E], writes=[E])
                for k in range(8):
                    bank = (b4, b5)[k // 4]
                    c.pe(lambda e: e.transpose(bank[:, (k % 4) * 128:(k % 4 + 1) * 128], HN[:, k * 128:(k + 1) * 128], ident[:]), reads=[HN, ident], writes=[bank])
                c.act(lambda e: e.copy(HNT[:, 0:4, :], b4[:, :].rearrange("p (k c) -> p k c", k=4)), reads=[b4, HNT], writes=[HNT])
                c.dve(lambda e: e.tensor_copy(HNT[:, 4:8, :], b5[:, :].rearrange("p (k c) -> p k c", k=4)), reads=[b5, HNT], writes=[HNT])
                for half in range(2):
                    bank = (b6, b7)[half]
                    for k in range(8):
                        c.pe(lambda e: e.matmul(bank[:, :], HNT[:, k, :], PG[:, k, half * 512:(half + 1) * 512], start=(k == 0), stop=(k == 7)), reads=[HNT, PG], writes=[bank])
                    hsl = slice(half * 512, (half + 1) * 512)
                    c.act(lambda e: e.activation(out=SG[:, hsl], in_=bank[:, :], func=AF.Sigmoid), reads=[bank, SG], writes=[SG])
                c.dve(lambda e: e.tensor_tensor(SG[:], SG[:], E[:], ALU.mult), reads=[SG, E], writes=[SG])
                c.dve(lambda e: e.tensor_tensor(SG[:], SG[:], HN[:], ALU.add), reads=[SG, HN], writes=[SG])
                c.dma("pool", out_dst[tok, :], SG[:], reads=[SG], acc=[out_dst])
                if hT_dst is not None:
                    for k in range(8):
                        bank = (b4, b5)[k // 4]
                        c.pe(lambda e: e.transpose(bank[:, (k % 4) * 128:(k % 4 + 1) * 128], SG[:, k * 128:(k + 1) * 128], ident[:]), reads=[SG, ident], writes=[bank])
                    c.act(lambda e: e.copy(OT[:, 0:4, :], b4[:, :].rearrange("p (k c) -> p k c", k=4)), reads=[b4, OT], writes=[OT])
                    c.dve(lambda e: e.tensor_copy(OT[:, 4:8, :], b5[:, :].rearrange("p (k c) -> p k c", k=4)), reads=[b5, OT], writes=[OT])
                    c.dma("pool", hT_dst.t.rearrange("(k p) l -> p k l", p=128)[:, :, tok], OT[:], reads=[OT], acc=[hT_dst])

    def l0_epilogue_setup(s, projT, oT, yT, bonusT):
        c = s.c
        TB = s.TBp
        I = s.inp
        E_ = {}
        E_["dnn"] = s.load_const("dn_norm_col", [128, 1])
        E_["lnc"] = s.load_const("rk_ln64", [64, 2, 8])
        for k in ("OF", "OB", "G"):
            E_[k] = c.sb([128, 4, TB], name="ep_" + k)
        for k in ("YF", "YB", "RG", "BO", "SQ"):
            E_[k] = c.sb([64, 8, TB], name="ep_" + k)
        E_["MDN"] = c.sb([128, 4, TB], BF16, name="ep_MDN")
        E_["MRK"] = c.sb([64, 8, TB], BF16, name="ep_MRK")
        ones = s.K["ones"]
        gdn = projT.t[1552:2064, :].rearrange("(h p) l -> p h l", p=128)
        grk = projT.t[3792:4304, :].rearrange("(h p) l -> p h l", p=64)
        bov = bonusT.t.rearrange("(e p) c l -> p c e l", e=2)

        def prep(b):
            t0 = b * TB
            ts_ = slice(t0, t0 + TB)
            OF, OB, G, YF, YB, RG, BO, SQ, MDN, MRK = (E_[k] for k in ("OF", "OB", "G", "YF", "YB", "RG", "BO", "SQ", "MDN", "MRK"))
            c.dma("sp", OF[:], oT[0][:, :, ts_], reads=[oT[0]], writes=[OF])
            c.dma("sp", OB[:], oT[1][:, :, ts_], reads=[oT[1]], writes=[OB])
            c.dma("sp", G[:], gdn[:, :, ts_], reads=[projT], writes=[G])
            c.dve(lambda e: e.tensor_tensor(OF[:], OF[:], OB[:], ALU.add), reads=[OF, OB], writes=[OF])
            c.act(lambda e: e.activation(out=OB[:], in_=OF[:], func=AF.Square), reads=[OF, OB], writes=[OB])
            for h in range(4):
                c.pe(lambda e: e.matmul(s.P[h][:, 0:TB], ones[:], OB[:, h, :], start=True, stop=True), reads=[OB, ones], writes=[s.P[h]])
            for h in range(4):
                c.dve(lambda e: e.tensor_scalar(OB[:, h, :], s.P[h][:, 0:TB], 1.0 / 128, RMS_EPS, ALU.mult, ALU.add), reads=[s.P[h], OB], writes=[OB])
            c.act(lambda e: e.activation(out=OB[:], in_=OB[:], func=AF.Sqrt), reads=[OB], writes=[OB])
            c.dve(lambda e: e.reciprocal(OB[:], OB[:]), reads=[OB], writes=[OB])
            c.dve(lambda e: e.scalar_tensor_tensor(OF[:], OF[:], E_["dnn"][:, 0:1], OB[:], ALU.mult, ALU.mult), reads=[OF, OB, E_["dnn"]], writes=[OF])
            c.act(lambda e: e.activation(out=G[:], in_=G[:], func=AF.Silu), reads=[G], writes=[G])
            c.dve(lambda e: e.tensor_tensor(MDN[:], OF[:], G[:], ALU.mult), reads=[OF, G, MDN], writes=[MDN])
            c.dma("sp", YF[:], yT[0][:, :, ts_], reads=[yT[0]], writes=[YF])
            c.dma("sp", YB[:], yT[1][:, :, ts_], reads=[yT[1]], writes=[YB])
            c.dma("sp", RG[:], grk[:, :, ts_], reads=[projT], writes=[RG])
            for cc in range(4):
                c.dma("sp", BO[:, 2 * cc:2 * cc + 2, :], bov[:, cc, :, ts_], reads=[bonusT], writes=[BO])
            c.dve(lambda e: e.tensor_tensor(YF[:], YF[:], YB[:], ALU.add), reads=[YF, YB], writes=[YF])
            for hg in range(2):
                for q in range(4):
                    h = hg * 4 + q
                    c.pe(lambda e: e.matmul(s.P[4 + q][0:64, 0:TB], ones[0:64, 0:64], YF[:, h, :], start=True, stop=True), reads=[YF, ones], writes=[s.P[4 + q]])
                for q in range(4):
                    h = hg * 4 + q
                    c.dve(lambda e: e.scalar_tensor_tensor(YF[:, h, :], s.P[4 + q][0:64, 0:TB], -1.0 / 64, YF[:, h, :], ALU.mult, ALU.add), reads=[s.P[4 + q], YF], writes=[YF])
            c.act(lambda e: e.activation(out=SQ[:], in_=YF[:], func=AF.Square), reads=[YF, SQ], writes=[SQ])
            for hg in range(2):
                for q in range(4):
                    h = hg * 4 + q
                    c.pe(lambda e: e.matmul(s.P[4 + q][0:64, 0:TB], ones[0:64, 0:64], SQ[:, h, :], start=True, stop=True), reads=[SQ, ones], writes=[s.P[4 + q]])
                for q in range(4):
                    h = hg * 4 + q
                    c.dve(lambda e: e.tensor_scalar(YB[:, h, :], s.P[4 + q][0:64, 0:TB], 1.0 / 64, GN_EPS, ALU.mult, ALU.add), reads=[s.P[4 + q], YB], writes=[YB])
            c.act(lambda e: e.activation(out=YB[:], in_=YB[:], func=AF.Sqrt), reads=[YB], writes=[YB])
            c.dve(lambda e: e.reciprocal(YB[:], YB[:]), reads=[YB], writes=[YB])
            c.dve(lambda e: e.tensor_tensor(YF[:], YF[:], YB[:], ALU.mult), reads=[YF, YB], writes=[YF])
            lnc = E_["lnc"]
            c.dve(lambda e: e.tensor_tensor(YF[:], YF[:], bc(lnc[:, 0, :].unsqueeze(2), [64, 8, TB]), ALU.mult), reads=[YF, lnc], writes=[YF])
            c.dve(lambda e: e.tensor_tensor(YF[:], YF[:], bc(lnc[:, 1, :].unsqueeze(2), [64, 8, TB]), ALU.add), reads=[YF, lnc], writes=[YF])
            c.dve(lambda e: e.tensor_tensor(YF[:], YF[:], BO[:], ALU.add), reads=[YF, BO], writes=[YF])
            c.act(lambda e: e.activation(out=RG[:], in_=RG[:], func=AF.Silu), reads=[RG], writes=[RG])
            c.dve(lambda e: e.tensor_tensor(MRK[:], YF[:], RG[:], ALU.mult), reads=[YF, RG, MRK], writes=[MRK])
            return [(MDN, k, 128, 0, k) for k in range(4)] + [(MRK, h, 64, 1, h) for h in range(8)]
        return prep

    def fft_fwd(s, U, ubufs, Kp, G, F):
        c = s.c
        N1 = s.N1
        cpb = min(G, 512 // (2 * N1))
        nb = G // cpb
        A_b = [s.P[i] for i in range(nb)]
        X_b = [s.P[2 + i] for i in range(nb)]
        w = 2 * N1
        for g in range(G):
            bank = A_b[g // cpb]
            c.pe(lambda e: e.matmul(bank[:, (g % cpb) * w:(g % cpb + 1) * w], U[:, g, :], F["F1"][0:Kp, :], start=True, stop=True), reads=ubufs + [F["F1"]], writes=[bank])
        T1, T2, RH1, RH2 = F["T1"], F["T2"], F["RH1"], F["RH2"]
        tw = F["TW"]
        for bi in range(nb):
            gs = slice(bi * cpb, (bi + 1) * cpb)
            Av = A_b[bi][:, 0:cpb * w].rearrange("p (g a k) -> p g a k", g=cpb, a=2)
            c.dve(lambda e: e.tensor_tensor(T1[:, gs], Av, bc(tw[:, 0:1, :].unsqueeze(1), [128, cpb, 2, N1]), ALU.mult), reads=[A_b[bi], tw, T1], writes=[T1])
            c.dve(lambda e: e.tensor_tensor(T2[:, gs], Av, bc(tw[:, 1:2, :].unsqueeze(1), [128, cpb, 2, N1]), ALU.mult), reads=[A_b[bi], tw, T2], writes=[T2])
        c.dve(lambda e: e.tensor_tensor(RH1[:, :, 0, :], T1[:, :, 0, :], T2[:, :, 1, :], ALU.subtract), reads=[T1, T2, RH1], writes=[RH1])
        c.dve(lambda e: e.tensor_tensor(RH1[:, :, 1, :], T1[:, :, 1, :], T2[:, :, 0, :], ALU.add), reads=[T1, T2, RH1], writes=[RH1])
        c.act(lambda e: e.mul(RH2[:, :, 0, :], RH1[:, :, 1, :], -1.0), reads=[RH1, RH2], writes=[RH2])
        c.act(lambda e: e.copy(RH2[:, :, 1, :], RH1[:, :, 0, :]), reads=[RH1, RH2], writes=[RH2])
        for bi in range(nb):
            gs = slice(bi * cpb, (bi + 1) * cpb)
            c.pe(lambda e: e.matmul(X_b[bi][:, 0:cpb * w], F["F2re"][:], RH1[:, gs].rearrange("p g a k -> p (g a k)"), start=True, stop=False), reads=[RH1, F["F2re"]], writes=[X_b[bi]])
            c.pe(lambda e: e.matmul(X_b[bi][:, 0:cpb * w], F["F2im"][:], RH2[:, gs].rearrange("p g a k -> p (g a k)"), start=False, stop=True), reads=[RH2, F["F2im"]], writes=[X_b[bi]])
        return X_b, cpb, nb

    def fft_conv(s, U, ubufs, Hs, G, F, out_bank):
        c = s.c
        N1 = s.N1
        Kp = N1 // 2
        w = 2 * N1
        X_b, cpb, nb = s.fft_fwd(U, ubufs, Kp, G, F)
        T1, T2, Y = F["T1"], F["T2"], F["Y"]
        for bi in range(nb):
            gs = slice(bi * cpb, (bi + 1) * cpb)
            Xv = X_b[bi][:, 0:cpb * w].rearrange("p (g a k) -> p g a k", g=cpb, a=2)
            c.dve(lambda e: e.tensor_tensor(T1[:, gs], Xv, bc(Hs[:, gs, 0:1, :], [128, cpb, 2, N1]), ALU.mult), reads=[X_b[bi], Hs, T1], writes=[T1])
            c.dve(lambda e: e.tensor_tensor(T2[:, gs], Xv, bc(Hs[:, gs, 1:2, :], [128, cpb, 2, N1]), ALU.mult), reads=[X_b[bi], Hs, T2], writes=[T2])
        c.dve(lambda e: e.tensor_tensor(Y[:, :, 0, :], T1[:, :, 0, :], T2[:, :, 1, :], ALU.subtract), reads=[T1, T2, Y], writes=[Y])
        c.dve(lambda e: e.tensor_tensor(Y[:, :, 1, :], T1[:, :, 1, :], T2[:, :, 0, :], ALU.add), reads=[T1, T2, Y], writes=[Y])
        B_b = [s.P[4 + i] for i in range(G // 2)]
        for g in range(G):
            bank = B_b[g // 2]
            o = bank[0:N1, (g % 2) * 256:(g % 2 + 1) * 256]
            c.pe(lambda e: e.matmul(o, Y[:, g, 0, :], F["CF1"][:], start=True, stop=False), reads=[Y, F["CF1"]], writes=[bank])
            c.pe(lambda e: e.matmul(o, Y[:, g, 1, :], F["CF2"][:], start=False, stop=True), reads=[Y, F["CF2"]], writes=[bank])
        S1, S2, BR, BI = F["S1"], F["S2"], F["BR"], F["BI"]
        twt = F["TWT"]
        for bi in range(G // 2):
            gs = slice(bi * 2, bi * 2 + 2)
            Bv = B_b[bi][0:N1, :].rearrange("p (g a n) -> p g a n", g=2, a=2)
            c.dve(lambda e: e.tensor_tensor(S1[:, gs], Bv, bc(twt[:, 0:1, :].unsqueeze(1), [N1, 2, 2, 128]), ALU.mult), reads=[B_b[bi], twt, S1], writes=[S1])
            c.dve(lambda e: e.tensor_tensor(S2[:, gs], Bv, bc(twt[:, 1:2, :].unsqueeze(1), [N1, 2, 2, 128]), ALU.mult), reads=[B_b[bi], twt, S2], writes=[S2])
        c.dve(lambda e: e.tensor_tensor(BR[:], S1[:, :, 0, :], S2[:, :, 1, :], ALU.add), reads=[S1, S2, BR], writes=[BR])
        c.dve(lambda e: e.tensor_tensor(BI[:], S1[:, :, 1, :], S2[:, :, 0, :], ALU.subtract), reads=[S1, S2, BI], writes=[BI])
        o = out_bank[0:Kp, 0:G * 128]
        c.pe(lambda e: e.matmul(o, F["IF1re"][:], BR[:].rearrange("p g n -> p (g n)"), start=True, stop=False), reads=[BR, F["IF1re"]], writes=[out_bank])
        c.pe(lambda e: e.matmul(o, F["IF1imn"][:], BI[:].rearrange("p g n -> p (g n)"), start=False, stop=True), reads=[BI, F["IF1imn"]], writes=[out_bank])

    def layer1(s, hT, hres, pT, stop):
        c = s.c
        L, TB, NB = s.L, s.TB, s.NB
        I = s.inp
        N1 = 2 * L // 128
        s.N1 = N1
        Kp = N1 // 2
        G = 4
        w_in1 = I("w_in1", [1024, 4096])
        projT = s.scratch("projT1", [4096, L])
        s.proj_fm(hT, w_in1, 4096, projT)
        xvT = s.scratch("hy_xvT", [3072, L])
        c.push()
        TB = min(256, L)
        NB = L // TB
        CW = s.load_const("hy_cw", [128, 24, 4])
        pv = projT.t[0:3072, :].rearrange("(t p) l -> p t l", p=128)
        xv = xvT.t.rearrange("(t p) l -> p t l", p=128)
        X = c.sb([128, 24, TB + 2], name="hyX"); A = c.sb([128, 24, TB], name="hyA"); A2 = c.sb([128, 24, TB], name="hyA2")
        for b in range(NB):
            t0 = b * TB
            lo = max(t0 - 1, 0); hi = min(t0 + TB + 1, L)
            if lo != t0 - 1 or hi != t0 + TB + 1:
                c.dve(lambda e: e.memset(X[:], 0.0), writes=[X])
            c.dma("sp", X[:, :, lo - (t0 - 1):hi - (t0 - 1)], pv[:, :, lo:hi], reads=[projT], writes=[X])
            c.dve(lambda e: e.tensor_tensor(A[:], X[:, :, 0:TB], bc(CW[:, :, 0:1], [128, 24, TB]), ALU.mult), reads=[X, CW, A], writes=[A])
            c.dve(lambda e: e.tensor_tensor(A2[:], X[:, :, 1:TB + 1], bc(CW[:, :, 1:2], [128, 24, TB]), ALU.mult), reads=[X, CW, A2], writes=[A2])
            c.dve(lambda e: e.tensor_tensor(A[:], A[:], A2[:], ALU.add), reads=[A, A2], writes=[A])
            c.dve(lambda e: e.tensor_tensor(A2[:], X[:, :, 2:TB + 2], bc(CW[:, :, 2:3], [128, 24, TB]), ALU.mult), reads=[X, CW, A2], writes=[A2])
            c.dve(lambda e: e.tensor_tensor(A[:], A[:], A2[:], ALU.add), reads=[A, A2], writes=[A])
            c.dve(lambda e: e.tensor_tensor(A[:], A[:], bc(CW[:, :, 3:4], [128, 24, TB]), ALU.add), reads=[A, CW], writes=[A])
            c.dma("pool", xv[:, :, t0:t0 + TB], A[:], reads=[A], acc=[xvT])
        c.pop()
        filt = s.scratch("hy_filt", [2, 1024, 2 * L])
        c.push()
        CH = min(512, L)
        NCH = L // CH
        FTd = I("hy_FT", [33, 2, L])
        W1 = s.load_const("hy_w1", [33, 64]); W2 = s.load_const("hy_w2", [64, 64]); W3 = s.load_const("hy_w3", [64, 64])
        BF = s.load_const("hy_bf", [64, 4])
        WO = s.load_const("hy_wo", [64, 4096])
        SK = s.load_const("hy_skipc", [128, 2, 8])
        ndd = I("hy_deltas", [1, 4096])
        ND = c.sb([65, 4096], name="hyND")
        c.dve(lambda e: e.memset(ND[:], 0.0), writes=[ND])
        c.dma("sp", ND[64:65, :], ndd.t, reads=[ndd], writes=[ND])
        c.act(lambda e: e.activation(out=ND[64:65, :], in_=ND[64:65, :], func=AF.Abs), reads=[ND], writes=[ND])
        c.act(lambda e: e.mul(ND[64:65, :], ND[64:65, :], -1.0), reads=[ND], writes=[ND])
        FB = c.sb([64, 3], name="hyFB")
        c.dve(lambda e: e.tensor_tensor(FB[:], BF[:, 0:3], bc(BF[:, 3:4], [64, 3]), ALU.mult), reads=[BF, FB], writes=[FB])
        HD = c.sb([65, 2, L], name="hyHD")
        FTc = c.sb([33, CH], name="hyFT"); Z = c.sb([64, CH], name="hyZ"); Mk = c.sb([64, CH], name="hyM")
        TWO_PI = 2 * math.pi
        for dirn in range(2):
            c.dma("sp", HD[64:65, dirn, :], FTd.t[0:1, dirn, :], reads=[FTd], writes=[HD])
            for ch in range(NCH):
                cs = slice(ch * CH, (ch + 1) * CH)
                c.dma("sp", FTc[:], FTd.t[:, dirn, cs], reads=[FTd], writes=[FTc])
                src, sbuf, Wl = FTc[:], FTc, [W1, W2, W3]
                for li in range(3):
                    bank = s.P[li]
                    kk = 33 if li == 0 else 64
                    c.pe(lambda e: e.matmul(bank[0:64, 0:CH], Wl[li][0:kk, :], src, start=True, stop=True), reads=[sbuf, Wl[li]], writes=[bank])
                    c.dve(lambda e: e.tensor_scalar(Z[:], bank[0:64, 0:CH], BF[:, 3:4], FB[:, li:li + 1], ALU.mult, ALU.add), reads=[bank, BF, FB, Z], writes=[Z])
                    for _ in range(2):
                        c.dve(lambda e: e.tensor_scalar(Mk[:], Z[:], math.pi, -TWO_PI, ALU.is_gt, ALU.mult), reads=[Z, Mk], writes=[Mk])
                        c.dve(lambda e: e.tensor_tensor(Z[:], Z[:], Mk[:], ALU.add), reads=[Z, Mk], writes=[Z])
                        c.dve(lambda e: e.tensor_scalar(Mk[:], Z[:], -math.pi, TWO_PI, ALU.is_lt, ALU.mult), reads=[Z, Mk], writes=[Mk])
                        c.dve(lambda e: e.tensor_tensor(Z[:], Z[:], Mk[:], ALU.add), reads=[Z, Mk], writes=[Z])
                    dst = Z[:] if li < 2 else HD[0:64, dirn, cs]
                    dbuf = Z if li < 2 else HD
                    c.act(lambda e: e.activation(out=dst, in_=Z[:], func=AF.Sin), reads=[Z, dbuf], writes=[dbuf])
                    src, sbuf = Z[:], Z
        ACC = c.sb([128, 32, NCH], name="hyACC"); NRM = c.sb([128, 16], name="hyNRM")
        EW = c.sb([128, CH], name="hyEW"); FV = c.sb([128, CH], name="hyFV"); JK = c.sb([128, CH], name="hyJK")
        c.dve(lambda e: e.memset(ACC[:], 0.0), writes=[ACC])
        for pas in range(2):
            for o in range(2):
                for dirn in range(2):
                    for ct in range(8):
                        col = o * 16 + dirn * 8 + ct
                        csl = slice(col * 128, (col + 1) * 128)
                        for ch in range(NCH):
                            cs = slice(ch * CH, (ch + 1) * CH)
                            bk1, bk2 = s.P[4 + (ch % 2) * 2], s.P[5 + (ch % 2) * 2]
                            c.pe(lambda e: e.matmul(bk1[:, 0:CH], WO[:, csl], HD[0:64, dirn, cs], start=True, stop=True), reads=[WO, HD], writes=[bk1])
                            c.pe(lambda e: e.matmul(bk2[:, 0:CH], ND[:, csl], HD[0:65, dirn, cs], start=True, stop=True), reads=[ND, HD], writes=[bk2])
                            c.act(lambda e: e.activation(out=EW[:], in_=bk2[:, 0:CH], func=AF.Exp), reads=[bk2, EW], writes=[EW])
                            c.dve(lambda e: e.tensor_tensor(FV[:], bk1[:, 0:CH], EW[:], ALU.mult), reads=[bk1, EW, FV], writes=[FV])
                            if dirn == 1 and ch == 0:
                                c.dve(lambda e: e.memset(FV[:, 0:1], 0.0), reads=[FV], writes=[FV])
                            if pas == 0:
                                c.act(lambda e: e.activation(out=JK[:], in_=FV[:], func=AF.Abs, accum_out=ACC[:, col, ch:ch + 1]), reads=[FV, JK, ACC], writes=[JK, ACC])
                            else:
                                c.dve(lambda e: e.tensor_scalar(FV[:], FV[:], NRM[:, o * 8 + ct:o * 8 + ct + 1], None, ALU.mult), reads=[FV, NRM], writes=[FV])
                                if dirn == 0 and ch == 0:
                                    c.dve(lambda e: e.tensor_tensor(FV[:, 0:1], FV[:, 0:1], SK[:, o, ct:ct + 1], ALU.add), reads=[FV, SK], writes=[FV])
                                c.dma("pool", filt[o, ct * 128:(ct + 1) * 128, dirn * L + ch * CH:dirn * L + (ch + 1) * CH], FV[:], reads=[FV], acc=[filt])
            if pas == 0:
                RED = c.sb([128, 32], name="hyRED")
                c.dve(lambda e: e.reduce_sum(RED[:], ACC[:], axis=AX.X), reads=[ACC, RED], writes=[RED])
                R4 = RED[:, :].rearrange("p (o d c) -> p o d c", o=2, d=2)
                N4 = NRM[:, :].rearrange("p (o c) -> p o c", o=2)
                c.dve(lambda e: e.tensor_tensor(N4, R4[:, :, 0, :], R4[:, :, 1, :], ALU.add), reads=[RED, NRM], writes=[NRM])
                c.dve(lambda e: e.tensor_scalar_add(NRM[:], NRM[:], RMS_EPS), reads=[NRM], writes=[NRM])
                c.dve(lambda e: e.reciprocal(NRM[:], NRM[:]), reads=[NRM], writes=[NRM])
        c.pop()
        if stop == "F":
            return
        c.push()
        F = {k: s.load_const(k, shp) for k, shp in s.fft_shapes.items()}
        for k in ("T1", "T2", "RH1", "RH2", "Y"):
            F[k] = c.sb([128, G, 2, N1], name="ff_" + k)
        for k in ("S1", "S2"):
            F[k] = c.sb([N1, G, 2, 128], name="ff_" + k)
        F["BR"] = c.sb([N1, G, 128], name="ff_BR"); F["BI"] = c.sb([N1, G, 128], name="ff_BI")
        NG = 1024 // G
        spec = s.scratch("hy_spec", [2, NG, 128, G * 2 * N1])
        mixT = s.scratch("hy_mixT", [1024, L])
        UF = c.sb([N1, G, 128], name="ff_UF"); SP = c.sb([128, G, 2, N1], name="ff_SP")
        fv = filt.t.rearrange("o c (a n) -> o a c n", n=128)
        for o in range(2):
            for gi in range(NG):
                c.dma("sp", UF[:], fv[o, :, gi * G:(gi + 1) * G, :], reads=[filt], writes=[UF])
                X_b, cpb, nb = s.fft_fwd(UF[:], [UF], N1, G, F)
                w = 2 * N1
                for bi in range(nb):
                    gs = slice(bi * cpb, (bi + 1) * cpb)
                    c.act(lambda e: e.copy(SP[:, gs].rearrange("p g a k -> p (g a k)"), X_b[bi][:, 0:cpb * w]), reads=[X_b[bi], SP], writes=[SP])
                c.dma("pool", spec[o, gi], SP[:].rearrange("p g a k -> p (g a k)"), reads=[SP], acc=[spec])
        xg = xvT.t.rearrange("(t c) (a n) -> t a c n", t=3, n=128)
        gt = projT.t[3072:4096, :].rearrange("c (a n) -> a c n", n=128)
        mv = mixT.t.rearrange("c (a n) -> a c n", n=128)
        V = c.sb([Kp, G, 128], name="hy_V"); X1 = c.sb([Kp, G, 128], name="hy_X1"); X2 = c.sb([Kp, G, 128], name="hy_X2"); GT = c.sb([Kp, G, 128], name="hy_GT")
        Zt = c.sb([Kp, G, 128], name="hy_Z"); H0 = c.sb([128, G, 2, N1], name="hy_H0"); H1 = c.sb([128, G, 2, N1], name="hy_H1")
        for gi in range(NG):
            cs = slice(gi * G, (gi + 1) * G)
            c.dma("sp", V[:], xg[2, :, cs, :], reads=[xvT], writes=[V])
            c.dma("sp", X1[:], xg[0, :, cs, :], reads=[xvT], writes=[X1])
            c.dma("sp", X2[:], xg[1, :, cs, :], reads=[xvT], writes=[X2])
            c.dma("sp", GT[:], gt[:, cs, :], reads=[projT], writes=[GT])
            c.dma("sp", H0[:].rearrange("p g a k -> p (g a k)"), spec[0, gi], reads=[spec], writes=[H0])
            c.dma("sp", H1[:].rearrange("p g a k -> p (g a k)"), spec[1, gi], reads=[spec], writes=[H1])
            ob = s.P[6]
            s.fft_conv(V[:], [V], H0, G, F, ob)
            c.dve(lambda e: e.tensor_tensor(Zt[:], X1[:], ob[0:Kp, 0:G * 128].rearrange("p (g n) -> p g n", g=G), ALU.mult), reads=[X1, ob, Zt], writes=[Zt])
            ob2 = s.P[7]
            s.fft_conv(Zt[:], [Zt], H1, G, F, ob2)
            c.act(lambda e: e.activation(out=GT[:], in_=GT[:], func=AF.Silu), reads=[GT], writes=[GT])
            c.dve(lambda e: e.tensor_tensor(X2[:], X2[:], ob2[0:Kp, 0:G * 128].rearrange("p (g n) -> p g n", g=G), ALU.mult), reads=[X2, ob2], writes=[X2])
            c.dve(lambda e: e.tensor_tensor(X2[:], X2[:], GT[:], ALU.mult), reads=[X2, GT], writes=[X2])
            c.dma("pool", mv[:, cs, :], X2[:], reads=[X2], acc=[mixT])
        c.pop()
        if stop == "G":
            return
        c.push()
        TB = s.TBp
        MS = c.sb([128, 8, TB], F32, name="l1_ms"); MB = c.sb([128, 8, TB], BF16, name="l1_mb")
        mxv = mixT.t.rearrange("(k p) l -> p k l", p=128)

        def prep(b):
            c.dma("sp", MS[:], mxv[:, :, b * TB:(b + 1) * TB], reads=[mixT], writes=[MS])
            c.act(lambda e: e.copy(MB[:], MS[:]), reads=[MS, MB], writes=[MB])
            return [(MB, k, 128, 0, k) for k in range(8)]
        s.post(1, prep, [(0, 1024, 128)], hres, pT, s.out, None)
        c.pop()

    def build(s, stop=None):
        c = s.c
        L, NT = s.L, s.NT
        I = s.inp
        xT = I("xT", [1024, L])
        x = I("x", [L, 1024])
        pT = I("pT", [2, 256, L])
        w_in0 = I("w_in0", [1024, EVEN_IN_PAD])
        out = s.c.dram("out", [L, 1024], F32, kind="ExternalOutput")
        s.out = out
        s.psum_banks()
        s.K = {k: s.load_const(k, shp) for k, shp in CONST_SHAPES.items()}
        last0 = (1 not in s.layers)
        s.h1 = out if last0 else s.scratch("h1", [L, 1024])
        s.h1T = None if last0 else s.scratch("h1T", [1024, L])
        if 0 in s.layers:
            s.layer0(xT, x, pT, w_in0, stop)
        if 1 in s.layers:
            N1 = 2 * L // 128
            s.fft_shapes = {"F1": [N1, 2 * N1], "TW": [128, 2, N1], "F2re": [128, 128], "F2im": [128, 128], "CF1": [128, 256], "CF2": [128, 256],
                            "TWT": [N1, 2, 128], "IF1re": [N1, N1 // 2], "IF1imn": [N1, N1 // 2]}
            if 0 in s.layers:
                s.layer1(s.h1T, s.h1, pT, stop)
            else:
                s.layer1(xT, x, pT, stop)
        return s.nc

    def layer0(s, xT, x, pT, w_in0, stop):
        c = s.c
        L, NT = s.L, s.NT
        I = s.inp
        projT = s.scratch("projT", [EVEN_IN_PAD, L])
        dtb_d = I("dn_dt_bias", [1, 8])
        alog_d = I("dn_a_log", [1, 8])
        dtb = s.load_bc(dtb_d.t, 8, dtb_d, "dtb")
        alog = s.load_bc(alog_d.t, 8, alog_d, "alog")
        ABraw = c.sb([128, NT, 16], name="ABraw")
        GB = c.sb([128, NT, 16], name="GB")
        s.GB = GB
        s.proj_fm(xT, w_in0, EVEN_IN_PAD, projT, tm_cols=(1536, 1552), tm_dst=ABraw)
        s.dn_gates(ABraw, GB, dtb, alog)
        if s.debug:
            gbd = s.scratch("gb_dbg", [128, NT, 16])
            c.dma("sp", gbd[:], GB[:], reads=[GB], writes=[gbd])
        if stop == "A1":
            return
        CW = s.load_const("dn_cw", [128, 3, 4, 5])
        qT = s.scratch("dn_qT", [128, 4, L])
        kT = s.scratch("dn_kT", [128, 4, L])
        k_tm = s.scratch("dn_ktm", [L, 4, 128])
        v_tm = s.scratch("dn_vtm", [L, 4, 128])
        s.dn_prep(projT, CW, qT, kT, k_tm, v_tm)
        if stop == "A2":
            return
        rk_tm = s.scratch("rk_tm", [7, L, 512])
        bonusT = s.scratch("rk_bonusT", [128, 4, L])
        s.rk_prep(projT, rk_tm, bonusT)
        if stop == "A4":
            return
        c.push()
        W = s.alloc_chunk_ws()
        oT = [s.scratch(f"dn_oT{d}", [128, 4, L]) for d in range(2)]
        for d in range(2):
            s.dn_pass(d, qT, kT, k_tm, v_tm, oT[d], W)
        c.pop()
        if stop == "B1":
            return
        c.push()
        W = {k: c.sb([128, 4, 128], name="w2_" + k) for k in ("R", "NA", "NB", "PA", "PB")}
        R_ = s.alloc_rk_ws()
        yT = [s.scratch(f"rk_yT{d}", [64, 8, L]) for d in range(2)]
        for d in range(2):
            s.rk_pass(d, rk_tm, yT[d], W, R_)
        c.pop()
        if stop == "B2":
            return
        c.push()
        prep = s.l0_epilogue_setup(projT, oT, yT, bonusT)
        s.post(0, prep, [(0, 512, 128), (512, 1024, 64)], x, pT, s.h1, s.h1T)
        c.pop()

    def finish(s):
        c = s.c
        c.barrier()
        c.close()


def host_layout(inp, b, L):
    f = lambda a: np.ascontiguousarray(np.asarray(a, dtype=np.float32))
    m = {}
    xb = np.asarray(inp["x"])[b, :L]
    m["x"] = f(xb)
    m["xT"] = f(xb.T)
    m["pT"] = f(np.asarray(inp["p"])[:, b, :L].transpose(0, 2, 1))
    w = np.asarray(inp["even_w_in"])[0]
    m["w_in0"] = f(np.pad(w, ((0, 0), (0, EVEN_IN_PAD - w.shape[1]))))
    m["dn_dt_bias"] = f(np.asarray(inp["dn_dt_bias"])[0].reshape(1, 8))
    m["dn_a_log"] = f(np.asarray(inp["dn_a_log"])[0].reshape(1, 8))
    dc = np.asarray(inp["dn_conv"])[0]
    m["dn_cw"] = f(dc.reshape(5, 3, 4, 128).transpose(3, 1, 2, 0))
    mu = np.asarray(inp["rk_mu"])[0]
    m["rk_mu_rkv"] = f(mu[:, :1536].reshape(2, 12, 128).transpose(2, 1, 0))
    m["rk_mu_lora"] = f(mu[:, 1536:].reshape(2, 3, 64).transpose(2, 1, 0))
    m["rk_w2"] = f(np.asarray(inp["rk_w2"])[0].transpose(1, 0, 2))
    m["rk_a2"] = f(np.asarray(inp["rk_a2"])[0])
    cols = np.stack([np.asarray(inp[k])[0].reshape(512) for k in ("rk_a0", "rk_k_k", "rk_k_a", "rk_r_k", "rk_ln_w", "rk_ln_b")], 0)
    m["rk_cols"] = f(cols.reshape(6, 4, 128).transpose(2, 0, 1))
    m["rk_w0"] = f(np.asarray(inp["rk_w0"])[0].reshape(1, 1024))
    m["rk_a0"] = f(np.asarray(inp["rk_a0"])[0].reshape(1, 512))
    m["dn_norm_col"] = f(np.asarray(inp["dn_norm"])[0].reshape(128, 1))
    m["rk_ln64"] = f(np.stack([np.asarray(inp["rk_ln_w"])[0].reshape(8, 64).T, np.asarray(inp["rk_ln_b"])[0].reshape(8, 64).T], 1))
    for l in range(2):
        m[f"w_out{l}"] = f(np.asarray(inp["w_out"])[l])
        m[f"ple_w{l}"] = f(np.asarray(inp["ple_w"])[l])
        m[f"ple_gate{l}"] = f(np.asarray(inp["ple_gate"])[l])
        m[f"vecs{l}"] = f(np.stack([np.asarray(inp["ln_g"])[l], np.asarray(inp["ln_b"])[l], np.asarray(inp["ple_norm"])[l]], 0))
    m["w_in1"] = f(np.asarray(inp["odd_w_in"])[0])
    cw = np.asarray(inp["hy_conv_w"])[0]; cb = np.asarray(inp["hy_conv_b"])[0]
    m["hy_cw"] = f(np.concatenate([cw, cb[None]], 0).reshape(4, 24, 128).transpose(2, 1, 0))
    m["hy_w1"] = f(np.asarray(inp["hy_ffn_w1"])[0]); m["hy_w2"] = f(np.asarray(inp["hy_ffn_w2"])[0]); m["hy_w3"] = f(np.asarray(inp["hy_ffn_w3"])[0])
    m["hy_bf"] = f(np.stack([np.asarray(inp[k])[0] for k in ("hy_ffn_b1", "hy_ffn_b2", "hy_ffn_b3", "hy_ffn_freq")], 1))
    m["hy_wo"] = f(np.asarray(inp["hy_ffn_out"])[0])
    m["hy_skipc"] = f(np.asarray(inp["hy_skip"])[0].reshape(2, 8, 128).transpose(2, 0, 1))
    m["hy_deltas"] = f(np.asarray(inp["hy_deltas"])[0].reshape(1, 4096))
    m.update(host_consts(L))
    m.update(hyena_consts(L))
    return m


def hyena_consts(L):
    c = {}
    bands = 16
    t = np.linspace(0.0, 1.0, L, dtype=np.float32).astype(np.float64)
    fr = np.linspace(1e-4, bands - 1, bands, dtype=np.float32).astype(np.float64)
    ang = (np.float32(2.0 * math.pi / L) * np.arange(L, dtype=np.float32)).astype(np.float64)[:, None] * fr[None, :]
    feats = np.concatenate([t[:, None], np.cos(ang), -np.sin(ang)], -1)
    rev = np.concatenate([feats[:1], feats[:0:-1]], 0)
    c["hy_FT"] = np.ascontiguousarray(np.stack([feats.T, rev.T], 1)).astype(np.float32)
    N = 2 * L
    N1 = N // 128
    n1 = np.arange(N1); k1 = np.arange(N1); n2 = np.arange(128); k2 = np.arange(128)
    a1 = 2 * np.pi * np.outer(n1, k1) / N1
    c["F1"] = np.concatenate([np.cos(a1), -np.sin(a1)], 1).astype(np.float32)
    atw = 2 * np.pi * np.outer(n2, k1) / N
    c["TW"] = np.stack([np.cos(atw), -np.sin(atw)], 1).astype(np.float32)
    a2 = 2 * np.pi * np.outer(n2, k2) / 128
    c["F2re"] = np.cos(a2).astype(np.float32); c["F2im"] = (-np.sin(a2)).astype(np.float32)
    c["CF1"] = np.concatenate([np.cos(a2), np.sin(a2)], 1).astype(np.float32)
    c["CF2"] = np.concatenate([-np.sin(a2), np.cos(a2)], 1).astype(np.float32)
    c["TWT"] = np.stack([np.cos(atw).T, -np.sin(atw).T], 1).astype(np.float32)
    c["IF1re"] = (np.cos(a1)[:, :N1 // 2] / N).astype(np.float32)
    c["IF1imn"] = (-np.sin(a1)[:, :N1 // 2] / N).astype(np.float32)
    return c


def run(inp, L, nb, debug=False, stop=None, layers=(0, 1)):
    pr = Prog(L, debug=debug, layers=layers)
    nc = pr.build(stop=stop)
    pr.finish()
    print("ninst", pr.c.ninst, "nwait", pr.c.nwait)
    maps = []
    for b in range(nb):
        m = host_layout(inp, b, L)
        maps.append({k: m[k] for k in pr.inputs})
    res = run_bass_kernel_spmd(nc, maps, core_ids=list(range(nb)))
    return res.results, pr


def kernel(**inputs):
    L = inputs["x"].shape[1]
    res, pr = run(inputs, L, NCORES)
    return np.stack([r["out"] for r in res], 0).astype(np.float32)
```

```python
import contextlib, math
import numpy as np
import concourse.bass as bass
import concourse.mybir as mybir
from concourse.bass_utils import run_bass_kernel_spmd

F32 = mybir.dt.float32
BF16 = mybir.dt.bfloat16
AF = mybir.ActivationFunctionType
ALU = mybir.AluOpType
AX = mybir.AxisListType

D = 1024
NCORES = 8
ALPHA = 4.0 ** 0.25
LN_EPS = 1e-5
RMS_EPS = 1e-6
GN_EPS = 64e-5
EVEN_IN_PAD = 4352


class Buf:
    __slots__ = ("name", "w", "r")

    def __init__(s, name):
        s.name = name
        s.w = {}
        s.r = {}


class Dom:
    def __init__(s, name, sem, inc):
        s.name = name
        s.sem = sem
        s.inc = inc
        s.count = 0


class Stream:
    def __init__(s, name, eng):
        s.name = name
        s.eng = eng
        s.seen = {}


class T:
    def __init__(s, t, buf):
        s.t = t
        s.buf = buf

    def __getitem__(s, k):
        return s.t[k]


class Ctx:
    def __init__(s, nc):
        s.nc = nc
        s.stacks = [contextlib.ExitStack()]
        s.nwait = 0
        s.ninst = 0
        s.uid = 0

        def sem(n):
            return s.stacks[0].enter_context(nc.semaphore(n))

        s.st = {"pe": Stream("pe", nc.tensor), "act": Stream("act", nc.scalar), "dve": Stream("dve", nc.vector),
                "pool": Stream("pool", nc.gpsimd), "sp": Stream("sp", nc.sync)}
        s.dom = {"pe": Dom("pe", sem("s_pe"), 1), "act": Dom("act", sem("s_act"), 1),
                 "dve": Dom("dve", sem("s_dve"), 1), "pool": Dom("pool", sem("s_pool"), 1)}
        s.M = 12
        s.dma_doms = {}
        s.dma_rr = {}
        for q in ("sp", "pool"):
            s.dma_doms[q] = [Dom(f"{q}_dma{i}", sem(f"s_{q}d{i}"), 16) for i in range(s.M)]
            s.dma_rr[q] = 0
            for dm in s.dma_doms[q]:
                s.dom[dm.name] = dm
        s.rr = 0

    def push(s):
        s.stacks.append(contextlib.ExitStack())

    def pop(s):
        s.barrier()
        s.stacks.pop().close()

    def sb(s, shape, dt=F32, name=None):
        s.uid += 1
        name = f"{name or 'sb'}_{s.uid}"
        t = s.stacks[-1].enter_context(s.nc.sbuf_tensor(name, list(shape), dt))
        return T(t, Buf(name))

    def ps(s, shape, dt=F32, name=None):
        s.uid += 1
        name = name or f"ps{s.uid}"
        t = s.stacks[-1].enter_context(s.nc.psum_tensor(name, list(shape), dt))
        return T(t, Buf(name))

    def dram(s, name, shape, dt=F32, kind="Internal"):
        t = s.nc.dram_tensor(name, list(shape), dt, kind=kind)
        return T(t.ap(), Buf(name))

    def _bufs(s, xs):
        out = []
        for x in xs:
            if x is None:
                continue
            out.append(x.buf if isinstance(x, T) else x)
        return out

    def emit(s, stream, dom, fn, reads=(), writes=(), acc=(), pre=None):
        st = s.st[stream]
        dm = s.dom[dom]
        deps = {}
        if pre is not None and pre[1] > 0:
            deps[pre[0]] = pre[1]

        def add(d):
            for k, v in d.items():
                if deps.get(k, 0) < v:
                    deps[k] = v

        for b in s._bufs(reads):
            add(b.w)
        for b in s._bufs(writes):
            add(b.w)
            add(b.r)
        for b in s._bufs(acc):
            add(b.r)
        for k, v in deps.items():
            if k == dom and dm.inc == 1:
                if stream == "pe":
                    continue
                if v <= dm.count - 8:
                    continue
            if st.seen.get(k, 0) >= v:
                continue
            st.eng.wait_ge(s.dom[k].sem, v * s.dom[k].inc)
            st.seen[k] = v
            s.nwait += 1
        ins = fn(st.eng)
        dm.count += 1
        n = dm.count
        ins.then_inc(dm.sem, dm.inc)
        s.ninst += 1
        for b in s._bufs(reads):
            if b.r.get(dom, 0) < n:
                b.r[dom] = n
        for b in s._bufs(writes):
            b.w = {dom: n}
            b.r = {}
        for b in s._bufs(acc):
            b.w[dom] = n
        return ins

    def pe(s, fn, reads=(), writes=(), acc=()):
        return s.emit("pe", "pe", fn, reads, writes, acc)

    def act(s, fn, reads=(), writes=(), acc=()):
        return s.emit("act", "act", fn, reads, writes, acc)

    def dve(s, fn, reads=(), writes=(), acc=()):
        return s.emit("dve", "dve", fn, reads, writes, acc)

    def pool(s, fn, reads=(), writes=(), acc=()):
        return s.emit("pool", "pool", fn, reads, writes, acc)

    def any2(s, fn, reads=(), writes=(), acc=()):
        s.rr += 1
        if s.rr % 3 == 0:
            return s.pool(fn, reads, writes, acc)
        return s.dve(fn, reads, writes, acc)

    def dma(s, q, out, in_, reads=(), writes=(), acc=(), **kw):
        stream = {"sp": "sp", "pool": "pool"}[q]
        dm = s.dma_doms[q][s.dma_rr[q] % s.M]
        s.dma_rr[q] += 1
        return s.emit(stream, dm.name, lambda e: e.dma_start(out=out, in_=in_, **kw), reads, writes, acc, pre=(dm.name, dm.count))

    def barrier(s):
        for st in s.st.values():
            for k, dm in s.dom.items():
                if dm.count > 0 and st.seen.get(k, 0) < dm.count:
                    if k == st.name and dm.inc == 1 and st.name == "pe":
                        continue
                    st.eng.wait_ge(dm.sem, dm.count * dm.inc)
                    st.seen[k] = dm.count
                    s.nwait += 1

    def close(s):
        while s.stacks:
            s.stacks.pop().close()


def bc(ap, shape):
    return ap.to_broadcast(list(shape))


def host_consts(L):
    c = {}
    i = np.arange(128)
    c["ident"] = np.eye(128, dtype=np.float32)
    c["ones"] = np.ones((128, 128), np.float32)
    bd = np.zeros((128, 128), np.float32)
    bd[:64, :64] = 1
    bd[64:, 64:] = 1
    c["bd"] = bd
    J, C = np.meshgrid(i, i, indexing="ij")
    su = (C > J).astype(np.float32)
    sl = (C < J).astype(np.float32)
    iu = (C >= J).astype(np.float32)
    il = (C <= J).astype(np.float32)
    def rep4(m):
        return np.ascontiguousarray(np.broadcast_to(m[:, None, :], (128, 4, 128))).astype(np.float32)
    for d, (tri, sT, s_, iT) in enumerate([(iu, su, sl, iu), (il, sl, su, il)]):
        c[f"tri{d}"] = tri.astype(np.float32)
        c[f"nsT{d}"] = rep4(-sT)
        c[f"ns{d}"] = rep4(-s_)
        c[f"iT{d}"] = rep4(iT)
        mid = 63 if d == 0 else 64
        sel = (tri[:, mid:mid + 1] * np.ones((1, 128))).astype(np.float32)
        c[f"trim{d}"] = (tri - sel).astype(np.float32)
        c[f"suf{d}"] = (sl if d == 0 else su).astype(np.float32)
        m2 = np.stack([sT, iT], 0)
        c[f"mpa{d}"] = np.ascontiguousarray(np.broadcast_to(m2.transpose(1, 0, 2)[:, None], (128, 4, 2, 128))).astype(np.float32)
        c[f"ps{d}"] = rep4(s_)
    return c


CONST_SHAPES = {"ident": [128, 128], "ones": [128, 128], "bd": [128, 128]}
for _d in range(2):
    CONST_SHAPES.update({f"tri{_d}": [128, 128], f"nsT{_d}": [128, 4, 128], f"ns{_d}": [128, 4, 128],
                         f"iT{_d}": [128, 4, 128], f"trim{_d}": [128, 128], f"suf{_d}": [128, 128],
                         f"mpa{_d}": [128, 4, 2, 128], f"ps{_d}": [128, 4, 128]})


class Prog:
    def __init__(s, L, debug=False, layers=(0, 1)):
        s.L = L
        s.NT = L // 128
        s.TB = min(512, L)
        s.NB = L // s.TB
        s.TBp = min(256, L)
        s.debug = debug
        s.layers = layers
        s.nc = bass.Bass("TRN2", target_bir_lowering=False)
        s.c = Ctx(s.nc)
        s.inputs = {}
        s.dbg_out = []

    def inp(s, name, shape):
        t = s.nc.dram_tensor(name, list(shape), F32, kind="ExternalInput").ap()
        s.inputs[name] = T(t, Buf(name))
        return s.inputs[name]

    def scratch(s, name, shape, dt=F32):
        kind = "ExternalOutput" if (s.debug and dt == F32) else "Internal"
        t = s.c.dram(name, shape, dt, kind=kind)
        if kind == "ExternalOutput":
            s.dbg_out.append(name)
        return t

    def load_const(s, name, shape, dt=F32, q="sp"):
        src = s.inp(name, shape)
        t = s.c.sb(shape, F32, name="c_" + name)
        s.c.dma(q, t[:], src[:], reads=[src], writes=[t])
        return t

    def load_bc(s, src_ap, n, srcbuf, name):
        t = s.c.sb([128, n], F32, name=name)
        s.c.dma("sp", t[:], src_ap.partition_broadcast(128), reads=[srcbuf], writes=[t])
        return t

    def psum_banks(s):
        s.P = [s.c.ps([128, 512], F32, name=f"bank{i}") for i in range(8)]

    def proj_fm(s, srcT, W, ncol, dst, tm_cols=None, tm_dst=None):
        c = s.c
        L, TB, NB = s.L, s.TB, s.NB
        nct = ncol // 128
        c.push()
        Wb = c.sb([128, 8, ncol], BF16, name="Wb")
        stage = c.sb([128, ncol], F32, name="wstage")
        Wv = W.t.rearrange("(k p) n -> p k n", p=128)
        for k in range(8):
            c.dma("sp", stage[:], Wv[:, k, :], reads=[W], writes=[stage])
            if k % 2 == 0:
                c.act(lambda e: e.copy(Wb[:, k, :], stage[:]), reads=[stage], acc=[Wb])
            else:
                c.dve(lambda e: e.tensor_copy(Wb[:, k, :], stage[:]), reads=[stage], acc=[Wb])
        xs = [c.sb([128, 8, TB], F32, name="xs0")] * 2
        xb = [c.sb([128, 8, TB], BF16, name=f"xb{i}") for i in range(2)]
        ob = [c.sb([128, 4, TB], F32, name=f"ob{i}") for i in range(2)]
        xv = srcT.t.rearrange("(k p) l -> p k l", p=128)
        dv = dst.t.rearrange("(g p) l -> p g l", p=128)
        nob = 0
        for b in range(NB):
            t0 = b * TB
            X, XB = xs[b % 2], xb[b % 2]
            c.dma("sp", X[:], xv[:, :, t0:t0 + TB], reads=[srcT], writes=[X])
            c.act(lambda e: e.copy(XB[:, 0:4, :], X[:, 0:4, :]), reads=[X], writes=[XB])
            c.dve(lambda e: e.tensor_copy(XB[:, 4:8, :], X[:, 4:8, :]), reads=[X], acc=[XB])
            for g0 in range(0, nct, 4):
                gn = min(4, nct - g0)
                O = ob[nob % 2]
                nob += 1
                for gi in range(gn):
                    ct = g0 + gi
                    pb = s.P[(ct) % 8]
                    for k in range(8):
                        c.pe(lambda e: e.matmul(pb[:, 0:TB], Wb[:, k, ct * 128:(ct + 1) * 128], XB[:, k, :],
                                                start=(k == 0), stop=(k == 7)),
                             reads=[Wb, XB], writes=[pb] if k == 0 else (), acc=[pb] if k else ())
                    if ct % 2 == 0:
                        c.act(lambda e: e.copy(O[:, gi, :], pb[:, 0:TB]), reads=[pb], writes=[O] if gi == 0 else (), acc=[O] if gi else ())
                    else:
                        c.dve(lambda e: e.tensor_copy(O[:, gi, :], pb[:, 0:TB]), reads=[pb], writes=[O] if gi == 0 else (), acc=[O] if gi else ())
                c.dma("pool", dv[:, g0:g0 + gn, t0:t0 + TB], O[:, 0:gn, :], reads=[O], acc=[dst])
            if tm_cols is not None:
                c0, c1 = tm_cols
                for tt in range(TB // 128):
                    pb = s.P[tt % 2]
                    for k in range(8):
                        c.pe(lambda e: e.matmul(pb[:, 0:c1 - c0], XB[:, k, tt * 128:(tt + 1) * 128], Wb[:, k, c0:c1],
                                                start=(k == 0), stop=(k == 7)),
                             reads=[Wb, XB], writes=[pb] if k == 0 else (), acc=[pb] if k else ())
                    ti = b * (TB // 128) + tt
                    c.dve(lambda e: e.tensor_copy(tm_dst[:, ti, :], pb[:, 0:c1 - c0]), reads=[pb], acc=[tm_dst])
        c.pop()

    def dn_gates(s, ABraw, GB, dtb, alog):
        c = s.c
        NT = s.NT
        c.push()
        x = c.sb([128, NT, 8], name="gx")
        t1 = c.sb([128, NT, 8], name="gt1")
        t2 = c.sb([128, NT, 8], name="gt2")
        nA = c.sb([128, 8], name="gnA")
        c.act(lambda e: e.activation(out=nA[:], in_=alog[:], func=AF.Exp), reads=[alog], writes=[nA])
        c.dve(lambda e: e.tensor_tensor(x[:], ABraw[:, :, 0:8], bc(dtb[:, :].unsqueeze(1), [128, NT, 8]), ALU.add),
              reads=[ABraw, dtb], writes=[x])
        c.act(lambda e: e.activation(out=t1[:], in_=x[:], func=AF.Abs), reads=[x], writes=[t1])
        c.act(lambda e: e.activation(out=t1[:], in_=t1[:], func=AF.Exp, scale=-1.0), reads=[t1], writes=[t1])
        c.dve(lambda e: e.tensor_scalar_add(t1[:], t1[:], 1.0), reads=[t1], writes=[t1])
        c.act(lambda e: e.activation(out=t1[:], in_=t1[:], func=AF.Ln), reads=[t1], writes=[t1])
        c.dve(lambda e: e.tensor_scalar_max(t2[:], x[:], 0.0), reads=[x], writes=[t2])
        c.dve(lambda e: e.tensor_tensor(t2[:], t2[:], t1[:], ALU.add), reads=[t1, t2], writes=[t2])
        c.dve(lambda e: e.scalar_tensor_tensor(GB[:, :, 0:8], t2[:], -1.0, bc(nA[:, :].unsqueeze(1), [128, NT, 8]),
                                               ALU.mult, ALU.mult), reads=[t2, nA], acc=[GB])
        c.act(lambda e: e.activation(out=GB[:, :, 8:16], in_=ABraw[:, :, 8:16], func=AF.Sigmoid), reads=[ABraw], acc=[GB])
        c.pop()

    def dn_prep(s, projT, CW, qT, kT, k_tm, v_tm):
        c = s.c
        L, TB, NB = s.L, s.TB, s.NB
        NTB = TB // 128
        c.push()
        pv = projT.t[0:1536, :].rearrange("(t h p) l -> p t h l", t=3, h=4)
        ktv = k_tm.t.rearrange("(n p) h d -> p n h d", p=128)
        vtv = v_tm.t.rearrange("(n p) h d -> p n h d", p=128)
        ident, ones = s.K["ident"], s.K["ones"]
        Xs = [c.sb([128, 3, TB + 4], name=f"dnX{i}") for i in range(2)]
        As = [c.sb([128, 3, TB], name=f"dnA{i}") for i in range(2)]
        Ss = [c.sb([128, 2, TB], name=f"dnS{i}") for i in range(2)]
        KVs = [c.sb([128, 2, NTB, 128], name=f"dnK{i}") for i in range(2)]
        it = 0
        for h in range(4):
            for b in range(NB):
                t0 = b * TB
                X, A, S, KV = Xs[it % 2], As[it % 2], Ss[it % 2], KVs[it % 2]
                it += 1
                lo = max(t0 - 2, 0)
                hi = min(t0 + TB + 2, L)
                if lo != t0 - 2 or hi != t0 + TB + 2:
                    c.dve(lambda e: e.memset(X[:], 0.0), writes=[X])
                c.dma("sp", X[:, :, lo - (t0 - 2):hi - (t0 - 2)], pv[:, :, h, lo:hi], reads=[projT], writes=[X])
                for t in range(3):
                    c.dve(lambda e: e.tensor_scalar(A[:, t, :], X[:, t, 0:TB], CW[:, t, h, 0:1], None, ALU.mult),
                          reads=[X, CW], writes=[A])
                    for j in range(1, 5):
                        c.dve(lambda e: e.scalar_tensor_tensor(A[:, t, :], X[:, t, j:j + TB], CW[:, t, h, j:j + 1], A[:, t, :],
                                                               ALU.mult, ALU.add), reads=[X, CW, A], writes=[A])
                c.act(lambda e: e.activation(out=A[:], in_=A[:], func=AF.Silu), reads=[A], writes=[A])
                c.act(lambda e: e.activation(out=S[:], in_=A[:, 0:2, :], func=AF.Square), reads=[A], writes=[S])
                for t in range(2):
                    pb = s.P[t]
                    c.pe(lambda e: e.matmul(pb[:, 0:TB], ones[:], S[:, t, :], start=True, stop=True), reads=[S, ones], writes=[pb])
                for t in range(2):
                    pb = s.P[t]
                    sc = 128.0 if t == 0 else 1.0
                    c.dve(lambda e: e.tensor_scalar(S[:, t, :], pb[:, 0:TB], sc, RMS_EPS * sc, ALU.mult, ALU.add),
                          reads=[pb, S], writes=[S])
                c.act(lambda e: e.activation(out=S[:], in_=S[:], func=AF.Sqrt), reads=[S], writes=[S])
                c.dve(lambda e: e.reciprocal(S[:], S[:]), reads=[S], writes=[S])
                c.dve(lambda e: e.tensor_tensor(A[:, 0:2, :], A[:, 0:2, :], S[:], ALU.mult), reads=[A, S], writes=[A])
                c.dma("pool", qT[:, h, t0:t0 + TB], A[:, 0, :], reads=[A], acc=[qT])
                c.dma("pool", kT[:, h, t0:t0 + TB], A[:, 1, :], reads=[A], acc=[kT])
                for ti, t in enumerate((1, 2)):
                    pb = s.P[2 + ti]
                    for tt in range(NTB):
                        c.pe(lambda e: e.transpose(pb[:, tt * 128:(tt + 1) * 128], A[:, t, tt * 128:(tt + 1) * 128], ident[:]),
                             reads=[A, ident], writes=[pb])
                for ti in range(2):
                    pb = s.P[2 + ti]
                    if ti == 0:
                        c.act(lambda e: e.copy(KV[:, ti].rearrange("p n d -> p (n d)"), pb[:, 0:TB]), reads=[pb, KV], writes=[KV])
                    else:
                        c.dve(lambda e: e.tensor_copy(KV[:, ti].rearrange("p n d -> p (n d)"), pb[:, 0:TB]), reads=[pb, KV], writes=[KV])
                n0 = t0 // 128
                c.dma("pool", ktv[:, n0:n0 + NTB, h, :], KV[:, 0], reads=[KV], acc=[k_tm])
                c.dma("pool", vtv[:, n0:n0 + NTB, h, :], KV[:, 1], reads=[KV], acc=[v_tm])
        c.pop()

    def rk_prep(s, projT, rk_tm, bonusT):
        c = s.c
        L = s.L
        TB = min(256, L)
        NB = L // TB
        NTB = TB // 128
        I = s.inp
        c.push()
        MU = s.load_const("rk_mu_rkv", [128, 12, 2])
        MUL = s.load_const("rk_mu_lora", [64, 3, 2])
        W2 = s.load_const("rk_w2", [64, 2, 512])
        A2 = s.load_const("rk_a2", [64, 512])
        COLS = s.load_const("rk_cols", [128, 6, 4])
        w0_d = I("rk_w0", [1, 1024]); a0_d = I("rk_a0", [1, 512])
        W0 = s.load_bc(w0_d.t, 1024, w0_d, "rk_w0bc")
        A0 = s.load_bc(a0_d.t, 512, a0_d, "rk_a0bc")
        bd, ident = s.K["bd"], s.K["ident"]
        C0 = c.sb([128, 12], name="rkC0"); C0L = c.sb([64, 3], name="rkC0L"); OMK = c.sb([128, 4], name="rkOMK")
        c.dve(lambda e: e.tensor_tensor(C0[:], MU[:, :, 0], MU[:, :, 1], ALU.add), reads=[MU], writes=[C0])
        c.dve(lambda e: e.tensor_scalar(C0[:], C0[:], -1.0, 1.0, ALU.mult, ALU.add), reads=[C0], writes=[C0])
        c.dve(lambda e: e.tensor_tensor(C0L[:], MUL[:, :, 0], MUL[:, :, 1], ALU.add), reads=[MUL], writes=[C0L])
        c.dve(lambda e: e.tensor_scalar(C0L[:], C0L[:], -1.0, 1.0, ALU.mult, ALU.add), reads=[C0L], writes=[C0L])
        c.dve(lambda e: e.tensor_scalar(OMK[:], COLS[:, 2, :], -1.0, 1.0, ALU.mult, ALU.add), reads=[COLS], writes=[OMK])
        X = c.sb([128, 12, TB + 2], name="rkX"); XL = c.sb([64, 3, TB + 2], name="rkXL")
        Sx = c.sb([128, 12, TB], name="rkSx"); SL = c.sb([64, 3, TB], name="rkSL")
        TMP = c.sb([128, 12, TB], name="rkTMP")
        ET = c.sb([128, 4, TB], name="rkET"); KP = c.sb([128, 4, TB], name="rkKP"); KK = c.sb([128, 4, TB], name="rkKK")
        RS = c.sb([128, 4, TB], name="rkRS")
        TM = [c.sb([128, NTB, 512], name=f"rkTM{i}") for i in range(7)]
        pv = projT.t[2064:2064 + 1536, :].rearrange("(t p) l -> p t l", p=128)
        pl = projT.t[3600:3600 + 192, :].rearrange("(t p) l -> p t l", p=64)
        tmv = rk_tm.t.rearrange("i (n p) c -> i p n c", p=128)
        for b in range(NB):
            t0 = b * TB
            lo = max(t0 - 1, 0); hi = min(t0 + TB + 1, L)
            if lo != t0 - 1 or hi != t0 + TB + 1:
                c.dve(lambda e: e.memset(X[:], 0.0), writes=[X])
                c.dve(lambda e: e.memset(XL[:], 0.0), writes=[XL])
            c.dma("sp", X[:, :, lo - (t0 - 1):hi - (t0 - 1)], pv[:, :, lo:hi], reads=[projT], writes=[X])
            c.dma("sp", XL[:, :, lo - (t0 - 1):hi - (t0 - 1)], pl[:, :, lo:hi], reads=[projT], writes=[XL])
            c.dve(lambda e: e.tensor_tensor(Sx[:], X[:, :, 1:TB + 1], bc(C0[:, :].unsqueeze(2), [128, 12, TB]), ALU.mult), reads=[X, C0], writes=[Sx])
            c.dve(lambda e: e.tensor_tensor(TMP[:], X[:, :, 0:TB], bc(MU[:, :, 0:1], [128, 12, TB]), ALU.mult), reads=[X, MU], writes=[TMP])
            c.dve(lambda e: e.tensor_tensor(Sx[:], Sx[:], TMP[:], ALU.add), reads=[Sx, TMP], writes=[Sx])
            c.dve(lambda e: e.tensor_tensor(TMP[:], X[:, :, 2:TB + 2], bc(MU[:, :, 1:2], [128, 12, TB]), ALU.mult), reads=[X, MU, TMP], writes=[TMP])
            c.dve(lambda e: e.tensor_tensor(Sx[:], Sx[:], TMP[:], ALU.add), reads=[Sx, TMP], writes=[Sx])
            TL = TMP[0:64, 0:3, :]
            c.dve(lambda e: e.tensor_tensor(SL[:], XL[:, :, 1:TB + 1], bc(C0L[:, :].unsqueeze(2), [64, 3, TB]), ALU.mult), reads=[XL, C0L, TMP], writes=[SL])
            c.dve(lambda e: e.tensor_tensor(TL, XL[:, :, 0:TB], bc(MUL[:, :, 0:1], [64, 3, TB]), ALU.mult), reads=[XL, MUL, TMP], writes=[TMP])
            c.dve(lambda e: e.tensor_tensor(SL[:], SL[:], TL, ALU.add), reads=[SL, TMP], writes=[SL])
            c.dve(lambda e: e.tensor_tensor(TL, XL[:, :, 2:TB + 2], bc(MUL[:, :, 1:2], [64, 3, TB]), ALU.mult), reads=[XL, MUL, TMP], writes=[TMP])
            c.dve(lambda e: e.tensor_tensor(SL[:], SL[:], TL, ALU.add), reads=[SL, TMP], writes=[SL])
            c.act(lambda e: e.activation(out=SL[:, 0:2, :], in_=SL[:, 0:2, :], func=AF.Tanh), reads=[SL], writes=[SL])
            for tt in range(NTB):
                tsl = slice(tt * 128, (tt + 1) * 128)
                for i, (lsrc, rhs, addv, dsti) in enumerate([(SL[:, 0, tsl], W2[:, 0, :], W0[:, 0:512], 5),
                                                             (SL[:, 1, tsl], W2[:, 1, :], W0[:, 512:1024], 6),
                                                             (SL[:, 2, tsl], A2[:, :], A0[:, :], 4)]):
                    pb = s.P[4 + i]
                    c.pe(lambda e: e.matmul(pb[:, :], lsrc, rhs, start=True, stop=True), reads=[SL, W2, A2], writes=[pb])
                    c.dve(lambda e: e.tensor_tensor(TM[dsti][:, tt, :], pb[:, :], addv, ALU.add), reads=[pb, W0, A0, TM[dsti]], writes=[TM[dsti]])
            for dsti in (4, 5, 6):
                c.act(lambda e: e.activation(out=TM[dsti][:], in_=TM[dsti][:], func=AF.Sigmoid), reads=[TM[dsti]], writes=[TM[dsti]])
            for dsti in (5, 6):
                c.dve(lambda e: e.tensor_scalar(TM[dsti][:], TM[dsti][:], -math.exp(-0.5), None, ALU.mult), reads=[TM[dsti]], writes=[TM[dsti]])
            for ct in range(4):
                pb = s.P[ct]
                c.pe(lambda e: e.matmul(pb[:, 0:TB], A2[:, ct * 128:(ct + 1) * 128], SL[:, 2, :], start=True, stop=True), reads=[SL, A2], writes=[pb])
                c.act(lambda e: e.activation(out=ET[:, ct, :], in_=pb[:, 0:TB], func=AF.Sigmoid, bias=COLS[:, 0, ct:ct + 1]), reads=[pb, COLS, ET], writes=[ET])
                c.dve(lambda e: e.tensor_scalar(ET[:, ct, :], ET[:, ct, :], COLS[:, 2, ct:ct + 1], OMK[:, ct:ct + 1], ALU.mult, ALU.add), reads=[ET, COLS, OMK], writes=[ET])
            c.dve(lambda e: e.tensor_tensor(KP[:], Sx[:, 4:8, :], ET[:], ALU.mult), reads=[Sx, ET], writes=[KP])
            c.dve(lambda e: e.tensor_tensor(KK[:], Sx[:, 4:8, :], bc(COLS[:, 1, :].unsqueeze(2), [128, 4, TB]), ALU.mult), reads=[Sx, COLS], writes=[KK])
            c.act(lambda e: e.activation(out=RS[:], in_=KK[:], func=AF.Square), reads=[KK], writes=[RS])
            for ct in range(4):
                pb = s.P[ct]
                c.pe(lambda e: e.matmul(pb[:, 0:TB], bd[:], RS[:, ct, :], start=True, stop=True), reads=[RS, bd], writes=[pb])
            for ct in range(4):
                pb = s.P[ct]
                c.dve(lambda e: e.tensor_scalar_add(RS[:, ct, :], pb[:, 0:TB], RMS_EPS), reads=[pb, RS], writes=[RS])
            c.act(lambda e: e.activation(out=RS[:], in_=RS[:], func=AF.Sqrt), reads=[RS], writes=[RS])
            c.dve(lambda e: e.reciprocal(RS[:], RS[:]), reads=[RS], writes=[RS])
            c.dve(lambda e: e.tensor_tensor(KK[:], KK[:], RS[:], ALU.mult), reads=[KK, RS], writes=[KK])
            c.dve(lambda e: e.tensor_tensor(RS[:], Sx[:, 0:4, :], KP[:], ALU.mult), reads=[Sx, KP, RS], writes=[RS])
            c.dve(lambda e: e.tensor_tensor(RS[:], RS[:], bc(COLS[:, 3, :].unsqueeze(2), [128, 4, TB]), ALU.mult), reads=[RS, COLS], writes=[RS])
            for ct in range(4):
                pb = s.P[ct]
                c.pe(lambda e: e.matmul(pb[:, 0:TB], bd[:], RS[:, ct, :], start=True, stop=True), reads=[RS, bd], writes=[pb])
            for ct in range(4):
                pb = s.P[ct]
                c.dve(lambda e: e.tensor_tensor(ET[:, ct, :], pb[:, 0:TB], Sx[:, 8 + ct, :], ALU.mult), reads=[pb, Sx, ET], writes=[ET])
            c.dma("pool", bonusT[:, :, t0:t0 + TB], ET[:], reads=[ET], acc=[bonusT])
            for i, src in enumerate([Sx[:, 0:4, :], KP[:], Sx[:, 8:12, :], KK[:]]):
                for tt in range(NTB):
                    pb = s.P[4 + (i * NTB + tt) % 4]
                    for ct in range(4):
                        c.pe(lambda e: e.transpose(pb[:, ct * 128:(ct + 1) * 128], src[:, ct, tt * 128:(tt + 1) * 128], ident[:]),
                             reads=[Sx, KP, KK, ident], writes=[pb])
                    if (i + tt) % 2 == 0:
                        c.act(lambda e: e.copy(TM[i][:, tt, :], pb[:, :]), reads=[pb, TM[i]], writes=[TM[i]])
                    else:
                        c.dve(lambda e: e.tensor_copy(TM[i][:, tt, :], pb[:, :]), reads=[pb, TM[i]], writes=[TM[i]])
            n0 = t0 // 128
            for i in range(7):
                c.dma("pool", tmv[i, :, n0:n0 + NTB, :], TM[i][:], reads=[TM[i]], acc=[rk_tm])
        c.pop()

    def tri_inverse(s, N0, P0, srcbufs, W, banks=None):
        c = s.c
        ident = s.K["ident"]
        R, NA, NB_, PA_, PB_ = W["R"], W["NA"], W["NB"], W["PA"], W["PB"]
        c.dve(lambda e: e.tensor_tensor(R[:], P0, bc(ident[:, :].unsqueeze(1), [128, 4, 128]), ALU.add),
              reads=srcbufs + [ident], writes=[R])
        Ncur, Pcur, nb, pb_ = N0, P0, srcbufs, srcbufs
        Nbufs, Pbufs = [NA, NB_], [PA_, PB_]
        v4 = lambda bank: bank[:, :].rearrange("p (h c) -> p h c", h=4)
        for k in range(1, 7):
            Nn, Pn = Nbufs[k % 2], Pbufs[k % 2]
            bN, bP, bR = banks if banks is not None else (s.P[5], s.P[6], s.P[7])
            for h in range(4):
                c.pe(lambda e: e.matmul(bN[:, h * 128:(h + 1) * 128], Pcur[:, h, :], Ncur[:, h, :], start=True, stop=True),
                     reads=nb + pb_, writes=[bN])
            if k < 6:
                for h in range(4):
                    c.pe(lambda e: e.matmul(bP[:, h * 128:(h + 1) * 128], Ncur[:, h, :], Pcur[:, h, :], start=True, stop=True),
                         reads=nb + pb_, writes=[bP])
            c.act(lambda e: e.copy(Nn[:], v4(bN)), reads=[bN, Nn], writes=[Nn])
            if k < 6:
                c.dve(lambda e: e.tensor_copy(Pn[:], v4(bP)), reads=[bP, Pn], writes=[Pn])
            for h in range(4):
                c.pe(lambda e: e.matmul(bR[:, h * 128:(h + 1) * 128], Nn[:, h, :], R[:, h, :], start=True, stop=True),
                     reads=[Nn, R], writes=[bR])
            c.dve(lambda e: e.tensor_tensor(R[:], R[:], v4(bR), ALU.add), reads=[R, bR], writes=[R])
            Ncur, Pcur, nb, pb_ = Nn[:], Pn[:], [Nn], [Pn]
            yield
        return

    def dn_pass(s, d, qT, kT, k_tm, v_tm, oT, W, PBK):
        c = s.c
        L, NT = s.L, s.NT
        GB = s.GB
        K = s.K
        tri, nsT, ns, iT, ident, ones = K[f"tri{d}"], K[f"nsT{d}"], K[f"ns{d}"], K[f"iT{d}"], K["ident"], K["ones"]
        lastc = 127 if d == 0 else 0
        H = W["H"]
        c.dve(lambda e: e.memset(H[:], 0.0), writes=[H])
        v4 = lambda bank: bank[:, :].rearrange("p (h c) -> p h c", h=4)
        ktv = k_tm.t.rearrange("(n p) h d -> n p h d", p=128)
        vtv = v_tm.t.rearrange("(n p) h d -> n p h d", p=128)
        order = range(NT) if d == 0 else range(NT - 1, -1, -1)
        for it, n in enumerate(order):
            ld = W["ld"][it % 2]
            KT, QT, Ktm, Vtm = ld["KT"], ld["QT"], ld["Ktm"], ld["Vtm"]
            tsl = slice(n * 128, (n + 1) * 128)
            c.dma("sp", KT[:], kT[:, :, tsl], reads=[kT], writes=[KT])
            c.dma("sp", QT[:], qT[:, :, tsl], reads=[qT], writes=[QT])
            c.dma("sp", Ktm[:], ktv[n], reads=[k_tm], writes=[Ktm])
            c.dma("sp", Vtm[:], vtv[n], reads=[v_tm], writes=[Vtm])
            g = GB[:, n, d * 4:(d + 1) * 4]
            beta = GB[:, n, 8 + d * 4:8 + (d + 1) * 4]
            Gc, RG, RB, Grow, E1, E2, X1, X2, AT = (W[k] for k in ("Gc", "RG", "RB", "Grow", "E1", "E2", "X1", "X2", "AT"))
            b0, b1, b2, b3 = PBK
            b4, b5, b6, b7 = PBK
            c.pe(lambda e: e.matmul(b0[:, 0:4], tri[:], g, start=True, stop=True), reads=[tri, GB], writes=[b0])
            c.act(lambda e: e.copy(Gc[:], b0[:, 0:4]), reads=[b0, Gc], writes=[Gc])
            c.dve(lambda e: e.tensor_tensor(RG[:], bc(tri[:, :].unsqueeze(1), [128, 4, 128]), bc(g.unsqueeze(2), [128, 4, 128]), ALU.mult),
                  reads=[tri, GB, RG], writes=[RG])
            c.dve(lambda e: e.tensor_tensor(RB[:], bc(ident[:, :].unsqueeze(1), [128, 4, 128]), bc(beta.unsqueeze(2), [128, 4, 128]), ALU.mult),
                   reads=[ident, GB, RB], writes=[RB])
            c.pe(lambda e: e.matmul(b1[:, :], ones[:], RG[:].rearrange("p h c -> p (h c)"), start=True, stop=True), reads=[ones, RG], writes=[b1])
            c.pe(lambda e: e.matmul(b2[:, :], ones[:], RB[:].rearrange("p h c -> p (h c)"), start=True, stop=True), reads=[ones, RB], writes=[b2])
            for h in range(4):
                c.pe(lambda e: e.matmul(b3[:, h * 128:(h + 1) * 128], KT[:, h, :], KT[:, h, :], start=True, stop=True), reads=[KT], writes=[b3])
            for h in range(4):
                c.pe(lambda e: e.matmul(b4[:, h * 128:(h + 1) * 128], KT[:, h, :], QT[:, h, :], start=True, stop=True), reads=[KT, QT], writes=[b4])
            c.act(lambda e: e.copy(Grow[:], v4(b1)), reads=[b1, Grow], writes=[Grow])
            yield
            c.dve(lambda e: e.tensor_tensor(E1[:], Grow[:], bc(Gc[:, :].unsqueeze(2), [128, 4, 128]), ALU.subtract), reads=[Grow, Gc, E1], writes=[E1])
            c.dve(lambda e: e.tensor_scalar_max(E2[:], E1[:], 0.0), reads=[E1, E2], writes=[E2])
            c.dve(lambda e: e.tensor_scalar_min(E1[:], E1[:], 0.0), reads=[E1], writes=[E1])
            c.act(lambda e: e.activation(out=E1[:], in_=E1[:], func=AF.Exp), reads=[E1], writes=[E1])
            c.act(lambda e: e.activation(out=E2[:], in_=E2[:], func=AF.Exp, scale=-1.0), reads=[E2], writes=[E2])
            yield
            c.dve(lambda e: e.tensor_tensor(X1[:], E1[:], nsT[:], ALU.mult), reads=[E1, nsT, X1], writes=[X1])
            c.dve(lambda e: e.tensor_tensor(X1[:], X1[:], v4(b2), ALU.mult), reads=[X1, b2], writes=[X1])
            c.dve(lambda e: e.tensor_tensor(X1[:], X1[:], v4(b3), ALU.mult), reads=[X1, b3], writes=[X1])
            c.dve(lambda e: e.tensor_tensor(X2[:], E2[:], ns[:], ALU.mult), reads=[E2, ns, X2], writes=[X2])
            c.dve(lambda e: e.tensor_tensor(X2[:], X2[:], bc(beta.unsqueeze(2), [128, 4, 128]), ALU.mult), reads=[X2, GB], writes=[X2])
            c.dve(lambda e: e.tensor_tensor(X2[:], X2[:], v4(b3), ALU.mult), reads=[X2, b3], writes=[X2])
            c.dve(lambda e: e.tensor_tensor(AT[:], E1[:], iT[:], ALU.mult), reads=[E1, iT, AT], writes=[AT])
            c.dve(lambda e: e.tensor_tensor(AT[:], AT[:], v4(b4), ALU.mult), reads=[AT, b4], writes=[AT])
            yield
            yield from s.tri_inverse(X2[:], X1[:], [X1, X2], W, banks=(PBK[1], PBK[2], PBK[3]))
            R = W["R"]
            S4, CO, EGT, GAM, KBN, VB, KTL, QD = (W[k] for k in ("S4", "CO", "EGT", "GAM", "KBN", "VB", "KTL", "QD"))
            c.act(lambda e: e.activation(out=S4[:], in_=Gc[:], func=AF.Exp), reads=[Gc, S4], writes=[S4])
            c.dve(lambda e: e.scalar_tensor_tensor(CO[:], S4[:], -1.0, beta, ALU.mult, ALU.mult), reads=[S4, GB, CO], writes=[CO])
            c.dve(lambda e: e.tensor_tensor(KBN[:], Ktm[:], bc(CO[:, :].unsqueeze(2), [128, 4, 128]), ALU.mult), reads=[Ktm, CO, KBN], writes=[KBN])
            c.dve(lambda e: e.tensor_tensor(VB[:], Vtm[:], bc(beta.unsqueeze(2), [128, 4, 128]), ALU.mult), reads=[Vtm, GB, VB], writes=[VB])
            c.dve(lambda e: e.tensor_tensor(EGT[:], Grow[:, :, lastc], Gc[:], ALU.subtract), reads=[Grow, Gc, EGT], writes=[EGT])
            c.act(lambda e: e.activation(out=EGT[:], in_=EGT[:], func=AF.Exp), reads=[EGT], writes=[EGT])
            c.act(lambda e: e.activation(out=GAM[:], in_=Grow[:, :, lastc], func=AF.Exp), reads=[Grow, GAM], writes=[GAM])
            c.dve(lambda e: e.tensor_tensor(KTL[:], Ktm[:], bc(EGT[:, :].unsqueeze(2), [128, 4, 128]), ALU.mult), reads=[Ktm, EGT, KTL], writes=[KTL])
            c.act(lambda e: e.activation(out=QD[:], in_=Grow[:], func=AF.Exp), reads=[Grow, QD], writes=[QD])
            c.dve(lambda e: e.tensor_tensor(QD[:], QD[:], QT[:], ALU.mult), reads=[QD, QT], writes=[QD])
            yield
            WT, U0, U, OS = W["WT"], W["U0"], W["U"], W["OS"]
            for h in range(4):
                c.pe(lambda e: e.matmul(b5[:, h * 128:(h + 1) * 128], KBN[:, h, :], R[:, h, :], start=True, stop=True), reads=[KBN, R], writes=[b5])
            for h in range(4):
                c.pe(lambda e: e.matmul(b6[:, h * 128:(h + 1) * 128], R[:, h, :], VB[:, h, :], start=True, stop=True), reads=[VB, R], writes=[b6])
            c.act(lambda e: e.copy(WT[:], v4(b5)), reads=[b5, WT], writes=[WT])
            c.dve(lambda e: e.tensor_copy(U0[:], v4(b6)), reads=[b6, U0], writes=[U0])
            yield
            for h in range(4):
                c.pe(lambda e: e.matmul(b7[:, h * 128:(h + 1) * 128], WT[:, h, :], H[:, h, :], start=True, stop=True), reads=[WT, H], writes=[b7])
            c.dve(lambda e: e.tensor_tensor(U[:], U0[:], v4(b7), ALU.add), reads=[U0, b7, U], writes=[U])
            yield
            for h in range(4):
                c.pe(lambda e: e.matmul(b5[:, h * 128:(h + 1) * 128], H[:, h, :], QD[:, h, :], start=True, stop=False), reads=[H, QD], writes=[b5])
                c.pe(lambda e: e.matmul(b5[:, h * 128:(h + 1) * 128], U[:, h, :], AT[:, h, :], start=False, stop=True), reads=[U, AT], writes=[b5])
            for h in range(4):
                c.pe(lambda e: e.matmul(b6[:, h * 128:(h + 1) * 128], KTL[:, h, :], U[:, h, :], start=True, stop=True), reads=[KTL, U], writes=[b6])
            c.act(lambda e: e.copy(OS[:], v4(b5)), reads=[b5, OS], writes=[OS])
            c.dma("pool", oT[:, :, tsl], OS[:], reads=[OS], acc=[oT])
            c.dve(lambda e: e.tensor_tensor(H[:], H[:], bc(GAM[:, :].unsqueeze(2), [128, 4, 128]), ALU.mult), reads=[H, GAM], writes=[H])
            c.dve(lambda e: e.tensor_tensor(H[:], H[:], v4(b6), ALU.add), reads=[H, b6], writes=[H])
            yield

    def run_chains(s, gens):
        gens = list(gens)
        while gens:
            for g in list(gens):
                try:
                    next(g)
                except StopIteration:
                    gens.remove(g)

    def alloc_chunk_ws(s):
        c = s.c
        W = {}
        for k in ("R", "NA", "NB", "PA", "PB", "RG", "RB", "Grow", "E1", "E2", "X1", "X2", "AT", "KBN", "VB", "KTL", "QD", "WT", "U0", "U", "OS", "H"):
            W[k] = c.sb([128, 4, 128], name="w_" + k)
        for k in ("Gc", "S4", "CO", "EGT", "GAM"):
            W[k] = c.sb([128, 4], name="w_" + k)
        W["ld"] = [{k: c.sb([128, 4, 128], name=f"ld{i}_{k}") for k in ("KT", "QT", "Ktm", "Vtm")} for i in range(2)]
        return W

    def rk_pass(s, d, rk_tm, yT, W, R_):
        c = s.c
        L, NT = s.L, s.NT
        K = s.K
        trim, suf, mpa, psm, ident, ones = K[f"trim{d}"], K[f"suf{d}"], K[f"mpa{d}"], K[f"ps{d}"], K["ident"], K["ones"]
        H, GAM = R_["H"], R_["GAM"]
        c.dve(lambda e: e.memset(H[:], 0.0), writes=R_["bufs"]["H"])
        order = range(NT) if d == 0 else range(NT - 1, -1, -1)
        b0, b1, b2, b3, b4, b5, b6, b7 = s.P
        q4 = lambda ap, n: ap.rearrange("p (q c) -> p q c", q=n)
        for it, n in enumerate(order):
            ld = R_["ld"][it % 2]
            tsl = slice(n * 128, (n + 1) * 128)
            for i, key in [(0, "r"), (1, "kp"), (2, "v"), (3, "kk"), (4, "eta"), (5 + d, "lw")]:
                c.dma("sp", ld[key][:], rk_tm[i, tsl, :], reads=[rk_tm], writes=[ld[key]])
            r, kp, V, kk, eta, LW = (ld[k] for k in ("r", "kp", "v", "kk", "eta", "lw"))
            Ep, En, Ex, Et, Bv, rt, kt, bt, at, kh, bh = (R_[k] for k in ("Ep", "En", "Ex", "Et", "B", "rt", "kt", "bt", "at", "kh", "bh"))
            c.pe(lambda e: e.matmul(b0[:, :], trim[:], LW[:], start=True, stop=True), reads=[trim, LW], writes=[b0])
            c.pe(lambda e: e.matmul(b1[:, :], suf[:], LW[:], start=True, stop=True), reads=[suf, LW], writes=[b1])
            for h in range(8):
                c.pe(lambda e: e.matmul(b2[0:64, h:h + 1], LW[:, h * 64:(h + 1) * 64], ones[:, 0:1], start=True, stop=True), reads=[LW, ones], writes=[b2])
            mid = 63 if d == 0 else 64
            for h in range(8):
                c.pe(lambda e: e.matmul(b2[0:64, 8 + h:9 + h], LW[:, h * 64:(h + 1) * 64], K[f"tri{d}"][:, mid:mid + 1], start=True, stop=True), reads=[LW, K[f"tri{d}"]], writes=[b2])
            c.act(lambda e: e.activation(out=GAM[:], in_=b2[0:64, 0:16], func=AF.Exp), reads=[b2, GAM], writes=[GAM])
            c.act(lambda e: e.activation(out=Ep[:], in_=b0[:, :], func=AF.Exp), reads=[b0, Ep], writes=[Ep])
            c.act(lambda e: e.activation(out=En[:], in_=b0[:, :], func=AF.Exp, scale=-1.0), reads=[b0, En], writes=[En])
            c.dve(lambda e: e.tensor_tensor(Ex[:], b0[:, :], LW[:], ALU.subtract), reads=[b0, LW, Ex], writes=[Ex])
            c.act(lambda e: e.activation(out=Ex[:], in_=Ex[:], func=AF.Exp), reads=[Ex], writes=[Ex])
            c.act(lambda e: e.activation(out=Et[:], in_=b1[:, :], func=AF.Exp), reads=[b1, Et], writes=[Et])
            c.dve(lambda e: e.tensor_tensor(Bv[:], kk[:], eta[:], ALU.mult), reads=[kk, eta, Bv], writes=[Bv])
            c.dve(lambda e: e.tensor_tensor(rt[:], r[:], Ep[:], ALU.mult), reads=[r, Ep, rt], writes=[rt])
            c.dve(lambda e: e.tensor_tensor(kt[:], kp[:], En[:], ALU.mult), reads=[kp, En, kt], writes=[kt])
            c.dve(lambda e: e.tensor_tensor(bt[:], Bv[:], En[:], ALU.mult), reads=[Bv, En, bt], writes=[bt])
            c.dve(lambda e: e.scalar_tensor_tensor(at[:], kk[:], -1.0, Ex[:], ALU.mult, ALU.mult), reads=[kk, Ex, at], writes=[at])
            c.dve(lambda e: e.tensor_tensor(kh[:], kp[:], Et[:], ALU.mult), reads=[kp, Et, kh], writes=[kh])
            c.dve(lambda e: e.tensor_tensor(bh[:], Bv[:], Et[:], ALU.mult), reads=[Bv, Et, bh], writes=[bh])
            RAT, BT, KT2 = R_["RAT"], R_["BT"], R_["KT2"]
            tb = [b3, b4, b5, b6]
            ti = 0
            for src, dst, dbuf in [(at, lambda hs: RAT[:, hs, 0, :], RAT), (rt, lambda hs: RAT[:, hs, 1, :], RAT), (bt, lambda hs: BT[:, hs, :], BT), (kt, lambda hs: KT2[:, hs, :], KT2)]:
                for half in range(2):
                    bank = tb[ti % 4]
                    ti += 1
                    for q in range(4):
                        h = half * 4 + q
                        c.pe(lambda e: e.transpose(bank[0:64, q * 128:(q + 1) * 128], src[:, h * 64:(h + 1) * 64], ident[:]), reads=[src, ident], writes=[bank])
                    hs = slice(half * 4, half * 4 + 4)
                    if ti % 2 == 0:
                        c.act(lambda e: e.copy(dst(hs), q4(bank[0:64, :], 4)), reads=[bank, dbuf], writes=[dbuf])
                    else:
                        c.dve(lambda e: e.tensor_copy(dst(hs), q4(bank[0:64, :], 4)), reads=[bank, dbuf], writes=[dbuf])
            PAs, AKs, NNs, WT, AV, U0, U, YS = (R_[k] for k in ("PAs", "AKs", "NNs", "WT", "AV", "U0", "U", "YS"))
            PAb, AKb, NNb, WTb, AVb, U0b, Ub, YSb, Hb, Hsb = (R_["bufs"][k] for k in ("PAs", "AKs", "NNs", "WT", "AV", "U0", "U", "YS", "H", "Hs"))
            Hs = R_["Hs"]

            def hg_chain(hg, BK, Wd):
                B0, B1, B2, B3 = BK
                hs = slice(hg * 4, hg * 4 + 4)
                for q in range(4):
                    h = hg * 4 + q
                    bpa = (B0, B1)[q // 2]
                    bak = (B2, B3)[q // 2]
                    osl = slice((q % 2) * 256, (q % 2 + 1) * 256)
                    rhs2 = RAT[:, h, :, :].rearrange("p a c -> p (a c)")
                    c.pe(lambda e: e.matmul(bpa[:, osl], BT[:, h, :], rhs2, start=True, stop=True), reads=[BT, RAT], writes=[bpa])
                    c.pe(lambda e: e.matmul(bak[:, osl], KT2[:, h, :], rhs2, start=True, stop=True), reads=[KT2, RAT], writes=[bak])
                yield
                v22 = lambda bank: bank[:, :].rearrange("p (q a c) -> p q a c", q=2, a=2)
                for j in range(2):
                    h2 = slice(hg * 4 + 2 * j, hg * 4 + 2 * j + 2)
                    c.dve(lambda e: e.tensor_tensor(PAs[:, h2], v22((B0, B1)[j]), mpa[:, 0:2], ALU.mult), reads=[(B0, B1)[j], mpa], writes=[PAb[hg]])
                    c.dve(lambda e: e.tensor_tensor(AKs[:, h2], v22((B2, B3)[j]), mpa[:, 0:2], ALU.mult), reads=[(B2, B3)[j], mpa], writes=[AKb[hg]])
                for q in range(4):
                    h = hg * 4 + q
                    c.pe(lambda e: e.matmul(B0[:, q * 128:(q + 1) * 128], RAT[:, h, 0, :], BT[:, h, :], start=True, stop=True), reads=[BT, RAT], writes=[B0])
                yield
                c.dve(lambda e: e.tensor_tensor(NNs[:, hs], q4(B0[:, :], 4), psm[:], ALU.mult), reads=[B0, psm], writes=[NNb[hg]])
                yield from s.tri_inverse(NNs[:, hs], PAs[:, hs, 0, :], [NNb[hg], PAb[hg]], Wd, banks=(B1, B2, B3))
                R = Wd["R"]
                for q in range(4):
                    h = hg * 4 + q
                    c.pe(lambda e: e.matmul(B0[0:64, q * 128:(q + 1) * 128], at[:, h * 64:(h + 1) * 64], R[:, q, :], start=True, stop=True), reads=[at, R], writes=[B0])
                    c.pe(lambda e: e.matmul(B1[:, q * 64:(q + 1) * 64], AKs[:, h, 0, :], V[:, h * 64:(h + 1) * 64], start=True, stop=True), reads=[AKb[hg], V], writes=[B1])
                yield
                c.act(lambda e: e.copy(WT[:, hs, :], q4(B0[0:64, :], 4)), reads=[B0], writes=[WTb[hg]])
                c.dve(lambda e: e.tensor_copy(AV[:, hs, :], q4(B1[:, 0:256], 4)), reads=[B1], writes=[AVb[hg]])
                for q in range(4):
                    h = hg * 4 + q
                    c.pe(lambda e: e.matmul(B2[:, q * 64:(q + 1) * 64], R[:, q, :], AV[:, h, :], start=True, stop=True), reads=[AVb[hg], R], writes=[B2])
                yield
                c.act(lambda e: e.copy(U0[:, hs, :], q4(B2[:, 0:256], 4)), reads=[B2], writes=[U0b[hg]])
                c.dve(lambda e: e.tensor_tensor(Hs[:, hs, :], H[:, hs, :], bc(GAM[:, 8 + hg * 4:12 + hg * 4].unsqueeze(2), [64, 4, 64]), ALU.mult), reads=[Hb[hg], GAM], writes=[Hsb[hg]])
                for q in range(4):
                    h = hg * 4 + q
                    c.pe(lambda e: e.matmul(B3[:, q * 64:(q + 1) * 64], WT[:, h, :], Hs[:, h, :], start=True, stop=True), reads=[WTb[hg], Hsb[hg]], writes=[B3])
                yield
                c.dve(lambda e: e.tensor_tensor(U[:, hs, :], U0[:, hs, :], q4(B3[:, 0:256], 4), ALU.add), reads=[U0b[hg], B3], writes=[Ub[hg]])
                for q in range(4):
                    h = hg * 4 + q
                    o = B0[0:64, q * 128:(q + 1) * 128]
                    c.pe(lambda e: e.matmul(o, Hs[:, h, :], RAT[:, h, 1, :], start=True, stop=False), reads=[Hsb[hg], RAT], writes=[B0])
                    c.pe(lambda e: e.matmul(o, U[:, h, :], PAs[:, h, 1, :], start=False, stop=False), reads=[Ub[hg], PAb[hg]], writes=[B0])
                    c.pe(lambda e: e.matmul(o, V[:, h * 64:(h + 1) * 64], AKs[:, h, 1, :], start=False, stop=True), reads=[V, AKb[hg]], writes=[B0])
                for q in range(4):
                    h = hg * 4 + q
                    o = B1[0:64, q * 64:(q + 1) * 64]
                    c.pe(lambda e: e.matmul(o, bh[:, h * 64:(h + 1) * 64], U[:, h, :], start=True, stop=False), reads=[bh, Ub[hg]], writes=[B1])
                    c.pe(lambda e: e.matmul(o, kh[:, h * 64:(h + 1) * 64], V[:, h * 64:(h + 1) * 64], start=False, stop=True), reads=[kh, V], writes=[B1])
                yield
                c.act(lambda e: e.copy(YS[:, hs, :], q4(B0[0:64, :], 4)), reads=[B0], writes=[YSb[hg]])
                c.dve(lambda e: e.tensor_tensor(H[:, hs, :], H[:, hs, :], bc(GAM[:, hs].unsqueeze(2), [64, 4, 64]), ALU.mult), reads=[Hb[hg], GAM], writes=[Hb[hg]])
                c.dve(lambda e: e.tensor_tensor(H[:, hs, :], H[:, hs, :], q4(B1[0:64, 0:256], 4), ALU.add), reads=[Hb[hg], B1], writes=[Hb[hg]])
                yield
            s.run_chains([hg_chain(hg, s.P[4 * hg:4 * hg + 4], W[hg]) for hg in range(2)])
            c.dma("pool", yT[:, :, tsl], YS[:], reads=YSb, acc=[yT])

    def alloc_rk_ws(s):
        c = s.c
        R_ = {}
        for k in ("Ep", "En", "Ex", "Et", "B", "rt", "kt", "bt", "at", "kh", "bh"):
            R_[k] = c.sb([128, 512], name="r_" + k)
        R_["ld"] = [{k: c.sb([128, 512], name=f"rld{i}_{k}") for k in ("r", "kp", "v", "kk", "eta", "lw")} for i in range(2)]
        R_["RAT"] = c.sb([64, 8, 2, 128], name="r_RAT")
        R_["BT"] = c.sb([64, 8, 128], name="r_BT")
        R_["KT2"] = c.sb([64, 8, 128], name="r_KT2")
        R_["PAs"] = c.sb([128, 8, 2, 128], name="r_PAs")
        R_["AKs"] = c.sb([128, 8, 2, 128], name="r_AKs")
        R_["NNs"] = c.sb([128, 8, 128], name="r_NNs")
        R_["WT"] = c.sb([64, 8, 128], name="r_WT")
        R_["YS"] = c.sb([64, 8, 128], name="r_YS")
        for k in ("AV", "U0", "U"):
            R_[k] = c.sb([128, 8, 64], name="r_" + k)
        R_["H"] = c.sb([64, 8, 64], name="r_H")
        R_["GAM"] = c.sb([64, 16], name="r_GAM")
        R_["Hs"] = c.sb([64, 8, 64], name="r_Hs")
        R_["bufs"] = {k: [Buf(f"{k}_{hg}") for hg in range(2)] for k in ("PAs", "AKs", "NNs", "WT", "AV", "U0", "U", "YS", "H", "Hs")}
        return R_

    def load_w_bf16(s, src_ap, srcbuf, P, nk, name):
        c = s.c
        Wt = c.sb([P, nk, 1024], BF16, name=name)
        if not hasattr(s, "_wst") or s._wst is None:
            s._wst = c.sb([128, 1024], F32, name=name + "_st")
        st = s._wst
        v = src_ap.rearrange("(k p) n -> p k n", p=P)
        for k in range(nk):
            c.dma("sp", st[:P], v[:, k, :], reads=[srcbuf], writes=[st])
            c.dve(lambda e: e.tensor_copy(Wt[:, k, :], st[:P]), reads=[st, Wt], writes=[Wt])
        return Wt

    def post(s, layer, prep, wparts, resid, pT, out_dst, hT_dst):
        c = s.c
        L = s.L
        TB = s.TBp
        NB = L // TB
        NTB = TB // 128
        I = s.inp
        ident = s.K["ident"]
        s._wst = None
        w_out = I(f"w_out{layer}", [1024, 1024]); ple_w = I(f"ple_w{layer}", [256, 1024]); ple_g = I(f"ple_gate{layer}", [1024, 1024])
        vec = I(f"vecs{layer}", [3, 1024])
        LG = s.load_bc(vec.t[0:1, :], 1024, vec, f"LG{layer}")
        LB = s.load_bc(vec.t[1:2, :], 1024, vec, f"LB{layer}")
        PN = s.load_bc(vec.t[2:3, :], 1024, vec, f"PN{layer}")
        Wp = []
        for (r0, r1, P) in wparts:
            Wp.append(s.load_w_bf16(w_out.t[r0:r1, :], w_out, P, (r1 - r0) // P, f"wout{layer}_{r0}"))
        PW = s.load_w_bf16(ple_w.t, ple_w, 128, 2, f"plew{layer}")
        PG = s.load_w_bf16(ple_g.t, ple_g, 128, 8, f"pleg{layer}")
        Pst = c.sb([128, 2, TB], F32, name="post_pst"); Pb = c.sb([128, 2, TB], BF16, name="post_pb")
        XT = c.sb([128, 1024], name="post_xt"); PRE = c.sb([128, 1024], name="post_pre"); JK = c.sb([128, 1024], name="post_jk")
        HN = c.sb([128, 1024], name="post_hn"); E = c.sb([128, 1024], name="post_e"); SG = c.sb([128, 1024], name="post_sg")
        HNT = c.sb([128, 8, 128], BF16, name="post_hnt"); OT = c.sb([128, 8, 128], F32, name="post_ot")
        ST = c.sb([128, 8], name="post_st")
        pv = pT.t[layer]
        pvv = pv.rearrange("(k p) l -> p k l", p=128)
        b0, b1, b2, b3, b4, b5, b6, b7 = s.P
        for b in range(NB):
            t0 = b * TB
            entries = prep(b)
            c.dma("sp", Pst[:], pvv[:, :, t0:t0 + TB], reads=[pT], writes=[Pst])
            c.act(lambda e: e.copy(Pb[:], Pst[:]), reads=[Pst, Pb], writes=[Pb])
            for tt in range(NTB):
                tok = slice(t0 + tt * 128, t0 + (tt + 1) * 128)
                tl = slice(tt * 128, (tt + 1) * 128)
                c.dma("sp", XT[:], resid[tok, :], reads=[resid], writes=[XT])
                for half in range(2):
                    bank = (b0, b1)[half]
                    ne = len(entries)
                    for ei, (mt, idx, P, wi, wk) in enumerate(entries):
                        c.pe(lambda e: e.matmul(bank[:, :], mt[0:P, idx, tl], Wp[wi][0:P, wk, half * 512:(half + 1) * 512], start=(ei == 0), stop=(ei == ne - 1)),
                             reads=[mt, Wp[wi]], writes=[bank])
                    c.dve(lambda e: e.scalar_tensor_tensor(PRE[:, half * 512:(half + 1) * 512], XT[:, half * 512:(half + 1) * 512], ALPHA, bank[:, :], ALU.mult, ALU.add),
                          reads=[XT, bank, PRE], writes=[PRE])
                c.dve(lambda e: e.reduce_sum(ST[:, 0:1], PRE[:], axis=AX.X), reads=[PRE, ST], writes=[ST])
                c.dve(lambda e: e.tensor_scalar(ST[:, 1:2], ST[:, 0:1], -1.0 / 1024, None, ALU.mult), reads=[ST], writes=[ST])
                c.dve(lambda e: e.tensor_scalar(PRE[:], PRE[:], ST[:, 1:2], None, ALU.add), reads=[PRE, ST], writes=[PRE])
                c.act(lambda e: e.activation(out=JK[:], in_=PRE[:], func=AF.Square, accum_out=ST[:, 2:3]), reads=[PRE, JK, ST], writes=[JK, ST])
                c.dve(lambda e: e.tensor_scalar(ST[:, 3:4], ST[:, 2:3], 1.0 / 1024, LN_EPS, ALU.mult, ALU.add), reads=[ST], writes=[ST])
                c.act(lambda e: e.activation(out=ST[:, 3:4], in_=ST[:, 3:4], func=AF.Sqrt), reads=[ST], writes=[ST])
                c.dve(lambda e: e.reciprocal(ST[:, 3:4], ST[:, 3:4]), reads=[ST], writes=[ST])
                c.dve(lambda e: e.scalar_tensor_tensor(HN[:], PRE[:], ST[:, 3:4], LG[:], ALU.mult, ALU.mult), reads=[PRE, ST, LG, HN], writes=[HN])
                c.dve(lambda e: e.tensor_tensor(HN[:], HN[:], LB[:], ALU.add), reads=[HN, LB], writes=[HN])
                for half in range(2):
                    bank = (b2, b3)[half]
                    for kk in range(2):
                        c.pe(lambda e: e.matmul(bank[:, :], Pb[:, kk, tl], PW[:, kk, half * 512:(half + 1) * 512], start=(kk == 0), stop=(kk == 1)),
                             reads=[Pb, PW], writes=[bank])
                    c.act(lambda e: e.activation(out=JK[:, half * 512:(half + 1) * 512], in_=bank[:, :], func=AF.Square, accum_out=ST[:, 4 + half:5 + half]),
                          reads=[bank, JK, ST], writes=[JK, ST])
                c.dve(lambda e: e.tensor_tensor(ST[:, 6:7], ST[:, 4:5], ST[:, 5:6], ALU.add), reads=[ST], writes=[ST])
                c.dve(lambda e: e.tensor_scalar(ST[:, 6:7], ST[:, 6:7], 1.0 / 1024, RMS_EPS, ALU.mult, ALU.add), reads=[ST], writes=[ST])
                c.act(lambda e: e.activation(out=ST[:, 6:7], in_=ST[:, 6:7], func=AF.Sqrt), reads=[ST], writes=[ST])
                c.dve(lambda e: e.reciprocal(ST[:, 6:7], ST[:, 6:7]), reads=[ST], writes=[ST])
                for half in range(2):
                    bank = (b2, b3)[half]
                    hsl = slice(half * 512, (half + 1) * 512)
                    c.dve(lambda e: e.scalar_tensor_tensor(E[:, hsl], bank[:, :], ST[:, 6:7], PN[:, hsl], ALU.mult, ALU.mult), reads=[bank, ST, PN, E], writes=[E])
                for k in range(8):
                    bank = (b4, b5)[k // 4]
                    c.pe(lambda e: e.transpose(bank[:, (k % 4) * 128:(k % 4 + 1) * 128], HN[:, k * 128:(k + 1) * 128], ident[:]), reads=[HN, ident], writes=[bank])
                c.act(lambda e: e.copy(HNT[:, 0:4, :], b4[:, :].rearrange("p (k c) -> p k c", k=4)), reads=[b4, HNT], writes=[HNT])
                c.dve(lambda e: e.tensor_copy(HNT[:, 4:8, :], b5[:, :].rearrange("p (k c) -> p k c", k=4)), reads=[b5, HNT], writes=[HNT])
                for half in range(2):
                    bank = (b6, b7)[half]
                    for k in range(8):
                        c.pe(lambda e: e.matmul(bank[:, :], HNT[:, k, :], PG[:, k, half * 512:(half + 1) * 512], start=(k == 0), stop=(k == 7)), reads=[HNT, PG], writes=[bank])
                    hsl = slice(half * 512, (half + 1) * 512)
                    c.act(lambda e: e.activation(out=SG[:, hsl], in_=bank[:, :], func=AF.Sigmoid), reads=[bank, SG], writes=[SG])
                c.dve(lambda e: e.tensor_tensor(SG[:], SG[:], E[:], ALU.mult), reads=[SG, E], writes=[SG])
                c.dve(lambda e: e.tensor_tensor(SG[:], SG[:], HN[:], ALU.add), reads=[SG, HN], writes=[SG])
                c.dma("pool", out_dst[tok, :], SG[:], reads=[SG], acc=[out_dst])
                if hT_dst is not None:
                    for k in range(8):
                        bank = (b4, b5)[k // 4]
                        c.pe(lambda e: e.transpose(bank[:, (k % 4) * 128:(k % 4 + 1) * 128], SG[:, k * 128:(k + 1) * 128], ident[:]), reads=[SG, ident], writes=[bank])
                    c.act(lambda e: e.copy(OT[:, 0:4, :], b4[:, :].rearrange("p (k c) -> p k c", k=4)), reads=[b4, OT], writes=[OT])
                    c.dve(lambda e: e.tensor_copy(OT[:, 4:8, :], b5[:, :].rearrange("p (k c) -> p k c", k=4)), reads=[b5, OT], writes=[OT])
                    c.dma("pool", hT_dst.t.rearrange("(k p) l -> p k l", p=128)[:, :, tok], OT[:], reads=[OT], acc=[hT_dst])

    def l0_epilogue_setup(s, projT, oT, yT, bonusT):
        c = s.c
        TB = s.TBp
        I = s.inp
        E_ = {}
        E_["dnn"] = s.load_const("dn_norm_col", [128, 1])
        E_["lnc"] = s.load_const("rk_ln64", [64, 2, 8])
        for k in ("OF", "OB", "G"):
            E_[k] = c.sb([128, 4, TB], name="ep_" + k)
        for k in ("YF", "YB", "RG", "BO", "SQ"):
            E_[k] = c.sb([64, 8, TB], name="ep_" + k)
        E_["MDN"] = c.sb([128, 4, TB], BF16, name="ep_MDN")
        E_["MRK"] = c.sb([64, 8, TB], BF16, name="ep_MRK")
        ones = s.K["ones"]
        gdn = projT.t[1552:2064, :].rearrange("(h p) l -> p h l", p=128)
        grk = projT.t[3792:4304, :].rearrange("(h p) l -> p h l", p=64)
        bov = bonusT.t.rearrange("(e p) c l -> p c e l", e=2)

        def prep(b):
            t0 = b * TB
            ts_ = slice(t0, t0 + TB)
            OF, OB, G, YF, YB, RG, BO, SQ, MDN, MRK = (E_[k] for k in ("OF", "OB", "G", "YF", "YB", "RG", "BO", "SQ", "MDN", "MRK"))
            c.dma("sp", OF[:], oT[0][:, :, ts_], reads=[oT[0]], writes=[OF])
            c.dma("sp", OB[:], oT[1][:, :, ts_], reads=[oT[1]], writes=[OB])
            c.dma("sp", G[:], gdn[:, :, ts_], reads=[projT], writes=[G])
            c.dve(lambda e: e.tensor_tensor(OF[:], OF[:], OB[:], ALU.add), reads=[OF, OB], writes=[OF])
            c.act(lambda e: e.activation(out=OB[:], in_=OF[:], func=AF.Square), reads=[OF, OB], writes=[OB])
            for h in range(4):
                c.pe(lambda e: e.matmul(s.P[h][:, 0:TB], ones[:], OB[:, h, :], start=True, stop=True), reads=[OB, ones], writes=[s.P[h]])
            for h in range(4):
                c.dve(lambda e: e.tensor_scalar(OB[:, h, :], s.P[h][:, 0:TB], 1.0 / 128, RMS_EPS, ALU.mult, ALU.add), reads=[s.P[h], OB], writes=[OB])
            c.act(lambda e: e.activation(out=OB[:], in_=OB[:], func=AF.Sqrt), reads=[OB], writes=[OB])
            c.dve(lambda e: e.reciprocal(OB[:], OB[:]), reads=[OB], writes=[OB])
            c.dve(lambda e: e.scalar_tensor_tensor(OF[:], OF[:], E_["dnn"][:, 0:1], OB[:], ALU.mult, ALU.mult), reads=[OF, OB, E_["dnn"]], writes=[OF])
            c.act(lambda e: e.activation(out=G[:], in_=G[:], func=AF.Silu), reads=[G], writes=[G])
            c.dve(lambda e: e.tensor_tensor(MDN[:], OF[:], G[:], ALU.mult), reads=[OF, G, MDN], writes=[MDN])
            c.dma("sp", YF[:], yT[0][:, :, ts_], reads=[yT[0]], writes=[YF])
            c.dma("sp", YB[:], yT[1][:, :, ts_], reads=[yT[1]], writes=[YB])
            c.dma("sp", RG[:], grk[:, :, ts_], reads=[projT], writes=[RG])
            for cc in range(4):
                c.dma("sp", BO[:, 2 * cc:2 * cc + 2, :], bov[:, cc, :, ts_], reads=[bonusT], writes=[BO])
            c.dve(lambda e: e.tensor_tensor(YF[:], YF[:], YB[:], ALU.add), reads=[YF, YB], writes=[YF])
            for hg in range(2):
                for q in range(4):
                    h = hg * 4 + q
                    c.pe(lambda e: e.matmul(s.P[4 + q][0:64, 0:TB], ones[0:64, 0:64], YF[:, h, :], start=True, stop=True), reads=[YF, ones], writes=[s.P[4 + q]])
                for q in range(4):
                    h = hg * 4 + q
                    c.dve(lambda e: e.scalar_tensor_tensor(YF[:, h, :], s.P[4 + q][0:64, 0:TB], -1.0 / 64, YF[:, h, :], ALU.mult, ALU.add), reads=[s.P[4 + q], YF], writes=[YF])
            c.act(lambda e: e.activation(out=SQ[:], in_=YF[:], func=AF.Square), reads=[YF, SQ], writes=[SQ])
            for hg in range(2):
                for q in range(4):
                    h = hg * 4 + q
                    c.pe(lambda e: e.matmul(s.P[4 + q][0:64, 0:TB], ones[0:64, 0:64], SQ[:, h, :], start=True, stop=True), reads=[SQ, ones], writes=[s.P[4 + q]])
                for q in range(4):
                    h = hg * 4 + q
                    c.dve(lambda e: e.tensor_scalar(YB[:, h, :], s.P[4 + q][0:64, 0:TB], 1.0 / 64, GN_EPS, ALU.mult, ALU.add), reads=[s.P[4 + q], YB], writes=[YB])
            c.act(lambda e: e.activation(out=YB[:], in_=YB[:], func=AF.Sqrt), reads=[YB], writes=[YB])
            c.dve(lambda e: e.reciprocal(YB[:], YB[:]), reads=[YB], writes=[YB])
            c.dve(lambda e: e.tensor_tensor(YF[:], YF[:], YB[:], ALU.mult), reads=[YF, YB], writes=[YF])
            lnc = E_["lnc"]
            c.dve(lambda e: e.tensor_tensor(YF[:], YF[:], bc(lnc[:, 0, :].unsqueeze(2), [64, 8, TB]), ALU.mult), reads=[YF, lnc], writes=[YF])
            c.dve(lambda e: e.tensor_tensor(YF[:], YF[:], bc(lnc[:, 1, :].unsqueeze(2), [64, 8, TB]), ALU.add), reads=[YF, lnc], writes=[YF])
            c.dve(lambda e: e.tensor_tensor(YF[:], YF[:], BO[:], ALU.add), reads=[YF, BO], writes=[YF])
            c.act(lambda e: e.activation(out=RG[:], in_=RG[:], func=AF.Silu), reads=[RG], writes=[RG])
            c.dve(lambda e: e.tensor_tensor(MRK[:], YF[:], RG[:], ALU.mult), reads=[YF, RG, MRK], writes=[MRK])
            return [(MDN, k, 128, 0, k) for k in range(4)] + [(MRK, h, 64, 1, h) for h in range(8)]
        return prep

    def fft_fwd(s, U, ubufs, Kp, G, F):
        c = s.c
        N1 = s.N1
        cpb = min(G, 512 // (2 * N1))
        nb = G // cpb
        A_b = [s.P[i] for i in range(nb)]
        X_b = [s.P[2 + i] for i in range(nb)]
        w = 2 * N1
        for g in range(G):
            bank = A_b[g // cpb]
            c.pe(lambda e: e.matmul(bank[:, (g % cpb) * w:(g % cpb + 1) * w], U[:, g, :], F["F1"][0:Kp, :], start=True, stop=True), reads=ubufs + [F["F1"]], writes=[bank])
        T1, T2, RH1, RH2 = F["T1"], F["T2"], F["RH1"], F["RH2"]
        tw = F["TW"]
        for bi in range(nb):
            gs = slice(bi * cpb, (bi + 1) * cpb)
            Av = A_b[bi][:, 0:cpb * w].rearrange("p (g a k) -> p g a k", g=cpb, a=2)
            c.dve(lambda e: e.tensor_tensor(T1[:, gs], Av, bc(tw[:, 0:1, :].unsqueeze(1), [128, cpb, 2, N1]), ALU.mult), reads=[A_b[bi], tw, T1], writes=[T1])
            c.dve(lambda e: e.tensor_tensor(T2[:, gs], Av, bc(tw[:, 1:2, :].unsqueeze(1), [128, cpb, 2, N1]), ALU.mult), reads=[A_b[bi], tw, T2], writes=[T2])
        c.dve(lambda e: e.tensor_tensor(RH1[:, :, 0, :], T1[:, :, 0, :], T2[:, :, 1, :], ALU.subtract), reads=[T1, T2, RH1], writes=[RH1])
        c.dve(lambda e: e.tensor_tensor(RH1[:, :, 1, :], T1[:, :, 1, :], T2[:, :, 0, :], ALU.add), reads=[T1, T2, RH1], writes=[RH1])
        c.act(lambda e: e.mul(RH2[:, :, 0, :], RH1[:, :, 1, :], -1.0), reads=[RH1, RH2], writes=[RH2])
        c.act(lambda e: e.copy(RH2[:, :, 1, :], RH1[:, :, 0, :]), reads=[RH1, RH2], writes=[RH2])
        for bi in range(nb):
            gs = slice(bi * cpb, (bi + 1) * cpb)
            c.pe(lambda e: e.matmul(X_b[bi][:, 0:cpb * w], F["F2re"][:], RH1[:, gs].rearrange("p g a k -> p (g a k)"), start=True, stop=False), reads=[RH1, F["F2re"]], writes=[X_b[bi]])
            c.pe(lambda e: e.matmul(X_b[bi][:, 0:cpb * w], F["F2im"][:], RH2[:, gs].rearrange("p g a k -> p (g a k)"), start=False, stop=True), reads=[RH2, F["F2im"]], writes=[X_b[bi]])
        return X_b, cpb, nb

    def fft_conv(s, U, ubufs, Hs, G, F, out_bank):
        c = s.c
        N1 = s.N1
        Kp = N1 // 2
        w = 2 * N1
        X_b, cpb, nb = s.fft_fwd(U, ubufs, Kp, G, F)
        T1, T2, Y = F["T1"], F["T2"], F["Y"]
        for bi in range(nb):
            gs = slice(bi * cpb, (bi + 1) * cpb)
            Xv = X_b[bi][:, 0:cpb * w].rearrange("p (g a k) -> p g a k", g=cpb, a=2)
            c.dve(lambda e: e.tensor_tensor(T1[:, gs], Xv, bc(Hs[:, gs, 0:1, :], [128, cpb, 2, N1]), ALU.mult), reads=[X_b[bi], Hs, T1], writes=[T1])
            c.dve(lambda e: e.tensor_tensor(T2[:, gs], Xv, bc(Hs[:, gs, 1:2, :], [128, cpb, 2, N1]), ALU.mult), reads=[X_b[bi], Hs, T2], writes=[T2])
        c.dve(lambda e: e.tensor_tensor(Y[:, :, 0, :], T1[:, :, 0, :], T2[:, :, 1, :], ALU.subtract), reads=[T1, T2, Y], writes=[Y])
        c.dve(lambda e: e.tensor_tensor(Y[:, :, 1, :], T1[:, :, 1, :], T2[:, :, 0, :], ALU.add), reads=[T1, T2, Y], writes=[Y])
        B_b = [s.P[4 + i] for i in range(G // 2)]
        for g in range(G):
            bank = B_b[g // 2]
            o = bank[0:N1, (g % 2) * 256:(g % 2 + 1) * 256]
            c.pe(lambda e: e.matmul(o, Y[:, g, 0, :], F["CF1"][:], start=True, stop=False), reads=[Y, F["CF1"]], writes=[bank])
            c.pe(lambda e: e.matmul(o, Y[:, g, 1, :], F["CF2"][:], start=False, stop=True), reads=[Y, F["CF2"]], writes=[bank])
        S1, S2, BR, BI = F["S1"], F["S2"], F["BR"], F["BI"]
        twt = F["TWT"]
        for bi in range(G // 2):
            gs = slice(bi * 2, bi * 2 + 2)
            Bv = B_b[bi][0:N1, :].rearrange("p (g a n) -> p g a n", g=2, a=2)
            c.dve(lambda e: e.tensor_tensor(S1[:, gs], Bv, bc(twt[:, 0:1, :].unsqueeze(1), [N1, 2, 2, 128]), ALU.mult), reads=[B_b[bi], twt, S1], writes=[S1])
            c.dve(lambda e: e.tensor_tensor(S2[:, gs], Bv, bc(twt[:, 1:2, :].unsqueeze(1), [N1, 2, 2, 128]), ALU.mult), reads=[B_b[bi], twt, S2], writes=[S2])
        c.dve(lambda e: e.tensor_tensor(BR[:], S1[:, :, 0, :], S2[:, :, 1, :], ALU.add), reads=[S1, S2, BR], writes=[BR])
        c.dve(lambda e: e.tensor_tensor(BI[:], S1[:, :, 1, :], S2[:, :, 0, :], ALU.subtract), reads=[S1, S2, BI], writes=[BI])
        o = out_bank[0:Kp, 0:G * 128]
        c.pe(lambda e: e.matmul(o, F["IF1re"][:], BR[:].rearrange("p g n -> p (g n)"), start=True, stop=False), reads=[BR, F["IF1re"]], writes=[out_bank])
        c.pe(lambda e: e.matmul(o, F["IF1imn"][:], BI[:].rearrange("p g n -> p (g n)"), start=False, stop=True), reads=[BI, F["IF1imn"]], writes=[out_bank])

    def layer1(s, hT, hres, pT, stop):
        c = s.c
        L, TB, NB = s.L, s.TB, s.NB
        I = s.inp
        N1 = 2 * L // 128
        s.N1 = N1
        Kp = N1 // 2
        G = 4
        w_in1 = I("w_in1", [1024, 4096])
        projT = s.scratch("projT1", [4096, L])
        s.proj_fm(hT, w_in1, 4096, projT)
        xvT = s.scratch("hy_xvT", [3072, L])
        c.push()
        TB = min(256, L)
        NB = L // TB
        CW = s.load_const("hy_cw", [128, 24, 4])
        pv = projT.t[0:3072, :].rearrange("(t p) l -> p t l", p=128)
        xv = xvT.t.rearrange("(t p) l -> p t l", p=128)
        X = c.sb([128, 24, TB + 2], name="hyX"); A = c.sb([128, 24, TB], name="hyA"); A2 = c.sb([128, 24, TB], name="hyA2")
        for b in range(NB):
            t0 = b * TB
            lo = max(t0 - 1, 0); hi = min(t0 + TB + 1, L)
            if lo != t0 - 1 or hi != t0 + TB + 1:
                c.dve(lambda e: e.memset(X[:], 0.0), writes=[X])
            c.dma("sp", X[:, :, lo - (t0 - 1):hi - (t0 - 1)], pv[:, :, lo:hi], reads=[projT], writes=[X])
            c.dve(lambda e: e.tensor_tensor(A[:], X[:, :, 0:TB], bc(CW[:, :, 0:1], [128, 24, TB]), ALU.mult), reads=[X, CW, A], writes=[A])
            c.dve(lambda e: e.tensor_tensor(A2[:], X[:, :, 1:TB + 1], bc(CW[:, :, 1:2], [128, 24, TB]), ALU.mult), reads=[X, CW, A2], writes=[A2])
            c.dve(lambda e: e.tensor_tensor(A[:], A[:], A2[:], ALU.add), reads=[A, A2], writes=[A])
            c.dve(lambda e: e.tensor_tensor(A2[:], X[:, :, 2:TB + 2], bc(CW[:, :, 2:3], [128, 24, TB]), ALU.mult), reads=[X, CW, A2], writes=[A2])
            c.dve(lambda e: e.tensor_tensor(A[:], A[:], A2[:], ALU.add), reads=[A, A2], writes=[A])
            c.dve(lambda e: e.tensor_tensor(A[:], A[:], bc(CW[:, :, 3:4], [128, 24, TB]), ALU.add), reads=[A, CW], writes=[A])
            c.dma("pool", xv[:, :, t0:t0 + TB], A[:], reads=[A], acc=[xvT])
        c.pop()
        filt = s.scratch("hy_filt", [2, 1024, 2 * L])
        c.push()
        CH = min(512, L)
        NCH = L // CH
        FTd = I("hy_FT", [33, 2, L])
        W1 = s.load_const("hy_w1", [33, 64]); W2 = s.load_const("hy_w2", [64, 64]); W3 = s.load_const("hy_w3", [64, 64])
        BF = s.load_const("hy_bf", [64, 4])
        WO = s.load_const("hy_wo", [64, 4096])
        SK = s.load_const("hy_skipc", [128, 2, 8])
        ndd = I("hy_deltas", [1, 4096])
        ND = c.sb([65, 4096], name="hyND")
        c.dve(lambda e: e.memset(ND[:], 0.0), writes=[ND])
        c.dma("sp", ND[64:65, :], ndd.t, reads=[ndd], writes=[ND])
        c.act(lambda e: e.activation(out=ND[64:65, :], in_=ND[64:65, :], func=AF.Abs), reads=[ND], writes=[ND])
        c.act(lambda e: e.mul(ND[64:65, :], ND[64:65, :], -1.0), reads=[ND], writes=[ND])
        FB = c.sb([64, 3], name="hyFB")
        c.dve(lambda e: e.tensor_tensor(FB[:], BF[:, 0:3], bc(BF[:, 3:4], [64, 3]), ALU.mult), reads=[BF, FB], writes=[FB])
        HD = c.sb([65, 2, L], name="hyHD")
        FTc = c.sb([33, CH], name="hyFT"); Z = c.sb([64, CH], name="hyZ"); Mk = c.sb([64, CH], name="hyM")
        TWO_PI = 2 * math.pi
        for dirn in range(2):
            c.dma("sp", HD[64:65, dirn, :], FTd.t[0:1, dirn, :], reads=[FTd], writes=[HD])
            for ch in range(NCH):
                cs = slice(ch * CH, (ch + 1) * CH)
                c.dma("sp", FTc[:], FTd.t[:, dirn, cs], reads=[FTd], writes=[FTc])
                src, sbuf, Wl = FTc[:], FTc, [W1, W2, W3]
                for li in range(3):
                    bank = s.P[li]
                    kk = 33 if li == 0 else 64
                    c.pe(lambda e: e.matmul(bank[0:64, 0:CH], Wl[li][0:kk, :], src, start=True, stop=True), reads=[sbuf, Wl[li]], writes=[bank])
                    c.dve(lambda e: e.tensor_scalar(Z[:], bank[0:64, 0:CH], BF[:, 3:4], FB[:, li:li + 1], ALU.mult, ALU.add), reads=[bank, BF, FB, Z], writes=[Z])
                    for _ in range(2):
                        c.dve(lambda e: e.tensor_scalar(Mk[:], Z[:], math.pi, -TWO_PI, ALU.is_gt, ALU.mult), reads=[Z, Mk], writes=[Mk])
                        c.dve(lambda e: e.tensor_tensor(Z[:], Z[:], Mk[:], ALU.add), reads=[Z, Mk], writes=[Z])
                        c.dve(lambda e: e.tensor_scalar(Mk[:], Z[:], -math.pi, TWO_PI, ALU.is_lt, ALU.mult), reads=[Z, Mk], writes=[Mk])
                        c.dve(lambda e: e.tensor_tensor(Z[:], Z[:], Mk[:], ALU.add), reads=[Z, Mk], writes=[Z])
                    dst = Z[:] if li < 2 else HD[0:64, dirn, cs]
                    dbuf = Z if li < 2 else HD
                    c.act(lambda e: e.activation(out=dst, in_=Z[:], func=AF.Sin), reads=[Z, dbuf], writes=[dbuf])
                    src, sbuf = Z[:], Z
        ACC = c.sb([128, 32, NCH], name="hyACC"); NRM = c.sb([128, 16], name="hyNRM")
        EW = c.sb([128, CH], name="hyEW"); FV = c.sb([128, CH], name="hyFV"); JK = c.sb([128, CH], name="hyJK")
        c.dve(lambda e: e.memset(ACC[:], 0.0), writes=[ACC])
        for pas in range(2):
            for o in range(2):
                for dirn in range(2):
                    for ct in range(8):
                        col = o * 16 + dirn * 8 + ct
                        csl = slice(col * 128, (col + 1) * 128)
                        for ch in range(NCH):
                            cs = slice(ch * CH, (ch + 1) * CH)
                            bk1, bk2 = s.P[4 + (ch % 2) * 2], s.P[5 + (ch % 2) * 2]
                            c.pe(lambda e: e.matmul(bk1[:, 0:CH], WO[:, csl], HD[0:64, dirn, cs], start=True, stop=True), reads=[WO, HD], writes=[bk1])
                            c.pe(lambda e: e.matmul(bk2[:, 0:CH], ND[:, csl], HD[0:65, dirn, cs], start=True, stop=True), reads=[ND, HD], writes=[bk2])
                            c.act(lambda e: e.activation(out=EW[:], in_=bk2[:, 0:CH], func=AF.Exp), reads=[bk2, EW], writes=[EW])
                            c.dve(lambda e: e.tensor_tensor(FV[:], bk1[:, 0:CH], EW[:], ALU.mult), reads=[bk1, EW, FV], writes=[FV])
                            if dirn == 1 and ch == 0:
                                c.dve(lambda e: e.memset(FV[:, 0:1], 0.0), reads=[FV], writes=[FV])
                            if pas == 0:
                                c.act(lambda e: e.activation(out=JK[:], in_=FV[:], func=AF.Abs, accum_out=ACC[:, col, ch:ch + 1]), reads=[FV, JK, ACC], writes=[JK, ACC])
                            else:
                                c.dve(lambda e: e.tensor_scalar(FV[:], FV[:], NRM[:, o * 8 + ct:o * 8 + ct + 1], None, ALU.mult), reads=[FV, NRM], writes=[FV])
                                if dirn == 0 and ch == 0:
                                    c.dve(lambda e: e.tensor_tensor(FV[:, 0:1], FV[:, 0:1], SK[:, o, ct:ct + 1], ALU.add), reads=[FV, SK], writes=[FV])
                                c.dma("pool", filt[o, ct * 128:(ct + 1) * 128, dirn * L + ch * CH:dirn * L + (ch + 1) * CH], FV[:], reads=[FV], acc=[filt])
            if pas == 0:
                RED = c.sb([128, 32], name="hyRED")
                c.dve(lambda e: e.reduce_sum(RED[:], ACC[:], axis=AX.X), reads=[ACC, RED], writes=[RED])
                R4 = RED[:, :].rearrange("p (o d c) -> p o d c", o=2, d=2)
                N4 = NRM[:, :].rearrange("p (o c) -> p o c", o=2)
                c.dve(lambda e: e.tensor_tensor(N4, R4[:, :, 0, :], R4[:, :, 1, :], ALU.add), reads=[RED, NRM], writes=[NRM])
                c.dve(lambda e: e.tensor_scalar_add(NRM[:], NRM[:], RMS_EPS), reads=[NRM], writes=[NRM])
                c.dve(lambda e: e.reciprocal(NRM[:], NRM[:]), reads=[NRM], writes=[NRM])
        c.pop()
        if stop == "F":
            return
        c.push()
        F = {k: s.load_const(k, shp) for k, shp in s.fft_shapes.items()}
        for k in ("T1", "T2", "RH1", "RH2", "Y"):
            F[k] = c.sb([128, G, 2, N1], name="ff_" + k)
        for k in ("S1", "S2"):
            F[k] = c.sb([N1, G, 2, 128], name="ff_" + k)
        F["BR"] = c.sb([N1, G, 128], name="ff_BR"); F["BI"] = c.sb([N1, G, 128], name="ff_BI")
        NG = 1024 // G
        spec = s.scratch("hy_spec", [2, NG, 128, G * 2 * N1])
        mixT = s.scratch("hy_mixT", [1024, L])
        UF = c.sb([N1, G, 128], name="ff_UF"); SP = c.sb([128, G, 2, N1], name="ff_SP")
        fv = filt.t.rearrange("o c (a n) -> o a c n", n=128)
        for o in range(2):
            for gi in range(NG):
                c.dma("sp", UF[:], fv[o, :, gi * G:(gi + 1) * G, :], reads=[filt], writes=[UF])
                X_b, cpb, nb = s.fft_fwd(UF[:], [UF], N1, G, F)
                w = 2 * N1
                for bi in range(nb):
                    gs = slice(bi * cpb, (bi + 1) * cpb)
                    c.act(lambda e: e.copy(SP[:, gs].rearrange("p g a k -> p (g a k)"), X_b[bi][:, 0:cpb * w]), reads=[X_b[bi], SP], writes=[SP])
                c.dma("pool", spec[o, gi], SP[:].rearrange("p g a k -> p (g a k)"), reads=[SP], acc=[spec])
        xg = xvT.t.rearrange("(t c) (a n) -> t a c n", t=3, n=128)
        gt = projT.t[3072:4096, :].rearrange("c (a n) -> a c n", n=128)
        mv = mixT.t.rearrange("c (a n) -> a c n", n=128)
        V = c.sb([Kp, G, 128], name="hy_V"); X1 = c.sb([Kp, G, 128], name="hy_X1"); X2 = c.sb([Kp, G, 128], name="hy_X2"); GT = c.sb([Kp, G, 128], name="hy_GT")
        Zt = c.sb([Kp, G, 128], name="hy_Z"); H0 = c.sb([128, G, 2, N1], name="hy_H0"); H1 = c.sb([128, G, 2, N1], name="hy_H1")
        for gi in range(NG):
            cs = slice(gi * G, (gi + 1) * G)
            c.dma("sp", V[:], xg[2, :, cs, :], reads=[xvT], writes=[V])
            c.dma("sp", X1[:], xg[0, :, cs, :], reads=[xvT], writes=[X1])
            c.dma("sp", X2[:], xg[1, :, cs, :], reads=[xvT], writes=[X2])
            c.dma("sp", GT[:], gt[:, cs, :], reads=[projT], writes=[GT])
            c.dma("sp", H0[:].rearrange("p g a k -> p (g a k)"), spec[0, gi], reads=[spec], writes=[H0])
            c.dma("sp", H1[:].rearrange("p g a k -> p (g a k)"), spec[1, gi], reads=[spec], writes=[H1])
            ob = s.P[6]
            s.fft_conv(V[:], [V], H0, G, F, ob)
            c.dve(lambda e: e.tensor_tensor(Zt[:], X1[:], ob[0:Kp, 0:G * 128].rearrange("p (g n) -> p g n", g=G), ALU.mult), reads=[X1, ob, Zt], writes=[Zt])
            ob2 = s.P[7]
            s.fft_conv(Zt[:], [Zt], H1, G, F, ob2)
            c.act(lambda e: e.activation(out=GT[:], in_=GT[:], func=AF.Silu), reads=[GT], writes=[GT])
            c.dve(lambda e: e.tensor_tensor(X2[:], X2[:], ob2[0:Kp, 0:G * 128].rearrange("p (g n) -> p g n", g=G), ALU.mult), reads=[X2, ob2], writes=[X2])
            c.dve(lambda e: e.tensor_tensor(X2[:], X2[:], GT[:], ALU.mult), reads=[X2, GT], writes=[X2])
            c.dma("pool", mv[:, cs, :], X2[:], reads=[X2], acc=[mixT])
        c.pop()
        if stop == "G":
            return
        c.push()
        TB = s.TBp
        MS = c.sb([128, 8, TB], F32, name="l1_ms"); MB = c.sb([128, 8, TB], BF16, name="l1_mb")
        mxv = mixT.t.rearrange("(k p) l -> p k l", p=128)

        def prep(b):
            c.dma("sp", MS[:], mxv[:, :, b * TB:(b + 1) * TB], reads=[mixT], writes=[MS])
            c.act(lambda e: e.copy(MB[:], MS[:]), reads=[MS, MB], writes=[MB])
            return [(MB, k, 128, 0, k) for k in range(8)]
        s.post(1, prep, [(0, 1024, 128)], hres, pT, s.out, None)
        c.pop()

    def build(s, stop=None):
        c = s.c
        L, NT = s.L, s.NT
        I = s.inp
        xT = I("xT", [1024, L])
        x = I("x", [L, 1024])
        pT = I("pT", [2, 256, L])
        w_in0 = I("w_in0", [1024, EVEN_IN_PAD])
        out = s.c.dram("out", [L, 1024], F32, kind="ExternalOutput")
        s.out = out
        s.psum_banks()
        s.K = {k: s.load_const(k, shp) for k, shp in CONST_SHAPES.items()}
        last0 = (1 not in s.layers)
        s.h1 = out if last0 else s.scratch("h1", [L, 1024])
        s.h1T = None if last0 else s.scratch("h1T", [1024, L])
        if 0 in s.layers:
            s.layer0(xT, x, pT, w_in0, stop)
        if 1 in s.layers:
            N1 = 2 * L // 128
            s.fft_shapes = {"F1": [N1, 2 * N1], "TW": [128, 2, N1], "F2re": [128, 128], "F2im": [128, 128], "CF1": [128, 256], "CF2": [128, 256],
                            "TWT": [N1, 2, 128], "IF1re": [N1, N1 // 2], "IF1imn": [N1, N1 // 2]}
            if 0 in s.layers:
                s.layer1(s.h1T, s.h1, pT, stop)
            else:
                s.layer1(xT, x, pT, stop)
        return s.nc

    def layer0(s, xT, x, pT, w_in0, stop):
        c = s.c
        L, NT = s.L, s.NT
        I = s.inp
        projT = s.scratch("projT", [EVEN_IN_PAD, L])
        dtb_d = I("dn_dt_bias", [1, 8])
        alog_d = I("dn_a_log", [1, 8])
        dtb = s.load_bc(dtb_d.t, 8, dtb_d, "dtb")
        alog = s.load_bc(alog_d.t, 8, alog_d, "alog")
        ABraw = c.sb([128, NT, 16], name="ABraw")
        GB = c.sb([128, NT, 16], name="GB")
        s.GB = GB
        s.proj_fm(xT, w_in0, EVEN_IN_PAD, projT, tm_cols=(1536, 1552), tm_dst=ABraw)
        s.dn_gates(ABraw, GB, dtb, alog)
        if s.debug:
            gbd = s.scratch("gb_dbg", [128, NT, 16])
            c.dma("sp", gbd[:], GB[:], reads=[GB], writes=[gbd])
        if stop == "A1":
            return
        CW = s.load_const("dn_cw", [128, 3, 4, 5])
        qT = s.scratch("dn_qT", [128, 4, L])
        kT = s.scratch("dn_kT", [128, 4, L])
        k_tm = s.scratch("dn_ktm", [L, 4, 128])
        v_tm = s.scratch("dn_vtm", [L, 4, 128])
        s.dn_prep(projT, CW, qT, kT, k_tm, v_tm)
        if stop == "A2":
            return
        rk_tm = s.scratch("rk_tm", [7, L, 512])
        bonusT = s.scratch("rk_bonusT", [128, 4, L])
        s.rk_prep(projT, rk_tm, bonusT)
        if stop == "A4":
            return
        c.push()
        Ws = [s.alloc_chunk_ws() for _ in range(2)]
        oT = [s.scratch(f"dn_oT{d}", [128, 4, L]) for d in range(2)]
        s.run_chains([s.dn_pass(d, qT, kT, k_tm, v_tm, oT[d], Ws[d], s.P[4 * d:4 * d + 4]) for d in range(2)])
        c.pop()
        if stop == "B1":
            return
        c.push()
        W = [{k: c.sb([128, 4, 128], name=f"w2_{k}{hg}") for k in ("R", "NA", "NB", "PA", "PB")} for hg in range(2)]
        R_ = s.alloc_rk_ws()
        yT = [s.scratch(f"rk_yT{d}", [64, 8, L]) for d in range(2)]
        for d in range(2):
            s.rk_pass(d, rk_tm, yT[d], W, R_)
        c.pop()
        if stop == "B2":
            return
        c.push()
        prep = s.l0_epilogue_setup(projT, oT, yT, bonusT)
        s.post(0, prep, [(0, 512, 128), (512, 1024, 64)], x, pT, s.h1, s.h1T)
        c.pop()

    def finish(s):
        c = s.c
        c.barrier()
        c.close()


def host_layout(inp, b, L):
    f = lambda a: np.ascontiguousarray(np.asarray(a, dtype=np.float32))
    m = {}
    xb = np.asarray(inp["x"])[b, :L]
    m["x"] = f(xb)
    m["xT"] = f(xb.T)
    m["pT"] = f(np.asarray(inp["p"])[:, b, :L].transpose(0, 2, 1))
    w = np.asarray(inp["even_w_in"])[0]
    m["w_in0"] = f(np.pad(w, ((0, 0), (0, EVEN_IN_PAD - w.shape[1]))))
    m["dn_dt_bias"] = f(np.asarray(inp["dn_dt_bias"])[0].reshape(1, 8))
    m["dn_a_log"] = f(np.asarray(inp["dn_a_log"])[0].reshape(1, 8))
    dc = np.asarray(inp["dn_conv"])[0]
    m["dn_cw"] = f(dc.reshape(5, 3, 4, 128).transpose(3, 1, 2, 0))
    mu = np.asarray(inp["rk_mu"])[0]
    m["rk_mu_rkv"] = f(mu[:, :1536].reshape(2, 12, 128).transpose(2, 1, 0))
    m["rk_mu_lora"] = f(mu[:, 1536:].reshape(2, 3, 64).transpose(2, 1, 0))
    m["rk_w2"] = f(np.asarray(inp["rk_w2"])[0].transpose(1, 0, 2))
    m["rk_a2"] = f(np.asarray(inp["rk_a2"])[0])
    cols = np.stack([np.asarray(inp[k])[0].reshape(512) for k in ("rk_a0", "rk_k_k", "rk_k_a", "rk_r_k", "rk_ln_w", "rk_ln_b")], 0)
    m["rk_cols"] = f(cols.reshape(6, 4, 128).transpose(2, 0, 1))
    m["rk_w0"] = f(np.asarray(inp["rk_w0"])[0].reshape(1, 1024))
    m["rk_a0"] = f(np.asarray(inp["rk_a0"])[0].reshape(1, 512))
    m["dn_norm_col"] = f(np.asarray(inp["dn_norm"])[0].reshape(128, 1))
    m["rk_ln64"] = f(np.stack([np.asarray(inp["rk_ln_w"])[0].reshape(8, 64).T, np.asarray(inp["rk_ln_b"])[0].reshape(8, 64).T], 1))
    for l in range(2):
        m[f"w_out{l}"] = f(np.asarray(inp["w_out"])[l])
        m[f"ple_w{l}"] = f(np.asarray(inp["ple_w"])[l])
        m[f"ple_gate{l}"] = f(np.asarray(inp["ple_gate"])[l])
        m[f"vecs{l}"] = f(np.stack([np.asarray(inp["ln_g"])[l], np.asarray(inp["ln_b"])[l], np.asarray(inp["ple_norm"])[l]], 0))
    m["w_in1"] = f(np.asarray(inp["odd_w_in"])[0])
    cw = np.asarray(inp["hy_conv_w"])[0]; cb = np.asarray(inp["hy_conv_b"])[0]
    m["hy_cw"] = f(np.concatenate([cw, cb[None]], 0).reshape(4, 24, 128).transpose(2, 1, 0))
    m["hy_w1"] = f(np.asarray(inp["hy_ffn_w1"])[0]); m["hy_w2"] = f(np.asarray(inp["hy_ffn_w2"])[0]); m["hy_w3"] = f(np.asarray(inp["hy_ffn_w3"])[0])
    m["hy_bf"] = f(np.stack([np.asarray(inp[k])[0] for k in ("hy_ffn_b1", "hy_ffn_b2", "hy_ffn_b3", "hy_ffn_freq")], 1))
    m["hy_wo"] = f(np.asarray(inp["hy_ffn_out"])[0])
    m["hy_skipc"] = f(np.asarray(inp["hy_skip"])[0].reshape(2, 8, 128).transpose(2, 0, 1))
    m["hy_deltas"] = f(np.asarray(inp["hy_deltas"])[0].reshape(1, 4096))
    m.update(host_consts(L))
    m.update(hyena_consts(L))
    return m


def hyena_consts(L):
    c = {}
    bands = 16
    t = np.linspace(0.0, 1.0, L, dtype=np.float32).astype(np.float64)
    fr = np.linspace(1e-4, bands - 1, bands, dtype=np.float32).astype(np.float64)
    ang = (np.float32(2.0 * math.pi / L) * np.arange(L, dtype=np.float32)).astype(np.float64)[:, None] * fr[None, :]
    feats = np.concatenate([t[:, None], np.cos(ang), -np.sin(ang)], -1)
    rev = np.concatenate([feats[:1], feats[:0:-1]], 0)
    c["hy_FT"] = np.ascontiguousarray(np.stack([feats.T, rev.T], 1)).astype(np.float32)
    N = 2 * L
    N1 = N // 128
    n1 = np.arange(N1); k1 = np.arange(N1); n2 = np.arange(128); k2 = np.arange(128)
    a1 = 2 * np.pi * np.outer(n1, k1) / N1
    c["F1"] = np.concatenate([np.cos(a1), -np.sin(a1)], 1).astype(np.float32)
    atw = 2 * np.pi * np.outer(n2, k1) / N
    c["TW"] = np.stack([np.cos(atw), -np.sin(atw)], 1).astype(np.float32)
    a2 = 2 * np.pi * np.outer(n2, k2) / 128
    c["F2re"] = np.cos(a2).astype(np.float32); c["F2im"] = (-np.sin(a2)).astype(np.float32)
    c["CF1"] = np.concatenate([np.cos(a2), np.sin(a2)], 1).astype(np.float32)
    c["CF2"] = np.concatenate([-np.sin(a2), np.cos(a2)], 1).astype(np.float32)
    c["TWT"] = np.stack([np.cos(atw).T, -np.sin(atw).T], 1).astype(np.float32)
    c["IF1re"] = (np.cos(a1)[:, :N1 // 2] / N).astype(np.float32)
    c["IF1imn"] = (-np.sin(a1)[:, :N1 // 2] / N).astype(np.float32)
    return c


def run(inp, L, nb, debug=False, stop=None, layers=(0, 1)):
    pr = Prog(L, debug=debug, layers=layers)
    nc = pr.build(stop=stop)
    pr.finish()
    print("ninst", pr.c.ninst, "nwait", pr.c.nwait)
    maps = []
    for b in range(nb):
        m = host_layout(inp, b, L)
        maps.append({k: m[k] for k in pr.inputs})
    res = run_bass_kernel_spmd(nc, maps, core_ids=list(range(nb)))
    return res.results, pr


def kernel(**inputs):
    L = inputs["x"].shape[1]
    res, pr = run(inputs, L, NCORES)
    return np.stack([r["out"] for r in res], 0).astype(np.float32)
```

```python
import contextlib, math
import numpy as np
import concourse.bass as bass
import concourse.mybir as mybir
from concourse.bass_utils import run_bass_kernel_spmd

F32 = mybir.dt.float32
BF16 = mybir.dt.bfloat16
AF = mybir.ActivationFunctionType
ALU = mybir.AluOpType
AX = mybir.AxisListType

D = 1024
NCORES = 8
ALPHA = 4.0 ** 0.25
LN_EPS = 1e-5
RMS_EPS = 1e-6
GN_EPS = 64e-5
EVEN_IN_PAD = 4352


class Buf:
    __slots__ = ("name", "w", "r")

    def __init__(s, name):
        s.name = name
        s.w = {}
        s.r = {}


class Dom:
    def __init__(s, name, sem, inc):
        s.name = name
        s.sem = sem
        s.inc = inc
        s.count = 0


class Stream:
    def __init__(s, name, eng):
        s.name = name
        s.eng = eng
        s.seen = {}


class T:
    def __init__(s, t, buf):
        s.t = t
        s.buf = buf

    def __getitem__(s, k):
        return s.t[k]


class Ctx:
    def __init__(s, nc):
        s.nc = nc
        s.stacks = [contextlib.ExitStack()]
        s.nwait = 0
        s.ninst = 0
        s.uid = 0

        def sem(n):
            return s.stacks[0].enter_context(nc.semaphore(n))

        s.st = {"pe": Stream("pe", nc.tensor), "act": Stream("act", nc.scalar), "dve": Stream("dve", nc.vector),
                "pool": Stream("pool", nc.gpsimd), "sp": Stream("sp", nc.sync)}
        s.dom = {"pe": Dom("pe", sem("s_pe"), 1), "act": Dom("act", sem("s_act"), 1),
                 "dve": Dom("dve", sem("s_dve"), 1), "pool": Dom("pool", sem("s_pool"), 1)}
        s.M = 12
        s.dma_doms = {}
        s.dma_rr = {}
        for q in ("sp", "pool"):
            s.dma_doms[q] = [Dom(f"{q}_dma{i}", sem(f"s_{q}d{i}"), 16) for i in range(s.M)]
            s.dma_rr[q] = 0
            for dm in s.dma_doms[q]:
                s.dom[dm.name] = dm
        s.rr = 0

    def push(s):
        s.stacks.append(contextlib.ExitStack())

    def pop(s):
        s.barrier()
        s.stacks.pop().close()

    def sb(s, shape, dt=F32, name=None):
        s.uid += 1
        name = f"{name or 'sb'}_{s.uid}"
        t = s.stacks[-1].enter_context(s.nc.sbuf_tensor(name, list(shape), dt))
        return T(t, Buf(name))

    def ps(s, shape, dt=F32, name=None):
        s.uid += 1
        name = name or f"ps{s.uid}"
        t = s.stacks[-1].enter_context(s.nc.psum_tensor(name, list(shape), dt))
        return T(t, Buf(name))

    def dram(s, name, shape, dt=F32, kind="Internal"):
        t = s.nc.dram_tensor(name, list(shape), dt, kind=kind)
        return T(t.ap(), Buf(name))

    def _bufs(s, xs):
        out = []
        for x in xs:
            if x is None:
                continue
            out.append(x.buf if isinstance(x, T) else x)
        return out

    def emit(s, stream, dom, fn, reads=(), writes=(), acc=(), pre=None):
        st = s.st[stream]
        dm = s.dom[dom]
        deps = {}
        if pre is not None and pre[1] > 0:
            deps[pre[0]] = pre[1]

        def add(d):
            for k, v in d.items():
                if deps.get(k, 0) < v:
                    deps[k] = v

        for b in s._bufs(reads):
            add(b.w)
        for b in s._bufs(writes):
            add(b.w)
            add(b.r)
        for b in s._bufs(acc):
            add(b.r)
        for k, v in deps.items():
            if k == dom and dm.inc == 1:
                if stream == "pe":
                    continue
                if v <= dm.count - 8:
                    continue
            if st.seen.get(k, 0) >= v:
                continue
            st.eng.wait_ge(s.dom[k].sem, v * s.dom[k].inc)
            st.seen[k] = v
            s.nwait += 1
        ins = fn(st.eng)
        dm.count += 1
        n = dm.count
        ins.then_inc(dm.sem, dm.inc)
        s.ninst += 1
        for b in s._bufs(reads):
            if b.r.get(dom, 0) < n:
                b.r[dom] = n
        for b in s._bufs(writes):
            b.w = {dom: n}
            b.r = {}
        for b in s._bufs(acc):
            b.w[dom] = n
        return ins

    def pe(s, fn, reads=(), writes=(), acc=()):
        return s.emit("pe", "pe", fn, reads, writes, acc)

    def act(s, fn, reads=(), writes=(), acc=()):
        return s.emit("act", "act", fn, reads, writes, acc)

    def dve(s, fn, reads=(), writes=(), acc=()):
        return s.emit("dve", "dve", fn, reads, writes, acc)

    def pool(s, fn, reads=(), writes=(), acc=()):
        return s.emit("pool", "pool", fn, reads, writes, acc)

    def any2(s, fn, reads=(), writes=(), acc=()):
        s.rr += 1
        if s.rr % 3 == 0:
            return s.pool(fn, reads, writes, acc)
        return s.dve(fn, reads, writes, acc)

    def dma(s, q, out, in_, reads=(), writes=(), acc=(), **kw):
        stream = {"sp": "sp", "pool": "pool"}[q]
        dm = s.dma_doms[q][s.dma_rr[q] % s.M]
        s.dma_rr[q] += 1
        return s.emit(stream, dm.name, lambda e: e.dma_start(out=out, in_=in_, **kw), reads, writes, acc, pre=(dm.name, dm.count))

    def barrier(s):
        for st in s.st.values():
            for k, dm in s.dom.items():
                if dm.count > 0 and st.seen.get(k, 0) < dm.count:
                    if k == st.name and dm.inc == 1 and st.name == "pe":
                        continue
                    st.eng.wait_ge(dm.sem, dm.count * dm.inc)
                    st.seen[k] = dm.count
                    s.nwait += 1

    def close(s):
        while s.stacks:
            s.stacks.pop().close()


def bc(ap, shape):
    return ap.to_broadcast(list(shape))


def host_consts(L):
    c = {}
    i = np.arange(128)
    c["ident"] = np.eye(128, dtype=np.float32)
    c["ones"] = np.ones((128, 128), np.float32)
    bd = np.zeros((128, 128), np.float32)
    bd[:64, :64] = 1
    bd[64:, 64:] = 1
    c["bd"] = bd
    J, C = np.meshgrid(i, i, indexing="ij")
    su = (C > J).astype(np.float32)
    sl = (C < J).astype(np.float32)
    iu = (C >= J).astype(np.float32)
    il = (C <= J).astype(np.float32)
    def rep4(m):
        return np.ascontiguousarray(np.broadcast_to(m[:, None, :], (128, 4, 128))).astype(np.float32)
    for d, (tri, sT, s_, iT) in enumerate([(iu, su, sl, iu), (il, sl, su, il)]):
        c[f"tri{d}"] = tri.astype(np.float32)
        c[f"nsT{d}"] = rep4(-sT)
        c[f"ns{d}"] = rep4(-s_)
        c[f"iT{d}"] = rep4(iT)
        mid = 63 if d == 0 else 64
        sel = (tri[:, mid:mid + 1] * np.ones((1, 128))).astype(np.float32)
        c[f"trim{d}"] = (tri - sel).astype(np.float32)
        c[f"suf{d}"] = (sl if d == 0 else su).astype(np.float32)
        m2 = np.stack([sT, iT], 0)
        c[f"mpa{d}"] = np.ascontiguousarray(np.broadcast_to(m2.transpose(1, 0, 2)[:, None], (128, 4, 2, 128))).astype(np.float32)
        c[f"ps{d}"] = rep4(s_)
    return c


CONST_SHAPES = {"ident": [128, 128], "ones": [128, 128], "bd": [128, 128]}
for _d in range(2):
    CONST_SHAPES.update({f"tri{_d}": [128, 128], f"nsT{_d}": [128, 4, 128], f"ns{_d}": [128, 4, 128],
                         f"iT{_d}": [128, 4, 128], f"trim{_d}": [128, 128], f"suf{_d}": [128, 128],
                         f"mpa{_d}": [128, 4, 2, 128], f"ps{_d}": [128, 4, 128]})


class Prog:
    def __init__(s, L, debug=False, layers=(0, 1)):
        s.L = L
        s.NT = L // 128
        s.TB = min(512, L)
        s.NB = L // s.TB
        s.TBp = min(256, L)
        s.debug = debug
        s.layers = layers
        s.nc = bass.Bass("TRN2", target_bir_lowering=False)
        s.c = Ctx(s.nc)
        s.inputs = {}
        s.dbg_out = []

    def inp(s, name, shape):
        t = s.nc.dram_tensor(name, list(shape), F32, kind="ExternalInput").ap()
        s.inputs[name] = T(t, Buf(name))
        return s.inputs[name]

    def scratch(s, name, shape, dt=F32):
        kind = "ExternalOutput" if (s.debug and dt == F32) else "Internal"
        t = s.c.dram(name, shape, dt, kind=kind)
        if kind == "ExternalOutput":
            s.dbg_out.append(name)
        return t

    def load_const(s, name, shape, dt=F32, q="sp"):
        src = s.inp(name, shape)
        t = s.c.sb(shape, F32, name="c_" + name)
        s.c.dma(q, t[:], src[:], reads=[src], writes=[t])
        return t

    def load_bc(s, src_ap, n, srcbuf, name):
        t = s.c.sb([128, n], F32, name=name)
        s.c.dma("sp", t[:], src_ap.partition_broadcast(128), reads=[srcbuf], writes=[t])
        return t

    def psum_banks(s):
        s.P = [s.c.ps([128, 512], F32, name=f"bank{i}") for i in range(8)]

    def proj_fm(s, srcT, W, ncol, dst, tm_cols=None, tm_dst=None):
        c = s.c
        L, TB, NB = s.L, s.TB, s.NB
        nct = ncol // 128
        c.push()
        Wb = c.sb([128, 8, ncol], BF16, name="Wb")
        stage = c.sb([128, ncol], F32, name="wstage")
        Wv = W.t.rearrange("(k p) n -> p k n", p=128)
        for k in range(8):
            c.dma("sp", stage[:], Wv[:, k, :], reads=[W], writes=[stage])
            if k % 2 == 0:
                c.act(lambda e: e.copy(Wb[:, k, :], stage[:]), reads=[stage], acc=[Wb])
            else:
                c.dve(lambda e: e.tensor_copy(Wb[:, k, :], stage[:]), reads=[stage], acc=[Wb])
        xs = [c.sb([128, 8, TB], F32, name="xs0")] * 2
        xb = [c.sb([128, 8, TB], BF16, name=f"xb{i}") for i in range(2)]
        ob = [c.sb([128, 4, TB], F32, name=f"ob{i}") for i in range(2)]
        obb = [[Buf(f"ob{i}_{g}") for g in range(4)] for i in range(2)]
        xbb = [[Buf(f"xb{i}_{g}") for g in range(2)] for i in range(2)]
        xv = srcT.t.rearrange("(k p) l -> p k l", p=128)
        dv = dst.t.rearrange("(g p) l -> p g l", p=128)
        nob = 0
        for b in range(NB):
            t0 = b * TB
            X, XB = xs[b % 2], xb[b % 2]
            c.dma("sp", X[:], xv[:, :, t0:t0 + TB], reads=[srcT], writes=[X])
            XBa, XBb = xbb[b % 2]
            c.act(lambda e: e.copy(XB[:, 0:4, :], X[:, 0:4, :]), reads=[X], writes=[XBa])
            c.dve(lambda e: e.tensor_copy(XB[:, 4:8, :], X[:, 4:8, :]), reads=[X], writes=[XBb])
            for g0 in range(0, nct, 4):
                gn = min(4, nct - g0)
                O = ob[nob % 2]
                Ob = obb[nob % 2]
                nob += 1
                for gi in range(gn):
                    ct = g0 + gi
                    pb = s.P[(ct) % 8]
                    for k in range(8):
                        c.pe(lambda e: e.matmul(pb[:, 0:TB], Wb[:, k, ct * 128:(ct + 1) * 128], XB[:, k, :],
                                                start=(k == 0), stop=(k == 7)),
                             reads=[Wb, XBa, XBb], writes=[pb])
                    if ct % 2 == 0:
                        c.act(lambda e: e.copy(O[:, gi, :], pb[:, 0:TB]), reads=[pb], writes=[Ob[gi]])
                    else:
                        c.dve(lambda e: e.tensor_copy(O[:, gi, :], pb[:, 0:TB]), reads=[pb], writes=[Ob[gi]])
                c.dma("pool", dv[:, g0:g0 + gn, t0:t0 + TB], O[:, 0:gn, :], reads=Ob[0:gn], acc=[dst])
            if tm_cols is not None:
                c0, c1 = tm_cols
                for tt in range(TB // 128):
                    pb = s.P[tt % 2]
                    for k in range(8):
                        c.pe(lambda e: e.matmul(pb[:, 0:c1 - c0], XB[:, k, tt * 128:(tt + 1) * 128], Wb[:, k, c0:c1],
                                                start=(k == 0), stop=(k == 7)),
                             reads=[Wb, XBa, XBb], writes=[pb])
                    ti = b * (TB // 128) + tt
                    c.dve(lambda e: e.tensor_copy(tm_dst[:, ti, :], pb[:, 0:c1 - c0]), reads=[pb], acc=[tm_dst])
        c.pop()

    def dn_gates(s, ABraw, GB, dtb, alog):
        c = s.c
        NT = s.NT
        c.push()
        x = c.sb([128, NT, 8], name="gx")
        t1 = c.sb([128, NT, 8], name="gt1")
        t2 = c.sb([128, NT, 8], name="gt2")
        nA = c.sb([128, 8], name="gnA")
        c.act(lambda e: e.activation(out=nA[:], in_=alog[:], func=AF.Exp), reads=[alog], writes=[nA])
        c.dve(lambda e: e.tensor_tensor(x[:], ABraw[:, :, 0:8], bc(dtb[:, :].unsqueeze(1), [128, NT, 8]), ALU.add),
              reads=[ABraw, dtb], writes=[x])
        c.act(lambda e: e.activation(out=t1[:], in_=x[:], func=AF.Abs), reads=[x], writes=[t1])
        c.act(lambda e: e.activation(out=t1[:], in_=t1[:], func=AF.Exp, scale=-1.0), reads=[t1], writes=[t1])
        c.dve(lambda e: e.tensor_scalar_add(t1[:], t1[:], 1.0), reads=[t1], writes=[t1])
        c.act(lambda e: e.activation(out=t1[:], in_=t1[:], func=AF.Ln), reads=[t1], writes=[t1])
        c.dve(lambda e: e.tensor_scalar_max(t2[:], x[:], 0.0), reads=[x], writes=[t2])
        c.dve(lambda e: e.tensor_tensor(t2[:], t2[:], t1[:], ALU.add), reads=[t1, t2], writes=[t2])
        c.dve(lambda e: e.scalar_tensor_tensor(GB[:, :, 0:8], t2[:], -1.0, bc(nA[:, :].unsqueeze(1), [128, NT, 8]),
                                               ALU.mult, ALU.mult), reads=[t2, nA], acc=[GB])
        c.act(lambda e: e.activation(out=GB[:, :, 8:16], in_=ABraw[:, :, 8:16], func=AF.Sigmoid), reads=[ABraw], acc=[GB])
        c.pop()

    def dn_prep(s, projT, CW, qT, kT, k_tm, v_tm):
        c = s.c
        L, TB, NB = s.L, s.TB, s.NB
        NTB = TB // 128
        c.push()
        pv = projT.t[0:1536, :].rearrange("(t h p) l -> p t h l", t=3, h=4)
        ktv = k_tm.t.rearrange("(n p) h d -> p n h d", p=128)
        vtv = v_tm.t.rearrange("(n p) h d -> p n h d", p=128)
        ident, ones = s.K["ident"], s.K["ones"]
        Xs = [c.sb([128, 3, TB + 4], name=f"dnX{i}") for i in range(2)]
        As = [c.sb([128, 3, TB], name=f"dnA{i}") for i in range(2)]
        Ss = [c.sb([128, 2, TB], name=f"dnS{i}") for i in range(2)]
        KVs = [c.sb([128, 2, NTB, 128], name=f"dnK{i}") for i in range(2)]
        it = 0
        for h in range(4):
            for b in range(NB):
                t0 = b * TB
                X, A, S, KV = Xs[it % 2], As[it % 2], Ss[it % 2], KVs[it % 2]
                it += 1
                lo = max(t0 - 2, 0)
                hi = min(t0 + TB + 2, L)
                if lo != t0 - 2 or hi != t0 + TB + 2:
                    c.dve(lambda e: e.memset(X[:], 0.0), writes=[X])
                c.dma("sp", X[:, :, lo - (t0 - 2):hi - (t0 - 2)], pv[:, :, h, lo:hi], reads=[projT], writes=[X])
                for t in range(3):
                    c.dve(lambda e: e.tensor_scalar(A[:, t, :], X[:, t, 0:TB], CW[:, t, h, 0:1], None, ALU.mult),
                          reads=[X, CW], writes=[A])
                    for j in range(1, 5):
                        c.dve(lambda e: e.scalar_tensor_tensor(A[:, t, :], X[:, t, j:j + TB], CW[:, t, h, j:j + 1], A[:, t, :],
                                                               ALU.mult, ALU.add), reads=[X, CW, A], writes=[A])
                c.act(lambda e: e.activation(out=A[:], in_=A[:], func=AF.Silu), reads=[A], writes=[A])
                c.act(lambda e: e.activation(out=S[:], in_=A[:, 0:2, :], func=AF.Square), reads=[A], writes=[S])
                for t in range(2):
                    pb = s.P[t]
                    c.pe(lambda e: e.matmul(pb[:, 0:TB], ones[:], S[:, t, :], start=True, stop=True), reads=[S, ones], writes=[pb])
                for t in range(2):
                    pb = s.P[t]
                    sc = 128.0 if t == 0 else 1.0
                    c.dve(lambda e: e.tensor_scalar(S[:, t, :], pb[:, 0:TB], sc, RMS_EPS * sc, ALU.mult, ALU.add),
                          reads=[pb, S], writes=[S])
                c.act(lambda e: e.activation(out=S[:], in_=S[:], func=AF.Sqrt), reads=[S], writes=[S])
                c.dve(lambda e: e.reciprocal(S[:], S[:]), reads=[S], writes=[S])
                c.dve(lambda e: e.tensor_tensor(A[:, 0:2, :], A[:, 0:2, :], S[:], ALU.mult), reads=[A, S], writes=[A])
                c.dma("pool", qT[:, h, t0:t0 + TB], A[:, 0, :], reads=[A], acc=[qT])
                c.dma("pool", kT[:, h, t0:t0 + TB], A[:, 1, :], reads=[A], acc=[kT])
                for ti, t in enumerate((1, 2)):
                    pb = s.P[2 + ti]
                    for tt in range(NTB):
                        c.pe(lambda e: e.transpose(pb[:, tt * 128:(tt + 1) * 128], A[:, t, tt * 128:(tt + 1) * 128], ident[:]),
                             reads=[A, ident], writes=[pb])
                for ti in range(2):
                    pb = s.P[2 + ti]
                    if ti == 0:
                        c.act(lambda e: e.copy(KV[:, ti].rearrange("p n d -> p (n d)"), pb[:, 0:TB]), reads=[pb, KV], writes=[KV])
                    else:
                        c.dve(lambda e: e.tensor_copy(KV[:, ti].rearrange("p n d -> p (n d)"), pb[:, 0:TB]), reads=[pb, KV], writes=[KV])
                n0 = t0 // 128
                c.dma("pool", ktv[:, n0:n0 + NTB, h, :], KV[:, 0], reads=[KV], acc=[k_tm])
                c.dma("pool", vtv[:, n0:n0 + NTB, h, :], KV[:, 1], reads=[KV], acc=[v_tm])
        c.pop()

    def rk_prep(s, projT, rk_tm, bonusT):
        c = s.c
        L = s.L
        TB = min(256, L)
        NB = L // TB
        NTB = TB // 128
        I = s.inp
        c.push()
        MU = s.load_const("rk_mu_rkv", [128, 12, 2])
        MUL = s.load_const("rk_mu_lora", [64, 3, 2])
        W2 = s.load_const("rk_w2", [64, 2, 512])
        A2 = s.load_const("rk_a2", [64, 512])
        COLS = s.load_const("rk_cols", [128, 6, 4])
        w0_d = I("rk_w0", [1, 1024]); a0_d = I("rk_a0", [1, 512])
        W0 = s.load_bc(w0_d.t, 1024, w0_d, "rk_w0bc")
        A0 = s.load_bc(a0_d.t, 512, a0_d, "rk_a0bc")
        bd, ident = s.K["bd"], s.K["ident"]
        C0 = c.sb([128, 12], name="rkC0"); C0L = c.sb([64, 3], name="rkC0L"); OMK = c.sb([128, 4], name="rkOMK")
        c.dve(lambda e: e.tensor_tensor(C0[:], MU[:, :, 0], MU[:, :, 1], ALU.add), reads=[MU], writes=[C0])
        c.dve(lambda e: e.tensor_scalar(C0[:], C0[:], -1.0, 1.0, ALU.mult, ALU.add), reads=[C0], writes=[C0])
        c.dve(lambda e: e.tensor_tensor(C0L[:], MUL[:, :, 0], MUL[:, :, 1], ALU.add), reads=[MUL], writes=[C0L])
        c.dve(lambda e: e.tensor_scalar(C0L[:], C0L[:], -1.0, 1.0, ALU.mult, ALU.add), reads=[C0L], writes=[C0L])
        c.dve(lambda e: e.tensor_scalar(OMK[:], COLS[:, 2, :], -1.0, 1.0, ALU.mult, ALU.add), reads=[COLS], writes=[OMK])
        X = c.sb([128, 12, TB + 2], name="rkX"); XL = c.sb([64, 3, TB + 2], name="rkXL")
        Sx = c.sb([128, 12, TB], name="rkSx"); SL = c.sb([64, 3, TB], name="rkSL")
        TMP = c.sb([128, 12, TB], name="rkTMP")
        ET = c.sb([128, 4, TB], name="rkET"); KP = c.sb([128, 4, TB], name="rkKP"); KK = c.sb([128, 4, TB], name="rkKK")
        RS = c.sb([128, 4, TB], name="rkRS")
        TM = [c.sb([128, NTB, 512], name=f"rkTM{i}") for i in range(7)]
        pv = projT.t[2064:2064 + 1536, :].rearrange("(t p) l -> p t l", p=128)
        pl = projT.t[3600:3600 + 192, :].rearrange("(t p) l -> p t l", p=64)
        tmv = rk_tm.t.rearrange("i (n p) c -> i p n c", p=128)
        for b in range(NB):
            t0 = b * TB
            lo = max(t0 - 1, 0); hi = min(t0 + TB + 1, L)
            if lo != t0 - 1 or hi != t0 + TB + 1:
                c.dve(lambda e: e.memset(X[:], 0.0), writes=[X])
                c.dve(lambda e: e.memset(XL[:], 0.0), writes=[XL])
            c.dma("sp", X[:, :, lo - (t0 - 1):hi - (t0 - 1)], pv[:, :, lo:hi], reads=[projT], writes=[X])
            c.dma("sp", XL[:, :, lo - (t0 - 1):hi - (t0 - 1)], pl[:, :, lo:hi], reads=[projT], writes=[XL])
            c.dve(lambda e: e.tensor_tensor(Sx[:], X[:, :, 1:TB + 1], bc(C0[:, :].unsqueeze(2), [128, 12, TB]), ALU.mult), reads=[X, C0], writes=[Sx])
            c.dve(lambda e: e.tensor_tensor(TMP[:], X[:, :, 0:TB], bc(MU[:, :, 0:1], [128, 12, TB]), ALU.mult), reads=[X, MU], writes=[TMP])
            c.dve(lambda e: e.tensor_tensor(Sx[:], Sx[:], TMP[:], ALU.add), reads=[Sx, TMP], writes=[Sx])
            c.dve(lambda e: e.tensor_tensor(TMP[:], X[:, :, 2:TB + 2], bc(MU[:, :, 1:2], [128, 12, TB]), ALU.mult), reads=[X, MU, TMP], writes=[TMP])
            c.dve(lambda e: e.tensor_tensor(Sx[:], Sx[:], TMP[:], ALU.add), reads=[Sx, TMP], writes=[Sx])
            TL = TMP[0:64, 0:3, :]
            c.dve(lambda e: e.tensor_tensor(SL[:], XL[:, :, 1:TB + 1], bc(C0L[:, :].unsqueeze(2), [64, 3, TB]), ALU.mult), reads=[XL, C0L, TMP], writes=[SL])
            c.dve(lambda e: e.tensor_tensor(TL, XL[:, :, 0:TB], bc(MUL[:, :, 0:1], [64, 3, TB]), ALU.mult), reads=[XL, MUL, TMP], writes=[TMP])
            c.dve(lambda e: e.tensor_tensor(SL[:], SL[:], TL, ALU.add), reads=[SL, TMP], writes=[SL])
            c.dve(lambda e: e.tensor_tensor(TL, XL[:, :, 2:TB + 2], bc(MUL[:, :, 1:2], [64, 3, TB]), ALU.mult), reads=[XL, MUL, TMP], writes=[TMP])
            c.dve(lambda e: e.tensor_tensor(SL[:], SL[:], TL, ALU.add), reads=[SL, TMP], writes=[SL])
            c.act(lambda e: e.activation(out=SL[:, 0:2, :], in_=SL[:, 0:2, :], func=AF.Tanh), reads=[SL], writes=[SL])
            for tt in range(NTB):
                tsl = slice(tt * 128, (tt + 1) * 128)
                for i, (lsrc, rhs, addv, dsti) in enumerate([(SL[:, 0, tsl], W2[:, 0, :], W0[:, 0:512], 5),
                                                             (SL[:, 1, tsl], W2[:, 1, :], W0[:, 512:1024], 6),
                                                             (SL[:, 2, tsl], A2[:, :], A0[:, :], 4)]):
                    pb = s.P[4 + i]
                    c.pe(lambda e: e.matmul(pb[:, :], lsrc, rhs, start=True, stop=True), reads=[SL, W2, A2], writes=[pb])
                    c.dve(lambda e: e.tensor_tensor(TM[dsti][:, tt, :], pb[:, :], addv, ALU.add), reads=[pb, W0, A0, TM[dsti]], writes=[TM[dsti]])
            for dsti in (4, 5, 6):
                c.act(lambda e: e.activation(out=TM[dsti][:], in_=TM[dsti][:], func=AF.Sigmoid), reads=[TM[dsti]], writes=[TM[dsti]])
            for dsti in (5, 6):
                c.dve(lambda e: e.tensor_scalar(TM[dsti][:], TM[dsti][:], -math.exp(-0.5), None, ALU.mult), reads=[TM[dsti]], writes=[TM[dsti]])
            for ct in range(4):
                pb = s.P[ct]
                c.pe(lambda e: e.matmul(pb[:, 0:TB], A2[:, ct * 128:(ct + 1) * 128], SL[:, 2, :], start=True, stop=True), reads=[SL, A2], writes=[pb])
                c.act(lambda e: e.activation(out=ET[:, ct, :], in_=pb[:, 0:TB], func=AF.Sigmoid, bias=COLS[:, 0, ct:ct + 1]), reads=[pb, COLS, ET], writes=[ET])
                c.dve(lambda e: e.tensor_scalar(ET[:, ct, :], ET[:, ct, :], COLS[:, 2, ct:ct + 1], OMK[:, ct:ct + 1], ALU.mult, ALU.add), reads=[ET, COLS, OMK], writes=[ET])
            c.dve(lambda e: e.tensor_tensor(KP[:], Sx[:, 4:8, :], ET[:], ALU.mult), reads=[Sx, ET], writes=[KP])
            c.dve(lambda e: e.tensor_tensor(KK[:], Sx[:, 4:8, :], bc(COLS[:, 1, :].unsqueeze(2), [128, 4, TB]), ALU.mult), reads=[Sx, COLS], writes=[KK])
            c.act(lambda e: e.activation(out=RS[:], in_=KK[:], func=AF.Square), reads=[KK], writes=[RS])
            for ct in range(4):
                pb = s.P[ct]
                c.pe(lambda e: e.matmul(pb[:, 0:TB], bd[:], RS[:, ct, :], start=True, stop=True), reads=[RS, bd], writes=[pb])
            for ct in range(4):
                pb = s.P[ct]
                c.dve(lambda e: e.tensor_scalar_add(RS[:, ct, :], pb[:, 0:TB], RMS_EPS), reads=[pb, RS], writes=[RS])
            c.act(lambda e: e.activation(out=RS[:], in_=RS[:], func=AF.Sqrt), reads=[RS], writes=[RS])
            c.dve(lambda e: e.reciprocal(RS[:], RS[:]), reads=[RS], writes=[RS])
            c.dve(lambda e: e.tensor_tensor(KK[:], KK[:], RS[:], ALU.mult), reads=[KK, RS], writes=[KK])
            c.dve(lambda e: e.tensor_tensor(RS[:], Sx[:, 0:4, :], KP[:], ALU.mult), reads=[Sx, KP, RS], writes=[RS])
            c.dve(lambda e: e.tensor_tensor(RS[:], RS[:], bc(COLS[:, 3, :].unsqueeze(2), [128, 4, TB]), ALU.mult), reads=[RS, COLS], writes=[RS])
            for ct in range(4):
                pb = s.P[ct]
                c.pe(lambda e: e.matmul(pb[:, 0:TB], bd[:], RS[:, ct, :], start=True, stop=True), reads=[RS, bd], writes=[pb])
            for ct in range(4):
                pb = s.P[ct]
                c.dve(lambda e: e.tensor_tensor(ET[:, ct, :], pb[:, 0:TB], Sx[:, 8 + ct, :], ALU.mult), reads=[pb, Sx, ET], writes=[ET])
            c.dma("pool", bonusT[:, :, t0:t0 + TB], ET[:], reads=[ET], acc=[bonusT])
            for i, src in enumerate([Sx[:, 0:4, :], KP[:], Sx[:, 8:12, :], KK[:]]):
                for tt in range(NTB):
                    pb = s.P[4 + (i * NTB + tt) % 4]
                    for ct in range(4):
                        c.pe(lambda e: e.transpose(pb[:, ct * 128:(ct + 1) * 128], src[:, ct, tt * 128:(tt + 1) * 128], ident[:]),
                             reads=[Sx, KP, KK, ident], writes=[pb])
                    if (i + tt) % 2 == 0:
                        c.act(lambda e: e.copy(TM[i][:, tt, :], pb[:, :]), reads=[pb, TM[i]], writes=[TM[i]])
                    else:
                        c.dve(lambda e: e.tensor_copy(TM[i][:, tt, :], pb[:, :]), reads=[pb, TM[i]], writes=[TM[i]])
            n0 = t0 // 128
            for i in range(7):
                c.dma("pool", tmv[i, :, n0:n0 + NTB, :], TM[i][:], reads=[TM[i]], acc=[rk_tm])
        c.pop()

    def tri_inverse(s, N0, P0, srcbufs, W, banks=None):
        c = s.c
        ident = s.K["ident"]
        R, NA, NB_, PA_, PB_ = W["R"], W["NA"], W["NB"], W["PA"], W["PB"]
        c.dve(lambda e: e.tensor_tensor(R[:], P0, bc(ident[:, :].unsqueeze(1), [128, 4, 128]), ALU.add),
              reads=srcbufs + [ident], writes=[R])
        Ncur, Pcur, nb, pb_ = N0, P0, srcbufs, srcbufs
        Nbufs, Pbufs = [NA, NB_], [PA_, PB_]
        v4 = lambda bank: bank[:, :].rearrange("p (h c) -> p h c", h=4)
        for k in range(1, 7):
            Nn, Pn = Nbufs[k % 2], Pbufs[k % 2]
            bN, bP, bR = banks if banks is not None else (s.P[5], s.P[6], s.P[7])
            for h in range(4):
                c.pe(lambda e: e.matmul(bN[:, h * 128:(h + 1) * 128], Pcur[:, h, :], Ncur[:, h, :], start=True, stop=True),
                     reads=nb + pb_, writes=[bN])
            if k < 6:
                for h in range(4):
                    c.pe(lambda e: e.matmul(bP[:, h * 128:(h + 1) * 128], Ncur[:, h, :], Pcur[:, h, :], start=True, stop=True),
                         reads=nb + pb_, writes=[bP])
            c.act(lambda e: e.copy(Nn[:], v4(bN)), reads=[bN, Nn], writes=[Nn])
            if k < 6:
                c.dve(lambda e: e.tensor_copy(Pn[:], v4(bP)), reads=[bP, Pn], writes=[Pn])
            for h in range(4):
                c.pe(lambda e: e.matmul(bR[:, h * 128:(h + 1) * 128], Nn[:, h, :], R[:, h, :], start=True, stop=True),
                     reads=[Nn, R], writes=[bR])
            c.dve(lambda e: e.tensor_tensor(R[:], R[:], v4(bR), ALU.add), reads=[R, bR], writes=[R])
            Ncur, Pcur, nb, pb_ = Nn[:], Pn[:], [Nn], [Pn]
            yield
        return

    def dn_pass(s, d, qT, kT, k_tm, v_tm, oT, W, PBK):
        c = s.c
        L, NT = s.L, s.NT
        GB = s.GB
        K = s.K
        tri, nsT, ns, iT, ident, ones = K[f"tri{d}"], K[f"nsT{d}"], K[f"ns{d}"], K[f"iT{d}"], K["ident"], K["ones"]
        lastc = 127 if d == 0 else 0
        H = W["H"]
        c.dve(lambda e: e.memset(H[:], 0.0), writes=[H])
        v4 = lambda bank: bank[:, :].rearrange("p (h c) -> p h c", h=4)
        ktv = k_tm.t.rearrange("(n p) h d -> n p h d", p=128)
        vtv = v_tm.t.rearrange("(n p) h d -> n p h d", p=128)
        order = range(NT) if d == 0 else range(NT - 1, -1, -1)
        for it, n in enumerate(order):
            ld = W["ld"][it % 2]
            KT, QT, Ktm, Vtm = ld["KT"], ld["QT"], ld["Ktm"], ld["Vtm"]
            tsl = slice(n * 128, (n + 1) * 128)
            c.dma("sp", KT[:], kT[:, :, tsl], reads=[kT], writes=[KT])
            c.dma("sp", QT[:], qT[:, :, tsl], reads=[qT], writes=[QT])
            c.dma("sp", Ktm[:], ktv[n], reads=[k_tm], writes=[Ktm])
            c.dma("sp", Vtm[:], vtv[n], reads=[v_tm], writes=[Vtm])
            g = GB[:, n, d * 4:(d + 1) * 4]
            beta = GB[:, n, 8 + d * 4:8 + (d + 1) * 4]
            Gc, RG, RB, Grow, E1, E2, X1, X2, AT = (W[k] for k in ("Gc", "RG", "RB", "Grow", "E1", "E2", "X1", "X2", "AT"))
            b0, b1, b2, b3 = PBK
            b4, b5, b6, b7 = PBK
            c.pe(lambda e: e.matmul(b0[:, 0:4], tri[:], g, start=True, stop=True), reads=[tri, GB], writes=[b0])
            c.act(lambda e: e.copy(Gc[:], b0[:, 0:4]), reads=[b0, Gc], writes=[Gc])
            c.dve(lambda e: e.tensor_tensor(RG[:], bc(tri[:, :].unsqueeze(1), [128, 4, 128]), bc(g.unsqueeze(2), [128, 4, 128]), ALU.mult),
                  reads=[tri, GB, RG], writes=[RG])
            c.dve(lambda e: e.tensor_tensor(RB[:], bc(ident[:, :].unsqueeze(1), [128, 4, 128]), bc(beta.unsqueeze(2), [128, 4, 128]), ALU.mult),
                   reads=[ident, GB, RB], writes=[RB])
            c.pe(lambda e: e.matmul(b1[:, :], ones[:], RG[:].rearrange("p h c -> p (h c)"), start=True, stop=True), reads=[ones, RG], writes=[b1])
            c.pe(lambda e: e.matmul(b2[:, :], ones[:], RB[:].rearrange("p h c -> p (h c)"), start=True, stop=True), reads=[ones, RB], writes=[b2])
            for h in range(4):
                c.pe(lambda e: e.matmul(b3[:, h * 128:(h + 1) * 128], KT[:, h, :], KT[:, h, :], start=True, stop=True), reads=[KT], writes=[b3])
            for h in range(4):
                c.pe(lambda e: e.matmul(b4[:, h * 128:(h + 1) * 128], KT[:, h, :], QT[:, h, :], start=True, stop=True), reads=[KT, QT], writes=[b4])
            c.act(lambda e: e.copy(Grow[:], v4(b1)), reads=[b1, Grow], writes=[Grow])
            yield
            c.dve(lambda e: e.tensor_tensor(E1[:], Grow[:], bc(Gc[:, :].unsqueeze(2), [128, 4, 128]), ALU.subtract), reads=[Grow, Gc, E1], writes=[E1])
            c.dve(lambda e: e.tensor_scalar_max(E2[:], E1[:], 0.0), reads=[E1, E2], writes=[E2])
            c.dve(lambda e: e.tensor_scalar_min(E1[:], E1[:], 0.0), reads=[E1], writes=[E1])
            c.act(lambda e: e.activation(out=E1[:], in_=E1[:], func=AF.Exp), reads=[E1], writes=[E1])
            c.act(lambda e: e.activation(out=E2[:], in_=E2[:], func=AF.Exp, scale=-1.0), reads=[E2], writes=[E2])
            yield
            c.dve(lambda e: e.tensor_tensor(X1[:], E1[:], nsT[:], ALU.mult), reads=[E1, nsT, X1], writes=[X1])
            c.dve(lambda e: e.tensor_tensor(X1[:], X1[:], v4(b2), ALU.mult), reads=[X1, b2], writes=[X1])
            c.dve(lambda e: e.tensor_tensor(X1[:], X1[:], v4(b3), ALU.mult), reads=[X1, b3], writes=[X1])
            c.dve(lambda e: e.tensor_tensor(X2[:], E2[:], ns[:], ALU.mult), reads=[E2, ns, X2], writes=[X2])
            c.dve(lambda e: e.tensor_tensor(X2[:], X2[:], bc(beta.unsqueeze(2), [128, 4, 128]), ALU.mult), reads=[X2, GB], writes=[X2])
            c.dve(lambda e: e.tensor_tensor(X2[:], X2[:], v4(b3), ALU.mult), reads=[X2, b3], writes=[X2])
            c.dve(lambda e: e.tensor_tensor(AT[:], E1[:], iT[:], ALU.mult), reads=[E1, iT, AT], writes=[AT])
            c.dve(lambda e: e.tensor_tensor(AT[:], AT[:], v4(b4), ALU.mult), reads=[AT, b4], writes=[AT])
            yield
            yield from s.tri_inverse(X2[:], X1[:], [X1, X2], W, banks=(PBK[1], PBK[2], PBK[3]))
            R = W["R"]
            S4, CO, EGT, GAM, KBN, VB, KTL, QD = (W[k] for k in ("S4", "CO", "EGT", "GAM", "KBN", "VB", "KTL", "QD"))
            c.act(lambda e: e.activation(out=S4[:], in_=Gc[:], func=AF.Exp), reads=[Gc, S4], writes=[S4])
            c.dve(lambda e: e.scalar_tensor_tensor(CO[:], S4[:], -1.0, beta, ALU.mult, ALU.mult), reads=[S4, GB, CO], writes=[CO])
            c.dve(lambda e: e.tensor_tensor(KBN[:], Ktm[:], bc(CO[:, :].unsqueeze(2), [128, 4, 128]), ALU.mult), reads=[Ktm, CO, KBN], writes=[KBN])
            c.dve(lambda e: e.tensor_tensor(VB[:], Vtm[:], bc(beta.unsqueeze(2), [128, 4, 128]), ALU.mult), reads=[Vtm, GB, VB], writes=[VB])
            c.dve(lambda e: e.tensor_tensor(EGT[:], Grow[:, :, lastc], Gc[:], ALU.subtract), reads=[Grow, Gc, EGT], writes=[EGT])
            c.act(lambda e: e.activation(out=EGT[:], in_=EGT[:], func=AF.Exp), reads=[EGT], writes=[EGT])
            c.act(lambda e: e.activation(out=GAM[:], in_=Grow[:, :, lastc], func=AF.Exp), reads=[Grow, GAM], writes=[GAM])
            c.dve(lambda e: e.tensor_tensor(KTL[:], Ktm[:], bc(EGT[:, :].unsqueeze(2), [128, 4, 128]), ALU.mult), reads=[Ktm, EGT, KTL], writes=[KTL])
            c.act(lambda e: e.activation(out=QD[:], in_=Grow[:], func=AF.Exp), reads=[Grow, QD], writes=[QD])
            c.dve(lambda e: e.tensor_tensor(QD[:], QD[:], QT[:], ALU.mult), reads=[QD, QT], writes=[QD])
            yield
            WT, U0, U, OS = W["WT"], W["U0"], W["U"], W["OS"]
            for h in range(4):
                c.pe(lambda e: e.matmul(b5[:, h * 128:(h + 1) * 128], KBN[:, h, :], R[:, h, :], start=True, stop=True), reads=[KBN, R], writes=[b5])
            for h in range(4):
                c.pe(lambda e: e.matmul(b6[:, h * 128:(h + 1) * 128], R[:, h, :], VB[:, h, :], start=True, stop=True), reads=[VB, R], writes=[b6])
            c.act(lambda e: e.copy(WT[:], v4(b5)), reads=[b5, WT], writes=[WT])
            c.dve(lambda e: e.tensor_copy(U0[:], v4(b6)), reads=[b6, U0], writes=[U0])
            yield
            for h in range(4):
                c.pe(lambda e: e.matmul(b7[:, h * 128:(h + 1) * 128], WT[:, h, :], H[:, h, :], start=True, stop=True), reads=[WT, H], writes=[b7])
            c.dve(lambda e: e.tensor_tensor(U[:], U0[:], v4(b7), ALU.add), reads=[U0, b7, U], writes=[U])
            yield
            for h in range(4):
                c.pe(lambda e: e.matmul(b5[:, h * 128:(h + 1) * 128], H[:, h, :], QD[:, h, :], start=True, stop=False), reads=[H, QD], writes=[b5])
                c.pe(lambda e: e.matmul(b5[:, h * 128:(h + 1) * 128], U[:, h, :], AT[:, h, :], start=False, stop=True), reads=[U, AT], writes=[b5])
            for h in range(4):
                c.pe(lambda e: e.matmul(b6[:, h * 128:(h + 1) * 128], KTL[:, h, :], U[:, h, :], start=True, stop=True), reads=[KTL, U], writes=[b6])
            c.act(lambda e: e.copy(OS[:], v4(b5)), reads=[b5, OS], writes=[OS])
            c.dma("pool", oT[:, :, tsl], OS[:], reads=[OS], acc=[oT])
            c.dve(lambda e: e.tensor_tensor(H[:], H[:], bc(GAM[:, :].unsqueeze(2), [128, 4, 128]), ALU.mult), reads=[H, GAM], writes=[H])
            c.dve(lambda e: e.tensor_tensor(H[:], H[:], v4(b6), ALU.add), reads=[H, b6], writes=[H])
            yield

    def run_chains(s, gens):
        gens = list(gens)
        while gens:
            for g in list(gens):
                try:
                    next(g)
                except StopIteration:
                    gens.remove(g)

    def alloc_chunk_ws(s):
        c = s.c
        W = {}
        for k in ("R", "NA", "NB", "PA", "PB", "RG", "RB", "Grow", "E1", "E2", "X1", "X2", "AT", "KBN", "VB", "KTL", "QD", "WT", "U0", "U", "OS", "H"):
            W[k] = c.sb([128, 4, 128], name="w_" + k)
        for k in ("Gc", "S4", "CO", "EGT", "GAM"):
            W[k] = c.sb([128, 4], name="w_" + k)
        W["ld"] = [{k: c.sb([128, 4, 128], name=f"ld{i}_{k}") for k in ("KT", "QT", "Ktm", "Vtm")} for i in range(2)]
        return W

    def rk_pass(s, d, rk_tm, yT, W, R_):
        c = s.c
        L, NT = s.L, s.NT
        K = s.K
        trim, suf, mpa, psm, ident, ones = K[f"trim{d}"], K[f"suf{d}"], K[f"mpa{d}"], K[f"ps{d}"], K["ident"], K["ones"]
        H, GAM = R_["H"], R_["GAM"]
        c.dve(lambda e: e.memset(H[:], 0.0), writes=R_["bufs"]["H"])
        order = range(NT) if d == 0 else range(NT - 1, -1, -1)
        b0, b1, b2, b3, b4, b5, b6, b7 = s.P
        q4 = lambda ap, n: ap.rearrange("p (q c) -> p q c", q=n)
        for it, n in enumerate(order):
            ld = R_["ld"][it % 2]
            tsl = slice(n * 128, (n + 1) * 128)
            for i, key in [(0, "r"), (1, "kp"), (2, "v"), (3, "kk"), (4, "eta"), (5 + d, "lw")]:
                c.dma("sp", ld[key][:], rk_tm[i, tsl, :], reads=[rk_tm], writes=[ld[key]])
            r, kp, V, kk, eta, LW = (ld[k] for k in ("r", "kp", "v", "kk", "eta", "lw"))
            Ep, En, Ex, Et, Bv, rt, kt, bt, at, kh, bh = (R_[k] for k in ("Ep", "En", "Ex", "Et", "B", "rt", "kt", "bt", "at", "kh", "bh"))
            c.pe(lambda e: e.matmul(b0[:, :], trim[:], LW[:], start=True, stop=True), reads=[trim, LW], writes=[b0])
            c.pe(lambda e: e.matmul(b1[:, :], suf[:], LW[:], start=True, stop=True), reads=[suf, LW], writes=[b1])
            for h in range(8):
                c.pe(lambda e: e.matmul(b2[0:64, h:h + 1], LW[:, h * 64:(h + 1) * 64], ones[:, 0:1], start=True, stop=True), reads=[LW, ones], writes=[b2])
            mid = 63 if d == 0 else 64
            for h in range(8):
                c.pe(lambda e: e.matmul(b2[0:64, 8 + h:9 + h], LW[:, h * 64:(h + 1) * 64], K[f"tri{d}"][:, mid:mid + 1], start=True, stop=True), reads=[LW, K[f"tri{d}"]], writes=[b2])
            c.act(lambda e: e.activation(out=GAM[:], in_=b2[0:64, 0:16], func=AF.Exp), reads=[b2, GAM], writes=[GAM])
            c.act(lambda e: e.activation(out=Ep[:], in_=b0[:, :], func=AF.Exp), reads=[b0, Ep], writes=[Ep])
            c.act(lambda e: e.activation(out=En[:], in_=b0[:, :], func=AF.Exp, scale=-1.0), reads=[b0, En], writes=[En])
            c.dve(lambda e: e.tensor_tensor(Ex[:], b0[:, :], LW[:], ALU.subtract), reads=[b0, LW, Ex], writes=[Ex])
            c.act(lambda e: e.activation(out=Ex[:], in_=Ex[:], func=AF.Exp), reads=[Ex], writes=[Ex])
            c.act(lambda e: e.activation(out=Et[:], in_=b1[:, :], func=AF.Exp), reads=[b1, Et], writes=[Et])
            c.dve(lambda e: e.tensor_tensor(Bv[:], kk[:], eta[:], ALU.mult), reads=[kk, eta, Bv], writes=[Bv])
            c.dve(lambda e: e.tensor_tensor(rt[:], r[:], Ep[:], ALU.mult), reads=[r, Ep, rt], writes=[rt])
            c.dve(lambda e: e.tensor_tensor(kt[:], kp[:], En[:], ALU.mult), reads=[kp, En, kt], writes=[kt])
            c.dve(lambda e: e.tensor_tensor(bt[:], Bv[:], En[:], ALU.mult), reads=[Bv, En, bt], writes=[bt])
            c.dve(lambda e: e.scalar_tensor_tensor(at[:], kk[:], -1.0, Ex[:], ALU.mult, ALU.mult), reads=[kk, Ex, at], writes=[at])
            c.dve(lambda e: e.tensor_tensor(kh[:], kp[:], Et[:], ALU.mult), reads=[kp, Et, kh], writes=[kh])
            c.dve(lambda e: e.tensor_tensor(bh[:], Bv[:], Et[:], ALU.mult), reads=[Bv, Et, bh], writes=[bh])
            RAT, BT, KT2 = R_["RAT"], R_["BT"], R_["KT2"]
            tb = [b3, b4, b5, b6]
            ti = 0
            for src, dst, dbuf in [(at, lambda hs: RAT[:, hs, 0, :], RAT), (rt, lambda hs: RAT[:, hs, 1, :], RAT), (bt, lambda hs: BT[:, hs, :], BT), (kt, lambda hs: KT2[:, hs, :], KT2)]:
                for half in range(2):
                    bank = tb[ti % 4]
                    ti += 1
                    for q in range(4):
                        h = half * 4 + q
                        c.pe(lambda e: e.transpose(bank[0:64, q * 128:(q + 1) * 128], src[:, h * 64:(h + 1) * 64], ident[:]), reads=[src, ident], writes=[bank])
                    hs = slice(half * 4, half * 4 + 4)
                    if ti % 2 == 0:
                        c.act(lambda e: e.copy(dst(hs), q4(bank[0:64, :], 4)), reads=[bank, dbuf], writes=[dbuf])
                    else:
                        c.dve(lambda e: e.tensor_copy(dst(hs), q4(bank[0:64, :], 4)), reads=[bank, dbuf], writes=[dbuf])
            PAs, AKs, NNs, WT, AV, U0, U, YS = (R_[k] for k in ("PAs", "AKs", "NNs", "WT", "AV", "U0", "U", "YS"))
            PAb, AKb, NNb, WTb, AVb, U0b, Ub, YSb, Hb, Hsb = (R_["bufs"][k] for k in ("PAs", "AKs", "NNs", "WT", "AV", "U0", "U", "YS", "H", "Hs"))
            Hs = R_["Hs"]

            def hg_chain(hg, BK, Wd):
                B0, B1, B2, B3 = BK
                hs = slice(hg * 4, hg * 4 + 4)
                for q in range(4):
                    h = hg * 4 + q
                    bpa = (B0, B1)[q // 2]
                    bak = (B2, B3)[q // 2]
                    osl = slice((q % 2) * 256, (q % 2 + 1) * 256)
                    rhs2 = RAT[:, h, :, :].rearrange("p a c -> p (a c)")
                    c.pe(lambda e: e.matmul(bpa[:, osl], BT[:, h, :], rhs2, start=True, stop=True), reads=[BT, RAT], writes=[bpa])
                    c.pe(lambda e: e.matmul(bak[:, osl], KT2[:, h, :], rhs2, start=True, stop=True), reads=[KT2, RAT], writes=[bak])
                yield
                v22 = lambda bank: bank[:, :].rearrange("p (q a c) -> p q a c", q=2, a=2)
                for j in range(2):
                    h2 = slice(hg * 4 + 2 * j, hg * 4 + 2 * j + 2)
                    c.dve(lambda e: e.tensor_tensor(PAs[:, h2], v22((B0, B1)[j]), mpa[:, 0:2], ALU.mult), reads=[(B0, B1)[j], mpa], writes=[PAb[hg]])
                    c.dve(lambda e: e.tensor_tensor(AKs[:, h2], v22((B2, B3)[j]), mpa[:, 0:2], ALU.mult), reads=[(B2, B3)[j], mpa], writes=[AKb[hg]])
                for q in range(4):
                    h = hg * 4 + q
                    c.pe(lambda e: e.matmul(B0[:, q * 128:(q + 1) * 128], RAT[:, h, 0, :], BT[:, h, :], start=True, stop=True), reads=[BT, RAT], writes=[B0])
                yield
                c.dve(lambda e: e.tensor_tensor(NNs[:, hs], q4(B0[:, :], 4), psm[:], ALU.mult), reads=[B0, psm], writes=[NNb[hg]])
                yield from s.tri_inverse(NNs[:, hs], PAs[:, hs, 0, :], [NNb[hg], PAb[hg]], Wd, banks=(B1, B2, B3))
                R = Wd["R"]
                for q in range(4):
                    h = hg * 4 + q
                    c.pe(lambda e: e.matmul(B0[0:64, q * 128:(q + 1) * 128], at[:, h * 64:(h + 1) * 64], R[:, q, :], start=True, stop=True), reads=[at, R], writes=[B0])
                    c.pe(lambda e: e.matmul(B1[:, q * 64:(q + 1) * 64], AKs[:, h, 0, :], V[:, h * 64:(h + 1) * 64], start=True, stop=True), reads=[AKb[hg], V], writes=[B1])
                yield
                c.act(lambda e: e.copy(WT[:, hs, :], q4(B0[0:64, :], 4)), reads=[B0], writes=[WTb[hg]])
                c.dve(lambda e: e.tensor_copy(AV[:, hs, :], q4(B1[:, 0:256], 4)), reads=[B1], writes=[AVb[hg]])
                for q in range(4):
                    h = hg * 4 + q
                    c.pe(lambda e: e.matmul(B2[:, q * 64:(q + 1) * 64], R[:, q, :], AV[:, h, :], start=True, stop=True), reads=[AVb[hg], R], writes=[B2])
                yield
                c.act(lambda e: e.copy(U0[:, hs, :], q4(B2[:, 0:256], 4)), reads=[B2], writes=[U0b[hg]])
                c.dve(lambda e: e.tensor_tensor(Hs[:, hs, :], H[:, hs, :], bc(GAM[:, 8 + hg * 4:12 + hg * 4].unsqueeze(2), [64, 4, 64]), ALU.mult), reads=[Hb[hg], GAM], writes=[Hsb[hg]])
                for q in range(4):
                    h = hg * 4 + q
                    c.pe(lambda e: e.matmul(B3[:, q * 64:(q + 1) * 64], WT[:, h, :], Hs[:, h, :], start=True, stop=True), reads=[WTb[hg], Hsb[hg]], writes=[B3])
                yield
                c.dve(lambda e: e.tensor_tensor(U[:, hs, :], U0[:, hs, :], q4(B3[:, 0:256], 4), ALU.add), reads=[U0b[hg], B3], writes=[Ub[hg]])
                for q in range(4):
                    h = hg * 4 + q
                    o = B0[0:64, q * 128:(q + 1) * 128]
                    c.pe(lambda e: e.matmul(o, Hs[:, h, :], RAT[:, h, 1, :], start=True, stop=False), reads=[Hsb[hg], RAT], writes=[B0])
                    c.pe(lambda e: e.matmul(o, U[:, h, :], PAs[:, h, 1, :], start=False, stop=False), reads=[Ub[hg], PAb[hg]], writes=[B0])
                    c.pe(lambda e: e.matmul(o, V[:, h * 64:(h + 1) * 64], AKs[:, h, 1, :], start=False, stop=True), reads=[V, AKb[hg]], writes=[B0])
                for q in range(4):
                    h = hg * 4 + q
                    o = B1[0:64, q * 64:(q + 1) * 64]
                    c.pe(lambda e: e.matmul(o, bh[:, h * 64:(h + 1) * 64], U[:, h, :], start=True, stop=False), reads=[bh, Ub[hg]], writes=[B1])
                    c.pe(lambda e: e.matmul(o, kh[:, h * 64:(h + 1) * 64], V[:, h * 64:(h + 1) * 64], start=False, stop=True), reads=[kh, V], writes=[B1])
                yield
                c.act(lambda e: e.copy(YS[:, hs, :], q4(B0[0:64, :], 4)), reads=[B0], writes=[YSb[hg]])
                c.dve(lambda e: e.tensor_tensor(H[:, hs, :], H[:, hs, :], bc(GAM[:, hs].unsqueeze(2), [64, 4, 64]), ALU.mult), reads=[Hb[hg], GAM], writes=[Hb[hg]])
                c.dve(lambda e: e.tensor_tensor(H[:, hs, :], H[:, hs, :], q4(B1[0:64, 0:256], 4), ALU.add), reads=[Hb[hg], B1], writes=[Hb[hg]])
                yield
            s.run_chains([hg_chain(hg, s.P[4 * hg:4 * hg + 4], W[hg]) for hg in range(2)])
            c.dma("pool", yT[:, :, tsl], YS[:], reads=YSb, acc=[yT])

    def alloc_rk_ws(s):
        c = s.c
        R_ = {}
        for k in ("Ep", "En", "Ex", "Et", "B", "rt", "kt", "bt", "at", "kh", "bh"):
            R_[k] = c.sb([128, 512], name="r_" + k)
        R_["ld"] = [{k: c.sb([128, 512], name=f"rld{i}_{k}") for k in ("r", "kp", "v", "kk", "eta", "lw")} for i in range(2)]
        R_["RAT"] = c.sb([64, 8, 2, 128], name="r_RAT")
        R_["BT"] = c.sb([64, 8, 128], name="r_BT")
        R_["KT2"] = c.sb([64, 8, 128], name="r_KT2")
        R_["PAs"] = c.sb([128, 8, 2, 128], name="r_PAs")
        R_["AKs"] = c.sb([128, 8, 2, 128], name="r_AKs")
        R_["NNs"] = c.sb([128, 8, 128], name="r_NNs")
        R_["WT"] = c.sb([64, 8, 128], name="r_WT")
        R_["YS"] = c.sb([64, 8, 128], name="r_YS")
        for k in ("AV", "U0", "U"):
            R_[k] = c.sb([128, 8, 64], name="r_" + k)
        R_["H"] = c.sb([64, 8, 64], name="r_H")
        R_["GAM"] = c.sb([64, 16], name="r_GAM")
        R_["Hs"] = c.sb([64, 8, 64], name="r_Hs")
        R_["bufs"] = {k: [Buf(f"{k}_{hg}") for hg in range(2)] for k in ("PAs", "AKs", "NNs", "WT", "AV", "U0", "U", "YS", "H", "Hs")}
        return R_

    def load_w_bf16(s, src_ap, srcbuf, P, nk, name):
        c = s.c
        Wt = c.sb([P, nk, 1024], BF16, name=name)
        if not hasattr(s, "_wst") or s._wst is None:
            s._wst = c.sb([128, 1024], F32, name=name + "_st")
        st = s._wst
        v = src_ap.rearrange("(k p) n -> p k n", p=P)
        for k in range(nk):
            c.dma("sp", st[:P], v[:, k, :], reads=[srcbuf], writes=[st])
            c.dve(lambda e: e.tensor_copy(Wt[:, k, :], st[:P]), reads=[st, Wt], writes=[Wt])
        return Wt

    def post(s, layer, prep, wparts, resid, pT, out_dst, hT_dst):
        c = s.c
        L = s.L
        TB = s.TBp
        NB = L // TB
        NTB = TB // 128
        I = s.inp
        ident = s.K["ident"]
        s._wst = None
        w_out = I(f"w_out{layer}", [1024, 1024]); ple_w = I(f"ple_w{layer}", [256, 1024]); ple_g = I(f"ple_gate{layer}", [1024, 1024])
        vec = I(f"vecs{layer}", [3, 1024])
        LG = s.load_bc(vec.t[0:1, :], 1024, vec, f"LG{layer}")
        LB = s.load_bc(vec.t[1:2, :], 1024, vec, f"LB{layer}")
        PN = s.load_bc(vec.t[2:3, :], 1024, vec, f"PN{layer}")
        Wp = []
        for (r0, r1, P) in wparts:
            Wp.append(s.load_w_bf16(w_out.t[r0:r1, :], w_out, P, (r1 - r0) // P, f"wout{layer}_{r0}"))
        PW = s.load_w_bf16(ple_w.t, ple_w, 128, 2, f"plew{layer}")
        PG = s.load_w_bf16(ple_g.t, ple_g, 128, 8, f"pleg{layer}")
        Pst = c.sb([128, 2, TB], F32, name="post_pst"); Pb = c.sb([128, 2, TB], BF16, name="post_pb")
        XT = c.sb([128, 1024], name="post_xt"); PRE = c.sb([128, 1024], name="post_pre"); JK = c.sb([128, 1024], name="post_jk")
        HN = c.sb([128, 1024], name="post_hn"); E = c.sb([128, 1024], name="post_e"); SG = c.sb([128, 1024], name="post_sg")
        HNT = c.sb([128, 8, 128], BF16, name="post_hnt"); OT = c.sb([128, 8, 128], F32, name="post_ot")
        ST = c.sb([128, 8], name="post_st")
        pv = pT.t[layer]
        pvv = pv.rearrange("(k p) l -> p k l", p=128)
        b0, b1, b2, b3, b4, b5, b6, b7 = s.P
        for b in range(NB):
            t0 = b * TB
            entries = prep(b)
            c.dma("sp", Pst[:], pvv[:, :, t0:t0 + TB], reads=[pT], writes=[Pst])
            c.act(lambda e: e.copy(Pb[:], Pst[:]), reads=[Pst, Pb], writes=[Pb])
            for tt in range(NTB):
                tok = slice(t0 + tt * 128, t0 + (tt + 1) * 128)
                tl = slice(tt * 128, (tt + 1) * 128)
                c.dma("sp", XT[:], resid[tok, :], reads=[resid], writes=[XT])
                for half in range(2):
                    bank = (b0, b1)[half]
                    ne = len(entries)
                    for ei, (mt, idx, P, wi, wk) in enumerate(entries):
                        c.pe(lambda e: e.matmul(bank[:, :], mt[0:P, idx, tl], Wp[wi][0:P, wk, half * 512:(half + 1) * 512], start=(ei == 0), stop=(ei == ne - 1)),
                             reads=[mt, Wp[wi]], writes=[bank])
                    c.dve(lambda e: e.scalar_tensor_tensor(PRE[:, half * 512:(half + 1) * 512], XT[:, half * 512:(half + 1) * 512], ALPHA, bank[:, :], ALU.mult, ALU.add),
                          reads=[XT, bank, PRE], writes=[PRE])
                c.dve(lambda e: e.reduce_sum(ST[:, 0:1], PRE[:], axis=AX.X), reads=[PRE, ST], writes=[ST])
                c.dve(lambda e: e.tensor_scalar(ST[:, 1:2], ST[:, 0:1], -1.0 / 1024, None, ALU.mult), reads=[ST], writes=[ST])
                c.dve(lambda e: e.tensor_scalar(PRE[:], PRE[:], ST[:, 1:2], None, ALU.add), reads=[PRE, ST], writes=[PRE])
                c.act(lambda e: e.activation(out=JK[:], in_=PRE[:], func=AF.Square, accum_out=ST[:, 2:3]), reads=[PRE, JK, ST], writes=[JK, ST])
                c.dve(lambda e: e.tensor_scalar(ST[:, 3:4], ST[:, 2:3], 1.0 / 1024, LN_EPS, ALU.mult, ALU.add), reads=[ST], writes=[ST])
                c.act(lambda e: e.activation(out=ST[:, 3:4], in_=ST[:, 3:4], func=AF.Sqrt), reads=[ST], writes=[ST])
                c.dve(lambda e: e.reciprocal(ST[:, 3:4], ST[:, 3:4]), reads=[ST], writes=[ST])
                c.dve(lambda e: e.scalar_tensor_tensor(HN[:], PRE[:], ST[:, 3:4], LG[:], ALU.mult, ALU.mult), reads=[PRE, ST, LG, HN], writes=[HN])
                c.dve(lambda e: e.tensor_tensor(HN[:], HN[:], LB[:], ALU.add), reads=[HN, LB], writes=[HN])
                for half in range(2):
                    bank = (b2, b3)[half]
                    for kk in range(2):
                        c.pe(lambda e: e.matmul(bank[:, :], Pb[:, kk, tl], PW[:, kk, half * 512:(half + 1) * 512], start=(kk == 0), stop=(kk == 1)),
                             reads=[Pb, PW], writes=[bank])
                    c.act(lambda e: e.activation(out=JK[:, half * 512:(half + 1) * 512], in_=bank[:, :], func=AF.Square, accum_out=ST[:, 4 + half:5 + half]),
                          reads=[bank, JK, ST], writes=[JK, ST])
                c.dve(lambda e: e.tensor_tensor(ST[:, 6:7], ST[:, 4:5], ST[:, 5:6], ALU.add), reads=[ST], writes=[ST])
                c.dve(lambda e: e.tensor_scalar(ST[:, 6:7], ST[:, 6:7], 1.0 / 1024, RMS_EPS, ALU.mult, ALU.add), reads=[ST], writes=[ST])
                c.act(lambda e: e.activation(out=ST[:, 6:7], in_=ST[:, 6:7], func=AF.Sqrt), reads=[ST], writes=[ST])
                c.dve(lambda e: e.reciprocal(ST[:, 6:7], ST[:, 6:7]), reads=[ST], writes=[ST])
                for half in range(2):
                    bank = (b2, b3)[half]
                    hsl = slice(half * 512, (half + 1) * 512)
                    c.dve(lambda e: e.scalar_tensor_tensor(E[:, hsl], bank[:, :], ST[:, 6:7], PN[:, hsl], ALU.mult, ALU.mult), reads=[bank, ST, PN, E], writes=[E])
                for k in range(8):
                    bank = (b4, b5)[k // 4]
                    c.pe(lambda e: e.transpose(bank[:, (k % 4) * 128:(k % 4 + 1) * 128], HN[:, k * 128:(k + 1) * 128], ident[:]), reads=[HN, ident], writes=[bank])
                c.act(lambda e: e.copy(HNT[:, 0:4, :], b4[:, :].rearrange("p (k c) -> p k c", k=4)), reads=[b4, HNT], writes=[HNT])
                c.dve(lambda e: e.tensor_copy(HNT[:, 4:8, :], b5[:, :].rearrange("p (k c) -> p k c", k=4)), reads=[b5, HNT], writes=[HNT])
                for half in range(2):
                    bank = (b6, b7)[half]
                    for k in range(8):
                        c.pe(lambda e: e.matmul(bank[:, :], HNT[:, k, :], PG[:, k, half * 512:(half + 1) * 512], start=(k == 0), stop=(k == 7)), reads=[HNT, PG], writes=[bank])
                    hsl = slice(half * 512, (half + 1) * 512)
                    c.act(lambda e: e.activation(out=SG[:, hsl], in_=bank[:, :], func=AF.Sigmoid), reads=[bank, SG], writes=[SG])
                c.dve(lambda e: e.tensor_tensor(SG[:], SG[:], E[:], ALU.mult), reads=[SG, E], writes=[SG])
                c.dve(lambda e: e.tensor_tensor(SG[:], SG[:], HN[:], ALU.add), reads=[SG, HN], writes=[SG])
                c.dma("pool", out_dst[tok, :], SG[:], reads=[SG], acc=[out_dst])
                if hT_dst is not None:
                    for k in range(8):
                        bank = (b4, b5)[k // 4]
                        c.pe(lambda e: e.transpose(bank[:, (k % 4) * 128:(k % 4 + 1) * 128], SG[:, k * 128:(k + 1) * 128], ident[:]), reads=[SG, ident], writes=[bank])
                    c.act(lambda e: e.copy(OT[:, 0:4, :], b4[:, :].rearrange("p (k c) -> p k c", k=4)), reads=[b4, OT], writes=[OT])
                    c.dve(lambda e: e.tensor_copy(OT[:, 4:8, :], b5[:, :].rearrange("p (k c) -> p k c", k=4)), reads=[b5, OT], writes=[OT])
                    c.dma("pool", hT_dst.t.rearrange("(k p) l -> p k l", p=128)[:, :, tok], OT[:], reads=[OT], acc=[hT_dst])

    def l0_epilogue_setup(s, projT, oT, yT, bonusT):
        c = s.c
        TB = s.TBp
        I = s.inp
        E_ = {}
        E_["dnn"] = s.load_const("dn_norm_col", [128, 1])
        E_["lnc"] = s.load_const("rk_ln64", [64, 2, 8])
        for k in ("OF", "OB", "G"):
            E_[k] = c.sb([128, 4, TB], name="ep_" + k)
        for k in ("YF", "YB", "RG", "BO", "SQ"):
            E_[k] = c.sb([64, 8, TB], name="ep_" + k)
        E_["MDN"] = c.sb([128, 4, TB], BF16, name="ep_MDN")
        E_["MRK"] = c.sb([64, 8, TB], BF16, name="ep_MRK")
        ones = s.K["ones"]
        gdn = projT.t[1552:2064, :].rearrange("(h p) l -> p h l", p=128)
        grk = projT.t[3792:4304, :].rearrange("(h p) l -> p h l", p=64)
        bov = bonusT.t.rearrange("(e p) c l -> p c e l", e=2)

        def prep(b):
            t0 = b * TB
            ts_ = slice(t0, t0 + TB)
            OF, OB, G, YF, YB, RG, BO, SQ, MDN, MRK = (E_[k] for k in ("OF", "OB", "G", "YF", "YB", "RG", "BO", "SQ", "MDN", "MRK"))
            c.dma("sp", OF[:], oT[0][:, :, ts_], reads=[oT[0]], writes=[OF])
            c.dma("sp", OB[:], oT[1][:, :, ts_], reads=[oT[1]], writes=[OB])
            c.dma("sp", G[:], gdn[:, :, ts_], reads=[projT], writes=[G])
            c.dve(lambda e: e.tensor_tensor(OF[:], OF[:], OB[:], ALU.add), reads=[OF, OB], writes=[OF])
            c.act(lambda e: e.activation(out=OB[:], in_=OF[:], func=AF.Square), reads=[OF, OB], writes=[OB])
            for h in range(4):
                c.pe(lambda e: e.matmul(s.P[h][:, 0:TB], ones[:], OB[:, h, :], start=True, stop=True), reads=[OB, ones], writes=[s.P[h]])
            for h in range(4):
                c.dve(lambda e: e.tensor_scalar(OB[:, h, :], s.P[h][:, 0:TB], 1.0 / 128, RMS_EPS, ALU.mult, ALU.add), reads=[s.P[h], OB], writes=[OB])
            c.act(lambda e: e.activation(out=OB[:], in_=OB[:], func=AF.Sqrt), reads=[OB], writes=[OB])
            c.dve(lambda e: e.reciprocal(OB[:], OB[:]), reads=[OB], writes=[OB])
            c.dve(lambda e: e.scalar_tensor_tensor(OF[:], OF[:], E_["dnn"][:, 0:1], OB[:], ALU.mult, ALU.mult), reads=[OF, OB, E_["dnn"]], writes=[OF])
            c.act(lambda e: e.activation(out=G[:], in_=G[:], func=AF.Silu), reads=[G], writes=[G])
            c.dve(lambda e: e.tensor_tensor(MDN[:], OF[:], G[:], ALU.mult), reads=[OF, G, MDN], writes=[MDN])
            c.dma("sp", YF[:], yT[0][:, :, ts_], reads=[yT[0]], writes=[YF])
            c.dma("sp", YB[:], yT[1][:, :, ts_], reads=[yT[1]], writes=[YB])
            c.dma("sp", RG[:], grk[:, :, ts_], reads=[projT], writes=[RG])
            for cc in range(4):
                c.dma("sp", BO[:, 2 * cc:2 * cc + 2, :], bov[:, cc, :, ts_], reads=[bonusT], writes=[BO])
            c.dve(lambda e: e.tensor_tensor(YF[:], YF[:], YB[:], ALU.add), reads=[YF, YB], writes=[YF])
            for hg in range(2):
                for q in range(4):
                    h = hg * 4 + q
                    c.pe(lambda e: e.matmul(s.P[4 + q][0:64, 0:TB], ones[0:64, 0:64], YF[:, h, :], start=True, stop=True), reads=[YF, ones], writes=[s.P[4 + q]])
                for q in range(4):
                    h = hg * 4 + q
                    c.dve(lambda e: e.scalar_tensor_tensor(YF[:, h, :], s.P[4 + q][0:64, 0:TB], -1.0 / 64, YF[:, h, :], ALU.mult, ALU.add), reads=[s.P[4 + q], YF], writes=[YF])
            c.act(lambda e: e.activation(out=SQ[:], in_=YF[:], func=AF.Square), reads=[YF, SQ], writes=[SQ])
            for hg in range(2):
                for q in range(4):
                    h = hg * 4 + q
                    c.pe(lambda e: e.matmul(s.P[4 + q][0:64, 0:TB], ones[0:64, 0:64], SQ[:, h, :], start=True, stop=True), reads=[SQ, ones], writes=[s.P[4 + q]])
                for q in range(4):
                    h = hg * 4 + q
                    c.dve(lambda e: e.tensor_scalar(YB[:, h, :], s.P[4 + q][0:64, 0:TB], 1.0 / 64, GN_EPS, ALU.mult, ALU.add), reads=[s.P[4 + q], YB], writes=[YB])
            c.act(lambda e: e.activation(out=YB[:], in_=YB[:], func=AF.Sqrt), reads=[YB], writes=[YB])
            c.dve(lambda e: e.reciprocal(YB[:], YB[:]), reads=[YB], writes=[YB])
            c.dve(lambda e: e.tensor_tensor(YF[:], YF[:], YB[:], ALU.mult), reads=[YF, YB], writes=[YF])
            lnc = E_["lnc"]
            c.dve(lambda e: e.tensor_tensor(YF[:], YF[:], bc(lnc[:, 0, :].unsqueeze(2), [64, 8, TB]), ALU.mult), reads=[YF, lnc], writes=[YF])
            c.dve(lambda e: e.tensor_tensor(YF[:], YF[:], bc(lnc[:, 1, :].unsqueeze(2), [64, 8, TB]), ALU.add), reads=[YF, lnc], writes=[YF])
            c.dve(lambda e: e.tensor_tensor(YF[:], YF[:], BO[:], ALU.add), reads=[YF, BO], writes=[YF])
            c.act(lambda e: e.activation(out=RG[:], in_=RG[:], func=AF.Silu), reads=[RG], writes=[RG])
            c.dve(lambda e: e.tensor_tensor(MRK[:], YF[:], RG[:], ALU.mult), reads=[YF, RG, MRK], writes=[MRK])
            return [(MDN, k, 128, 0, k) for k in range(4)] + [(MRK, h, 64, 1, h) for h in range(8)]
        return prep

    def fft_fwd(s, U, ubufs, Kp, G, F, PBK=None):
        c = s.c
        N1 = s.N1
        cpb = min(G, 512 // (2 * N1))
        nb = G // cpb
        PBK = PBK if PBK is not None else s.P[0:4]
        A_b = [PBK[i] for i in range(nb)]
        X_b = [PBK[2 + i] for i in range(nb)]
        F["X_b"], F["cpb"], F["nb"] = X_b, cpb, nb
        w = 2 * N1
        for g in range(G):
            bank = A_b[g // cpb]
            c.pe(lambda e: e.matmul(bank[:, (g % cpb) * w:(g % cpb + 1) * w], U[:, g, :], F["F1"][0:Kp, :], start=True, stop=True), reads=ubufs + [F["F1"]], writes=[bank])
        T1, T2, RH1, RH2 = F["T1"], F["T2"], F["RH1"], F["RH2"]
        tw = F["TW"]
        yield
        for bi in range(nb):
            gs = slice(bi * cpb, (bi + 1) * cpb)
            Av = A_b[bi][:, 0:cpb * w].rearrange("p (g a k) -> p g a k", g=cpb, a=2)
            c.dve(lambda e: e.tensor_tensor(T1[:, gs], Av, bc(tw[:, 0:1, :].unsqueeze(1), [128, cpb, 2, N1]), ALU.mult), reads=[A_b[bi], tw, T1], writes=[T1])
            c.dve(lambda e: e.tensor_tensor(T2[:, gs], Av, bc(tw[:, 1:2, :].unsqueeze(1), [128, cpb, 2, N1]), ALU.mult), reads=[A_b[bi], tw, T2], writes=[T2])
        c.dve(lambda e: e.tensor_tensor(RH1[:, :, 0, :], T1[:, :, 0, :], T2[:, :, 1, :], ALU.subtract), reads=[T1, T2, RH1], writes=[RH1])
        c.dve(lambda e: e.tensor_tensor(RH1[:, :, 1, :], T1[:, :, 1, :], T2[:, :, 0, :], ALU.add), reads=[T1, T2, RH1], writes=[RH1])
        c.act(lambda e: e.mul(RH2[:, :, 0, :], RH1[:, :, 1, :], -1.0), reads=[RH1, RH2], writes=[RH2])
        c.act(lambda e: e.copy(RH2[:, :, 1, :], RH1[:, :, 0, :]), reads=[RH1, RH2], writes=[RH2])
        yield
        for bi in range(nb):
            gs = slice(bi * cpb, (bi + 1) * cpb)
            c.pe(lambda e: e.matmul(X_b[bi][:, 0:cpb * w], F["F2re"][:], RH1[:, gs].rearrange("p g a k -> p (g a k)"), start=True, stop=False), reads=[RH1, F["F2re"]], writes=[X_b[bi]])
            c.pe(lambda e: e.matmul(X_b[bi][:, 0:cpb * w], F["F2im"][:], RH2[:, gs].rearrange("p g a k -> p (g a k)"), start=False, stop=True), reads=[RH2, F["F2im"]], writes=[X_b[bi]])
        yield

    def fft_conv(s, U, ubufs, Hs, G, F, out_bank, PBK=None):
        c = s.c
        N1 = s.N1
        Kp = N1 // 2
        w = 2 * N1
        PBK = PBK if PBK is not None else s.P[0:4]
        yield from s.fft_fwd(U, ubufs, Kp, G, F, PBK)
        X_b, cpb, nb = F["X_b"], F["cpb"], F["nb"]
        T1, T2, Y = F["T1"], F["T2"], F["Y"]
        for bi in range(nb):
            gs = slice(bi * cpb, (bi + 1) * cpb)
            Xv = X_b[bi][:, 0:cpb * w].rearrange("p (g a k) -> p g a k", g=cpb, a=2)
            c.dve(lambda e: e.tensor_tensor(T1[:, gs], Xv, bc(Hs[:, gs, 0:1, :], [128, cpb, 2, N1]), ALU.mult), reads=[X_b[bi], Hs, T1], writes=[T1])
            c.dve(lambda e: e.tensor_tensor(T2[:, gs], Xv, bc(Hs[:, gs, 1:2, :], [128, cpb, 2, N1]), ALU.mult), reads=[X_b[bi], Hs, T2], writes=[T2])
        c.dve(lambda e: e.tensor_tensor(Y[:, :, 0, :], T1[:, :, 0, :], T2[:, :, 1, :], ALU.subtract), reads=[T1, T2, Y], writes=[Y])
        c.dve(lambda e: e.tensor_tensor(Y[:, :, 1, :], T1[:, :, 1, :], T2[:, :, 0, :], ALU.add), reads=[T1, T2, Y], writes=[Y])
        yield
        B_b = [PBK[i] for i in range(G // 2)]
        for g in range(G):
            bank = B_b[g // 2]
            o = bank[0:N1, (g % 2) * 256:(g % 2 + 1) * 256]
            c.pe(lambda e: e.matmul(o, Y[:, g, 0, :], F["CF1"][:], start=True, stop=False), reads=[Y, F["CF1"]], writes=[bank])
            c.pe(lambda e: e.matmul(o, Y[:, g, 1, :], F["CF2"][:], start=False, stop=True), reads=[Y, F["CF2"]], writes=[bank])
        S1, S2, BR, BI = F["S1"], F["S2"], F["BR"], F["BI"]
        twt = F["TWT"]
        yield
        for bi in range(G // 2):
            gs = slice(bi * 2, bi * 2 + 2)
            Bv = B_b[bi][0:N1, :].rearrange("p (g a n) -> p g a n", g=2, a=2)
            c.dve(lambda e: e.tensor_tensor(S1[:, gs], Bv, bc(twt[:, 0:1, :].unsqueeze(1), [N1, 2, 2, 128]), ALU.mult), reads=[B_b[bi], twt, S1], writes=[S1])
            c.dve(lambda e: e.tensor_tensor(S2[:, gs], Bv, bc(twt[:, 1:2, :].unsqueeze(1), [N1, 2, 2, 128]), ALU.mult), reads=[B_b[bi], twt, S2], writes=[S2])
        c.dve(lambda e: e.tensor_tensor(BR[:], S1[:, :, 0, :], S2[:, :, 1, :], ALU.add), reads=[S1, S2, BR], writes=[BR])
        c.dve(lambda e: e.tensor_tensor(BI[:], S1[:, :, 1, :], S2[:, :, 0, :], ALU.subtract), reads=[S1, S2, BI], writes=[BI])
        yield
        o = out_bank[0:Kp, 0:G * 128]
        c.pe(lambda e: e.matmul(o, F["IF1re"][:], BR[:].rearrange("p g n -> p (g n)"), start=True, stop=False), reads=[BR, F["IF1re"]], writes=[out_bank])
        c.pe(lambda e: e.matmul(o, F["IF1imn"][:], BI[:].rearrange("p g n -> p (g n)"), start=False, stop=True), reads=[BI, F["IF1imn"]], writes=[out_bank])
        yield

    def layer1(s, hT, hres, pT, stop):
        c = s.c
        L, TB, NB = s.L, s.TB, s.NB
        I = s.inp
        N1 = 2 * L // 128
        s.N1 = N1
        Kp = N1 // 2
        G = 4
        w_in1 = I("w_in1", [1024, 4096])
        projT = s.scratch("projT1", [4096, L])
        s.proj_fm(hT, w_in1, 4096, projT)
        xvT = s.scratch("hy_xvT", [3072, L])
        c.push()
        TB = min(256, L)
        NB = L // TB
        CW = s.load_const("hy_cw", [128, 24, 4])
        pv = projT.t[0:3072, :].rearrange("(t p) l -> p t l", p=128)
        xv = xvT.t.rearrange("(t p) l -> p t l", p=128)
        X = c.sb([128, 24, TB + 2], name="hyX"); A = c.sb([128, 24, TB], name="hyA"); A2 = c.sb([128, 24, TB], name="hyA2")
        for b in range(NB):
            t0 = b * TB
            lo = max(t0 - 1, 0); hi = min(t0 + TB + 1, L)
            if lo != t0 - 1 or hi != t0 + TB + 1:
                c.dve(lambda e: e.memset(X[:], 0.0), writes=[X])
            c.dma("sp", X[:, :, lo - (t0 - 1):hi - (t0 - 1)], pv[:, :, lo:hi], reads=[projT], writes=[X])
            c.dve(lambda e: e.tensor_tensor(A[:], X[:, :, 0:TB], bc(CW[:, :, 0:1], [128, 24, TB]), ALU.mult), reads=[X, CW, A], writes=[A])
            c.dve(lambda e: e.tensor_tensor(A2[:], X[:, :, 1:TB + 1], bc(CW[:, :, 1:2], [128, 24, TB]), ALU.mult), reads=[X, CW, A2], writes=[A2])
            c.dve(lambda e: e.tensor_tensor(A[:], A[:], A2[:], ALU.add), reads=[A, A2], writes=[A])
            c.dve(lambda e: e.tensor_tensor(A2[:], X[:, :, 2:TB + 2], bc(CW[:, :, 2:3], [128, 24, TB]), ALU.mult), reads=[X, CW, A2], writes=[A2])
            c.dve(lambda e: e.tensor_tensor(A[:], A[:], A2[:], ALU.add), reads=[A, A2], writes=[A])
            c.dve(lambda e: e.tensor_tensor(A[:], A[:], bc(CW[:, :, 3:4], [128, 24, TB]), ALU.add), reads=[A, CW], writes=[A])
            c.dma("pool", xv[:, :, t0:t0 + TB], A[:], reads=[A], acc=[xvT])
        c.pop()
        filt = s.scratch("hy_filt", [2, 1024, 2 * L])
        c.push()
        CH = min(512, L)
        NCH = L // CH
        FTd = I("hy_FT", [33, 2, L])
        W1 = s.load_const("hy_w1", [33, 64]); W2 = s.load_const("hy_w2", [64, 64]); W3 = s.load_const("hy_w3", [64, 64])
        BF = s.load_const("hy_bf", [64, 4])
        WO = s.load_const("hy_wo", [64, 4096])
        SK = s.load_const("hy_skipc", [128, 2, 8])
        ndd = I("hy_deltas", [1, 4096])
        ND = c.sb([65, 4096], name="hyND")
        c.dve(lambda e: e.memset(ND[:], 0.0), writes=[ND])
        c.dma("sp", ND[64:65, :], ndd.t, reads=[ndd], writes=[ND])
        c.act(lambda e: e.activation(out=ND[64:65, :], in_=ND[64:65, :], func=AF.Abs), reads=[ND], writes=[ND])
        c.act(lambda e: e.mul(ND[64:65, :], ND[64:65, :], -1.0), reads=[ND], writes=[ND])
        FB = c.sb([64, 3], name="hyFB")
        c.dve(lambda e: e.tensor_tensor(FB[:], BF[:, 0:3], bc(BF[:, 3:4], [64, 3]), ALU.mult), reads=[BF, FB], writes=[FB])
        HD = c.sb([65, 2, L], name="hyHD")
        FTc = c.sb([33, CH], name="hyFT"); Z = c.sb([64, CH], name="hyZ"); Mk = c.sb([64, CH], name="hyM")
        TWO_PI = 2 * math.pi
        for dirn in range(2):
            c.dma("sp", HD[64:65, dirn, :], FTd.t[0:1, dirn, :], reads=[FTd], writes=[HD])
            for ch in range(NCH):
                cs = slice(ch * CH, (ch + 1) * CH)
                c.dma("sp", FTc[:], FTd.t[:, dirn, cs], reads=[FTd], writes=[FTc])
                src, sbuf, Wl = FTc[:], FTc, [W1, W2, W3]
                for li in range(3):
                    bank = s.P[li]
                    kk = 33 if li == 0 else 64
                    c.pe(lambda e: e.matmul(bank[0:64, 0:CH], Wl[li][0:kk, :], src, start=True, stop=True), reads=[sbuf, Wl[li]], writes=[bank])
                    c.dve(lambda e: e.tensor_scalar(Z[:], bank[0:64, 0:CH], BF[:, 3:4], FB[:, li:li + 1], ALU.mult, ALU.add), reads=[bank, BF, FB, Z], writes=[Z])
                    for _ in range(2):
                        c.dve(lambda e: e.tensor_scalar(Mk[:], Z[:], math.pi, -TWO_PI, ALU.is_gt, ALU.mult), reads=[Z, Mk], writes=[Mk])
                        c.dve(lambda e: e.tensor_tensor(Z[:], Z[:], Mk[:], ALU.add), reads=[Z, Mk], writes=[Z])
                        c.dve(lambda e: e.tensor_scalar(Mk[:], Z[:], -math.pi, TWO_PI, ALU.is_lt, ALU.mult), reads=[Z, Mk], writes=[Mk])
                        c.dve(lambda e: e.tensor_tensor(Z[:], Z[:], Mk[:], ALU.add), reads=[Z, Mk], writes=[Z])
                    dst = Z[:] if li < 2 else HD[0:64, dirn, cs]
                    dbuf = Z if li < 2 else HD
                    c.act(lambda e: e.activation(out=dst, in_=Z[:], func=AF.Sin), reads=[Z, dbuf], writes=[dbuf])
                    src, sbuf = Z[:], Z
        ACC = c.sb([128, 32, NCH], name="hyACC"); NRM = c.sb([128, 16], name="hyNRM")
        EW = c.sb([128, CH], name="hyEW"); FV = c.sb([128, CH], name="hyFV"); JK = c.sb([128, CH], name="hyJK")
        c.dve(lambda e: e.memset(ACC[:], 0.0), writes=[ACC])
        for pas in range(2):
            for o in range(2):
                for dirn in range(2):
                    for ct in range(8):
                        col = o * 16 + dirn * 8 + ct
                        csl = slice(col * 128, (col + 1) * 128)
                        for ch in range(NCH):
                            cs = slice(ch * CH, (ch + 1) * CH)
                            bk1, bk2 = s.P[4 + (ch % 2) * 2], s.P[5 + (ch % 2) * 2]
                            c.pe(lambda e: e.matmul(bk1[:, 0:CH], WO[:, csl], HD[0:64, dirn, cs], start=True, stop=True), reads=[WO, HD], writes=[bk1])
                            c.pe(lambda e: e.matmul(bk2[:, 0:CH], ND[:, csl], HD[0:65, dirn, cs], start=True, stop=True), reads=[ND, HD], writes=[bk2])
                            c.act(lambda e: e.activation(out=EW[:], in_=bk2[:, 0:CH], func=AF.Exp), reads=[bk2, EW], writes=[EW])
                            c.dve(lambda e: e.tensor_tensor(FV[:], bk1[:, 0:CH], EW[:], ALU.mult), reads=[bk1, EW, FV], writes=[FV])
                            if dirn == 1 and ch == 0:
                                c.dve(lambda e: e.memset(FV[:, 0:1], 0.0), reads=[FV], writes=[FV])
                            if pas == 0:
                                c.act(lambda e: e.activation(out=JK[:], in_=FV[:], func=AF.Abs, accum_out=ACC[:, col, ch:ch + 1]), reads=[FV, JK, ACC], writes=[JK, ACC])
                            else:
                                c.dve(lambda e: e.tensor_scalar(FV[:], FV[:], NRM[:, o * 8 + ct:o * 8 + ct + 1], None, ALU.mult), reads=[FV, NRM], writes=[FV])
                                if dirn == 0 and ch == 0:
                                    c.dve(lambda e: e.tensor_tensor(FV[:, 0:1], FV[:, 0:1], SK[:, o, ct:ct + 1], ALU.add), reads=[FV, SK], writes=[FV])
                                c.dma("pool", filt[o, ct * 128:(ct + 1) * 128, dirn * L + ch * CH:dirn * L + (ch + 1) * CH], FV[:], reads=[FV], acc=[filt])
            if pas == 0:
                RED = c.sb([128, 32], name="hyRED")
                c.dve(lambda e: e.reduce_sum(RED[:], ACC[:], axis=AX.X), reads=[ACC, RED], writes=[RED])
                R4 = RED[:, :].rearrange("p (o d c) -> p o d c", o=2, d=2)
                N4 = NRM[:, :].rearrange("p (o c) -> p o c", o=2)
                c.dve(lambda e: e.tensor_tensor(N4, R4[:, :, 0, :], R4[:, :, 1, :], ALU.add), reads=[RED, NRM], writes=[NRM])
                c.dve(lambda e: e.tensor_scalar_add(NRM[:], NRM[:], RMS_EPS), reads=[NRM], writes=[NRM])
                c.dve(lambda e: e.reciprocal(NRM[:], NRM[:]), reads=[NRM], writes=[NRM])
        c.pop()
        if stop == "F":
            return
        c.push()
        Fc = {k: s.load_const(k, shp) for k, shp in s.fft_shapes.items()}
        NG = 1024 // G
        spec = s.scratch("hy_spec", [2, NG, 128, G * 2 * N1])
        mixT = s.scratch("hy_mixT", [1024, L])
        fv = filt.t.rearrange("o c (a n) -> o a c n", n=128)
        xg = xvT.t.rearrange("(t c) (a n) -> t a c n", t=3, n=128)
        gt = projT.t[3072:4096, :].rearrange("c (a n) -> a c n", n=128)
        mv = mixT.t.rearrange("c (a n) -> a c n", n=128)
        w = 2 * N1
        Fs = []
        for ch in range(2):
            F = dict(Fc)
            for k in ("T1", "T2", "RH1", "RH2", "Y"):
                F[k] = c.sb([128, G, 2, N1], name=f"ff{ch}_" + k)
            for k in ("S1", "S2"):
                F[k] = c.sb([N1, G, 2, 128], name=f"ff{ch}_" + k)
            F["BR"] = c.sb([N1, G, 128], name=f"ff{ch}_BR"); F["BI"] = c.sb([N1, G, 128], name=f"ff{ch}_BI")
            F["UF"] = c.sb([N1, G, 128], name=f"ff{ch}_UF"); F["SP"] = c.sb([128, G, 2, N1], name=f"ff{ch}_SP")
            for k in ("V", "X1", "X2", "GT", "Z"):
                F[k] = c.sb([Kp, G, 128], name=f"hy{ch}_" + k)
            F["H0"] = c.sb([128, G, 2, N1], name=f"hy{ch}_H0"); F["H1"] = c.sb([128, G, 2, N1], name=f"hy{ch}_H1")
            Fs.append(F)

        def filt_chain(ch):
            F = Fs[ch]
            PBK = s.P[4 * ch:4 * ch + 4]
            UF, SP = F["UF"], F["SP"]
            for idx in range(ch, 2 * NG, 2):
                o, gi = idx // NG, idx % NG
                c.dma("sp", UF[:], fv[o, :, gi * G:(gi + 1) * G, :], reads=[filt], writes=[UF])
                yield from s.fft_fwd(UF[:], [UF], N1, G, F, PBK)
                X_b, cpb, nb = F["X_b"], F["cpb"], F["nb"]
                for bi in range(nb):
                    gs = slice(bi * cpb, (bi + 1) * cpb)
                    c.act(lambda e: e.copy(SP[:, gs].rearrange("p g a k -> p (g a k)"), X_b[bi][:, 0:cpb * w]), reads=[X_b[bi], SP], writes=[SP])
                c.dma("pool", spec[o, gi], SP[:].rearrange("p g a k -> p (g a k)"), reads=[SP], acc=[spec])
                yield
        s.run_chains([filt_chain(ch) for ch in range(2)])

        def conv_chain(ch):
            F = Fs[ch]
            PBK = s.P[4 * ch:4 * ch + 4]
            V, X1, X2, GT, Zt, H0, H1 = (F[k] for k in ("V", "X1", "X2", "GT", "Z", "H0", "H1"))
            for gi in range(ch, NG, 2):
                cs = slice(gi * G, (gi + 1) * G)
                c.dma("sp", V[:], xg[2, :, cs, :], reads=[xvT], writes=[V])
                c.dma("sp", X1[:], xg[0, :, cs, :], reads=[xvT], writes=[X1])
                c.dma("sp", X2[:], xg[1, :, cs, :], reads=[xvT], writes=[X2])
                c.dma("sp", GT[:], gt[:, cs, :], reads=[projT], writes=[GT])
                c.dma("sp", H0[:].rearrange("p g a k -> p (g a k)"), spec[0, gi], reads=[spec], writes=[H0])
                c.dma("sp", H1[:].rearrange("p g a k -> p (g a k)"), spec[1, gi], reads=[spec], writes=[H1])
                ob = PBK[2]
                yield from s.fft_conv(V[:], [V], H0, G, F, ob, PBK)
                c.dve(lambda e: e.tensor_tensor(Zt[:], X1[:], ob[0:Kp, 0:G * 128].rearrange("p (g n) -> p g n", g=G), ALU.mult), reads=[X1, ob, Zt], writes=[Zt])
                ob2 = PBK[3]
                yield from s.fft_conv(Zt[:], [Zt], H1, G, F, ob2, PBK)
                c.act(lambda e: e.activation(out=GT[:], in_=GT[:], func=AF.Silu), reads=[GT], writes=[GT])
                c.dve(lambda e: e.tensor_tensor(X2[:], X2[:], ob2[0:Kp, 0:G * 128].rearrange("p (g n) -> p g n", g=G), ALU.mult), reads=[X2, ob2], writes=[X2])
                c.dve(lambda e: e.tensor_tensor(X2[:], X2[:], GT[:], ALU.mult), reads=[X2, GT], writes=[X2])
                c.dma("pool", mv[:, cs, :], X2[:], reads=[X2], acc=[mixT])
                yield
        s.run_chains([conv_chain(ch) for ch in range(2)])
        c.pop()
        if stop == "G":
            return
        c.push()
        TB = s.TBp
        MS = c.sb([128, 8, TB], F32, name="l1_ms"); MB = c.sb([128, 8, TB], BF16, name="l1_mb")
        mxv = mixT.t.rearrange("(k p) l -> p k l", p=128)

        def prep(b):
            c.dma("sp", MS[:], mxv[:, :, b * TB:(b + 1) * TB], reads=[mixT], writes=[MS])
            c.act(lambda e: e.copy(MB[:], MS[:]), reads=[MS, MB], writes=[MB])
            return [(MB, k, 128, 0, k) for k in range(8)]
        s.post(1, prep, [(0, 1024, 128)], hres, pT, s.out, None)
        c.pop()

    def build(s, stop=None):
        c = s.c
        L, NT = s.L, s.NT
        I = s.inp
        xT = I("xT", [1024, L])
        x = I("x", [L, 1024])
        pT = I("pT", [2, 256, L])
        w_in0 = I("w_in0", [1024, EVEN_IN_PAD])
        out = s.c.dram("out", [L, 1024], F32, kind="ExternalOutput")
        s.out = out
        s.psum_banks()
        s.K = {k: s.load_const(k, shp) for k, shp in CONST_SHAPES.items()}
        last0 = (1 not in s.layers)
        s.h1 = out if last0 else s.scratch("h1", [L, 1024])
        s.h1T = None if last0 else s.scratch("h1T", [1024, L])
        if 0 in s.layers:
            s.layer0(xT, x, pT, w_in0, stop)
        if 1 in s.layers:
            N1 = 2 * L // 128
            s.fft_shapes = {"F1": [N1, 2 * N1], "TW": [128, 2, N1], "F2re": [128, 128], "F2im": [128, 128], "CF1": [128, 256], "CF2": [128, 256],
                            "TWT": [N1, 2, 128], "IF1re": [N1, N1 // 2], "IF1imn": [N1, N1 // 2]}
            if 0 in s.layers:
                s.layer1(s.h1T, s.h1, pT, stop)
            else:
                s.layer1(xT, x, pT, stop)
        return s.nc

    def layer0(s, xT, x, pT, w_in0, stop):
        c = s.c
        L, NT = s.L, s.NT
        I = s.inp
        projT = s.scratch("projT", [EVEN_IN_PAD, L])
        dtb_d = I("dn_dt_bias", [1, 8])
        alog_d = I("dn_a_log", [1, 8])
        dtb = s.load_bc(dtb_d.t, 8, dtb_d, "dtb")
        alog = s.load_bc(alog_d.t, 8, alog_d, "alog")
        ABraw = c.sb([128, NT, 16], name="ABraw")
        GB = c.sb([128, NT, 16], name="GB")
        s.GB = GB
        s.proj_fm(xT, w_in0, EVEN_IN_PAD, projT, tm_cols=(1536, 1552), tm_dst=ABraw)
        s.dn_gates(ABraw, GB, dtb, alog)
        if s.debug:
            gbd = s.scratch("gb_dbg", [128, NT, 16])
            c.dma("sp", gbd[:], GB[:], reads=[GB], writes=[gbd])
        if stop == "A1":
            return
        CW = s.load_const("dn_cw", [128, 3, 4, 5])
        qT = s.scratch("dn_qT", [128, 4, L])
        kT = s.scratch("dn_kT", [128, 4, L])
        k_tm = s.scratch("dn_ktm", [L, 4, 128])
        v_tm = s.scratch("dn_vtm", [L, 4, 128])
        s.dn_prep(projT, CW, qT, kT, k_tm, v_tm)
        if stop == "A2":
            return
        rk_tm = s.scratch("rk_tm", [7, L, 512])
        bonusT = s.scratch("rk_bonusT", [128, 4, L])
        s.rk_prep(projT, rk_tm, bonusT)
        if stop == "A4":
            return
        c.push()
        Ws = [s.alloc_chunk_ws() for _ in range(2)]
        oT = [s.scratch(f"dn_oT{d}", [128, 4, L]) for d in range(2)]
        s.run_chains([s.dn_pass(d, qT, kT, k_tm, v_tm, oT[d], Ws[d], s.P[4 * d:4 * d + 4]) for d in range(2)])
        c.pop()
        if stop == "B1":
            return
        c.push()
        W = [{k: c.sb([128, 4, 128], name=f"w2_{k}{hg}") for k in ("R", "NA", "NB", "PA", "PB")} for hg in range(2)]
        R_ = s.alloc_rk_ws()
        yT = [s.scratch(f"rk_yT{d}", [64, 8, L]) for d in range(2)]
        for d in range(2):
            s.rk_pass(d, rk_tm, yT[d], W, R_)
        c.pop()
        if stop == "B2":
            return
        c.push()
        prep = s.l0_epilogue_setup(projT, oT, yT, bonusT)
        s.post(0, prep, [(0, 512, 128), (512, 1024, 64)], x, pT, s.h1, s.h1T)
        c.pop()

    def finish(s):
        c = s.c
        c.barrier()
        c.close()


def host_layout(inp, b, L):
    f = lambda a: np.ascontiguousarray(np.asarray(a, dtype=np.float32))
    m = {}
    xb = np.asarray(inp["x"])[b, :L]
    m["x"] = f(xb)
    m["xT"] = f(xb.T)
    m["pT"] = f(np.asarray(inp["p"])[:, b, :L].transpose(0, 2, 1))
    w = np.asarray(inp["even_w_in"])[0]
    m["w_in0"] = f(np.pad(w, ((0, 0), (0, EVEN_IN_PAD - w.shape[1]))))
    m["dn_dt_bias"] = f(np.asarray(inp["dn_dt_bias"])[0].reshape(1, 8))
    m["dn_a_log"] = f(np.asarray(inp["dn_a_log"])[0].reshape(1, 8))
    dc = np.asarray(inp["dn_conv"])[0]
    m["dn_cw"] = f(dc.reshape(5, 3, 4, 128).transpose(3, 1, 2, 0))
    mu = np.asarray(inp["rk_mu"])[0]
    m["rk_mu_rkv"] = f(mu[:, :1536].reshape(2, 12, 128).transpose(2, 1, 0))
    m["rk_mu_lora"] = f(mu[:, 1536:].reshape(2, 3, 64).transpose(2, 1, 0))
    m["rk_w2"] = f(np.asarray(inp["rk_w2"])[0].transpose(1, 0, 2))
    m["rk_a2"] = f(np.asarray(inp["rk_a2"])[0])
    cols = np.stack([np.asarray(inp[k])[0].reshape(512) for k in ("rk_a0", "rk_k_k", "rk_k_a", "rk_r_k", "rk_ln_w", "rk_ln_b")], 0)
    m["rk_cols"] = f(cols.reshape(6, 4, 128).transpose(2, 0, 1))
    m["rk_w0"] = f(np.asarray(inp["rk_w0"])[0].reshape(1, 1024))
    m["rk_a0"] = f(np.asarray(inp["rk_a0"])[0].reshape(1, 512))
    m["dn_norm_col"] = f(np.asarray(inp["dn_norm"])[0].reshape(128, 1))
    m["rk_ln64"] = f(np.stack([np.asarray(inp["rk_ln_w"])[0].reshape(8, 64).T, np.asarray(inp["rk_ln_b"])[0].reshape(8, 64).T], 1))
    for l in range(2):
        m[f"w_out{l}"] = f(np.asarray(inp["w_out"])[l])
        m[f"ple_w{l}"] = f(np.asarray(inp["ple_w"])[l])
        m[f"ple_gate{l}"] = f(np.asarray(inp["ple_gate"])[l])
        m[f"vecs{l}"] = f(np.stack([np.asarray(inp["ln_g"])[l], np.asarray(inp["ln_b"])[l], np.asarray(inp["ple_norm"])[l]], 0))
    m["w_in1"] = f(np.asarray(inp["odd_w_in"])[0])
    cw = np.asarray(inp["hy_conv_w"])[0]; cb = np.asarray(inp["hy_conv_b"])[0]
    m["hy_cw"] = f(np.concatenate([cw, cb[None]], 0).reshape(4, 24, 128).transpose(2, 1, 0))
    m["hy_w1"] = f(np.asarray(inp["hy_ffn_w1"])[0]); m["hy_w2"] = f(np.asarray(inp["hy_ffn_w2"])[0]); m["hy_w3"] = f(np.asarray(inp["hy_ffn_w3"])[0])
    m["hy_bf"] = f(np.stack([np.asarray(inp[k])[0] for k in ("hy_ffn_b1", "hy_ffn_b2", "hy_ffn_b3", "hy_ffn_freq")], 1))
    m["hy_wo"] = f(np.asarray(inp["hy_ffn_out"])[0])
    m["hy_skipc"] = f(np.asarray(inp["hy_skip"])[0].reshape(2, 8, 128).transpose(2, 0, 1))
    m["hy_deltas"] = f(np.asarray(inp["hy_deltas"])[0].reshape(1, 4096))
    m.update(host_consts(L))
    m.update(hyena_consts(L))
    return m


def hyena_consts(L):
    c = {}
    bands = 16
    t = np.linspace(0.0, 1.0, L, dtype=np.float32).astype(np.float64)
    fr = np.linspace(1e-4, bands - 1, bands, dtype=np.float32).astype(np.float64)
    ang = (np.float32(2.0 * math.pi / L) * np.arange(L, dtype=np.float32)).astype(np.float64)[:, None] * fr[None, :]
    feats = np.concatenate([t[:, None], np.cos(ang), -np.sin(ang)], -1)
    rev = np.concatenate([feats[:1], feats[:0:-1]], 0)
    c["hy_FT"] = np.ascontiguousarray(np.stack([feats.T, rev.T], 1)).astype(np.float32)
    N = 2 * L
    N1 = N // 128
    n1 = np.arange(N1); k1 = np.arange(N1); n2 = np.arange(128); k2 = np.arange(128)
    a1 = 2 * np.pi * np.outer(n1, k1) / N1
    c["F1"] = np.concatenate([np.cos(a1), -np.sin(a1)], 1).astype(np.float32)
    atw = 2 * np.pi * np.outer(n2, k1) / N
    c["TW"] = np.stack([np.cos(atw), -np.sin(atw)], 1).astype(np.float32)
    a2 = 2 * np.pi * np.outer(n2, k2) / 128
    c["F2re"] = np.cos(a2).astype(np.float32); c["F2im"] = (-np.sin(a2)).astype(np.float32)
    c["CF1"] = np.concatenate([np.cos(a2), np.sin(a2)], 1).astype(np.float32)
    c["CF2"] = np.concatenate([-np.sin(a2), np.cos(a2)], 1).astype(np.float32)
    c["TWT"] = np.stack([np.cos(atw).T, -np.sin(atw).T], 1).astype(np.float32)
    c["IF1re"] = (np.cos(a1)[:, :N1 // 2] / N).astype(np.float32)
    c["IF1imn"] = (-np.sin(a1)[:, :N1 // 2] / N).astype(np.float32)
    return c


def run(inp, L, nb, debug=False, stop=None, layers=(0, 1)):
    pr = Prog(L, debug=debug, layers=layers)
    nc = pr.build(stop=stop)
    pr.finish()
    print("ninst", pr.c.ninst, "nwait", pr.c.nwait)
    maps = []
    for b in range(nb):
        m = host_layout(inp, b, L)
        maps.append({k: m[k] for k in pr.inputs})
    res = run_bass_kernel_spmd(nc, maps, core_ids=list(range(nb)))
    return res.results, pr


def kernel(**inputs):
    L = inputs["x"].shape[1]
    res, pr = run(inputs, L, NCORES)
    return np.stack([r["out"] for r in res], 0).astype(np.float32)
```
